# Optimizing a Trainium2 kernel written in Bass

```python
import jax, jax.numpy as jnp
from jax import lax
import numpy as np

D_MODEL = 2048
BATCH = 4
SEQ = 2048
DEPTH = 4
DEC_BATCH = 8
DEC_SEQ = 4
PAST_LEN = 16384
PAGE_SIZE = 128

N_MIXERS = 3
N_A = len([i for i in range(DEPTH) if i % N_MIXERS == 0])
N_B = len([i for i in range(DEPTH) if i % N_MIXERS == 1])
N_C = len([i for i in range(DEPTH) if i % N_MIXERS == 2])
A_CHUNK = 128
A_WIDTH = D_MODEL
A_HEADS = 16
A_HD = A_WIDTH // A_HEADS
B_WINDOWS = (128, 512, 2048)
B_DILATIONS = (1, 4, 16)
B_GROUPS = 3
B_HD = 128
B_HEADS = D_MODEL // B_HD
B_WIDTH = B_HEADS * B_HD
B_SCALE = B_HD ** -0.5
POOL_WINDOWS = (2, 4, 8, 16)
C_GROUPS = 4
C_GW = D_MODEL // C_GROUPS
POOL_STATE = max(POOL_WINDOWS) - 1
D_FF = ((8 * D_MODEL + 3 * 256 - 1) // (3 * 256)) * 256
EPS = 1e-6

kernel_name = 'hybrid_gmlp_dilated_pool_decoder_step'


def rmsnorm(x, g):
    xf = x.astype(jnp.float32)
    r = lax.rsqrt(jnp.mean(xf * xf, axis=-1, keepdims=True) + EPS)
    return (xf * r).astype(x.dtype) * g


def layernorm(x, g, b):
    xf = x.astype(jnp.float32)
    mu = jnp.mean(xf, axis=-1, keepdims=True)
    var = jnp.mean(jnp.square(xf - mu), axis=-1, keepdims=True)
    return ((xf - mu) * lax.rsqrt(var + EPS)).astype(x.dtype) * g + b


def swiglu(h, w1, w3, w2):
    return (jax.nn.silu(h @ w1) * (h @ w3)) @ w2


def chunk_mlp(h, w_in, ln_g, ln_b, w_s, b_s, w_out):
    bsz, s_len, _ = h.shape
    z = jax.nn.gelu(h @ w_in)
    u, v = z[..., :A_WIDTH], z[..., A_WIDTH:]
    v = layernorm(v, ln_g, ln_b)
    t_len = min(s_len, A_CHUNK)
    n_chunks = s_len // t_len
    vb = v.reshape(bsz, n_chunks, t_len, A_HEADS, A_HD)
    w = w_s[:, :t_len, :t_len] * jnp.tril(jnp.ones((t_len, t_len), w_s.dtype))
    mixed = jnp.einsum('hij,bcjhd->bcihd', w, vb) + jnp.transpose(b_s[:, :t_len])[None, None, :, :, None]
    y = (u * mixed.reshape(bsz, s_len, A_WIDTH)) @ w_out
    return y, v


def qkv_groups(h, w_qkv, q_g, k_g):
    bsz, s_len, _ = h.shape
    qkv = (h @ w_qkv).reshape(bsz, s_len, B_GROUPS, 3, B_HEADS, B_HD)
    q = rmsnorm(qkv[:, :, :, 0], q_g[:, None, :])
    k = rmsnorm(qkv[:, :, :, 1], k_g[:, None, :])
    return q, k, qkv[:, :, :, 2]


def dilated_prompt_group(q, k, v, dil, n):
    bsz, s_len, nh, hd = q.shape
    L = s_len // dil
    nb = -(-L // n)
    lp = nb * n

    def to_blocks(a):
        a = a.reshape(bsz, L, dil, nh, hd).transpose(0, 2, 1, 3, 4)
        a = jnp.pad(a, ((0, 0), (0, 0), (0, lp - L), (0, 0), (0, 0)))
        return a.reshape(bsz, dil, nb, n, nh, hd)

    def with_prev(a):
        prev = jnp.pad(a, ((0, 0), (0, 0), (1, 0), (0, 0), (0, 0), (0, 0)))[:, :, :-1]
        return jnp.concatenate([prev, a], axis=3)

    qb = to_blocks(q)
    kb = with_prev(to_blocks(k))
    vb = with_prev(to_blocks(v))
    s = jnp.einsum('brcqhe,brckhe->brchqk', qb, kb).astype(jnp.float32) * B_SCALE
    qi = jnp.arange(n)[:, None]
    kj = jnp.arange(2 * n)[None, :]
    dist = n + qi - kj
    key_idx = jnp.arange(nb)[:, None, None] * n + kj - n
    valid = (dist >= 0) & (dist <= n) & (key_idx >= 0)
    s = jnp.where(valid[None, None, :, None], s, -jnp.inf)
    m = jnp.max(s, axis=-1)
    p = jnp.exp(s - m[..., None])
    l = jnp.sum(p, axis=-1)
    o = jnp.einsum('brchqk,brckhe->brcqhe', p.astype(v.dtype), vb).astype(jnp.float32)
    o = o / jnp.swapaxes(l, -1, -2)[..., None]

    def from_blocks(a):
        a = a.reshape((bsz, dil, lp) + a.shape[4:])[:, :, :L]
        a = jnp.swapaxes(a, 1, 2)
        return a.reshape((bsz, s_len) + a.shape[3:])

    return from_blocks(o), from_blocks(jnp.swapaxes(m, -1, -2)), from_blocks(jnp.swapaxes(l, -1, -2))


def dilated_sample_group(q, k_all, v_all, dil, n):
    t_len = q.shape[1]
    r = k_all.shape[1] - t_len
    idx = r + jnp.arange(t_len)[:, None] - dil * jnp.arange(n + 1)[None, :]
    valid = idx >= 0
    idx = jnp.maximum(idx, 0)
    kg = k_all[:, idx]
    vg = v_all[:, idx]
    s = jnp.einsum('bthe,btkhe->bthk', q, kg).astype(jnp.float32) * B_SCALE
    s = jnp.where(valid[None, :, None, :], s, -jnp.inf)
    m = jnp.max(s, axis=-1)
    p = jnp.exp(s - m[..., None])
    l = jnp.sum(p, axis=-1)
    o = jnp.einsum('bthk,btkhe->bthe', p.astype(v_all.dtype), vg).astype(jnp.float32) / l[..., None]
    return o, m, l


def combine_groups(results):
    o = jnp.stack([r[0] for r in results])
    m = jnp.stack([r[1] for r in results])
    l = jnp.stack([r[2] for r in results])
    w = l * jnp.exp(m - jnp.max(m, axis=0, keepdims=True))
    return jnp.sum(w[..., None] * o, axis=0) / jnp.sum(w, axis=0)[..., None]


def dilated_attention_prompt(h, w_qkv, q_g, k_g, w_out):
    bsz, s_len, _ = h.shape
    q, k, v = qkv_groups(h, w_qkv, q_g, k_g)
    results, new_kv = [], []
    for g in range(B_GROUPS):
        dil = B_DILATIONS[g]
        results.append(dilated_prompt_group(q[:, :, g], k[:, :, g], v[:, :, g], dil, B_WINDOWS[g] // dil))
        keep = min(B_WINDOWS[g], s_len)
        new_kv.append(jnp.stack([k[:, s_len - keep:, g], v[:, s_len - keep:, g]], axis=2))
    o = combine_groups(results)
    y = o.reshape(bsz, s_len, B_WIDTH).astype(h.dtype) @ w_out
    return y, new_kv


def dilated_attention_sample(h, kv_caches, w_qkv, q_g, k_g, w_out):
    bsz, t_len, _ = h.shape
    q, k, v = qkv_groups(h, w_qkv, q_g, k_g)
    results, new_kv = [], []
    for g in range(B_GROUPS):
        dil = B_DILATIONS[g]
        kv = kv_caches[g]
        k_all = jnp.concatenate([kv[:, :, 0], k[:, :, g]], axis=1)
        v_all = jnp.concatenate([kv[:, :, 1], v[:, :, g]], axis=1)
        results.append(dilated_sample_group(q[:, :, g], k_all, v_all, dil, B_WINDOWS[g] // dil))
        new_kv.append(jnp.stack([k[:, :, g], v[:, :, g]], axis=2))
    o = combine_groups(results)
    y = o.reshape(bsz, t_len, B_WIDTH).astype(h.dtype) @ w_out
    return y, new_kv


def pool_mix(seq, n_new, pos0, w_c, c_scale):
    bsz, L, d = seq.shape
    r = L - n_new
    sf = seq.astype(jnp.float32)
    cs = jnp.concatenate([jnp.zeros((bsz, 1, d), jnp.float32), jnp.cumsum(sf, axis=1)], axis=1)
    upper = cs[:, r + 1:]
    pos = pos0 + jnp.arange(n_new)
    outs = []
    for g, w in enumerate(POOL_WINDOWS):
        sl = slice(g * C_GW, (g + 1) * C_GW)
        lo_idx = jnp.maximum(jnp.arange(r + 1, L + 1) - w, 0)
        lower = cs[:, lo_idx, sl]
        cnt = jnp.minimum(w, pos + 1).astype(jnp.float32)
        pooled = (upper[:, :, sl] - lower) / cnt[:, None]
        outs.append(pooled - sf[:, r:, sl])
    z = jnp.stack(outs, axis=2).astype(seq.dtype)
    y = jnp.einsum('btgc,gce->btge', z, w_c).reshape(bsz, n_new, d)
    return y * c_scale


def setup_inputs(seed: int = 0) -> dict:
    key = jax.random.key(seed)
    ks = jax.random.split(key, 32)
    f32 = jnp.float32

    def nrm(k, shape, scale):
        return jax.random.normal(k, shape, f32) * scale

    inp = {}
    inp['x_prompt'] = nrm(ks[0], (BATCH, SEQ, D_MODEL), 1.0)
    inp['x_sample'] = nrm(ks[1], (DEC_BATCH, DEC_SEQ, D_MODEL), 1.0)
    for g in range(B_GROUPS):
        inp['cache_b_kv%d' % g] = nrm(ks[2 + g], (N_B, DEC_BATCH, min(B_WINDOWS[g], PAST_LEN), 2, B_HEADS, B_HD), 1.0)
    inp['state_c_pool'] = nrm(ks[5], (N_C, DEC_BATCH, POOL_STATE, D_MODEL), 1.0)
    inp['norm_mix_g'] = 1.0 + nrm(ks[6], (DEPTH, D_MODEL), 0.02)
    inp['norm_ffn_g'] = 1.0 + nrm(ks[7], (DEPTH, D_MODEL), 0.02)
    inp['a_w_in'] = nrm(ks[8], (N_A, D_MODEL, 2 * A_WIDTH), D_MODEL ** -0.5)
    inp['a_ln_g'] = 1.0 + nrm(ks[9], (N_A, A_WIDTH), 0.02)
    inp['a_ln_b'] = nrm(ks[10], (N_A, A_WIDTH), 0.02)
    inp['a_w_s'] = nrm(ks[11], (N_A, A_HEADS, A_CHUNK, A_CHUNK), A_CHUNK ** -0.5)
    inp['a_b_s'] = 1.0 + nrm(ks[12], (N_A, A_HEADS, A_CHUNK), 0.02)
    inp['a_w_out'] = nrm(ks[13], (N_A, A_WIDTH, D_MODEL), A_WIDTH ** -0.5)
    inp['b_w_qkv'] = nrm(ks[14], (N_B, D_MODEL, B_GROUPS * 3 * B_WIDTH), D_MODEL ** -0.5)
    inp['b_q_g'] = 1.0 + nrm(ks[15], (N_B, B_GROUPS, B_HD), 0.02)
    inp['b_k_g'] = 1.0 + nrm(ks[16], (N_B, B_GROUPS, B_HD), 0.02)
    inp['b_w_out'] = nrm(ks[17], (N_B, B_WIDTH, D_MODEL), B_WIDTH ** -0.5)
    inp['c_w'] = nrm(ks[18], (N_C, C_GROUPS, C_GW, C_GW), C_GW ** -0.5)
    inp['c_scale'] = 1.0 + nrm(ks[19], (N_C, D_MODEL), 0.02)
    inp['ffn_w1'] = nrm(ks[20], (DEPTH, D_MODEL, D_FF), D_MODEL ** -0.5)
    inp['ffn_w3'] = nrm(ks[21], (DEPTH, D_MODEL, D_FF), D_MODEL ** -0.5)
    inp['ffn_w2'] = nrm(ks[22], (DEPTH, D_FF, D_MODEL), D_FF ** -0.5)
    return inp


def reference(x_prompt, x_sample, cache_b_kv0, cache_b_kv1, cache_b_kv2, state_c_pool,
              norm_mix_g, norm_ffn_g,
              a_w_in, a_ln_g, a_ln_b, a_w_s, a_b_s, a_w_out,
              b_w_qkv, b_q_g, b_k_g, b_w_out,
              c_w, c_scale,
              ffn_w1, ffn_w3, ffn_w2):
    xp, xs = x_prompt, x_sample
    n_new = xs.shape[1]
    a_v_s, pool_p, pool_s = [], [], []
    kv_p = [[] for _ in range(B_GROUPS)]
    kv_s = [[] for _ in range(B_GROUPS)]
    ia = ib = ic = 0
    for layer in range(DEPTH):
        kind = layer % N_MIXERS
        hp = rmsnorm(xp, norm_mix_g[layer])
        hs = rmsnorm(xs, norm_mix_g[layer])
        if kind == 0:
            yp, _ = chunk_mlp(hp, a_w_in[ia], a_ln_g[ia], a_ln_b[ia], a_w_s[ia], a_b_s[ia], a_w_out[ia])
            ys, v_s = chunk_mlp(hs, a_w_in[ia], a_ln_g[ia], a_ln_b[ia], a_w_s[ia], a_b_s[ia], a_w_out[ia])
            a_v_s.append(v_s)
            ia += 1
        elif kind == 1:
            yp, kvp = dilated_attention_prompt(hp, b_w_qkv[ib], b_q_g[ib], b_k_g[ib], b_w_out[ib])
            caches = (cache_b_kv0[ib], cache_b_kv1[ib], cache_b_kv2[ib])
            ys, kvs = dilated_attention_sample(hs, caches, b_w_qkv[ib], b_q_g[ib], b_k_g[ib], b_w_out[ib])
            for g in range(B_GROUPS):
                kv_p[g].append(kvp[g])
                kv_s[g].append(kvs[g])
            ib += 1
        else:
            yp = pool_mix(hp, hp.shape[1], 0, c_w[ic], c_scale[ic])
            seq_s = jnp.concatenate([state_c_pool[ic], hs], axis=1)
            ys = pool_mix(seq_s, n_new, PAST_LEN, c_w[ic], c_scale[ic])
            pool_p.append(hp[:, -POOL_STATE:])
            pool_s.append(seq_s[:, -POOL_STATE:])
            ic += 1
        xp = xp + yp
        xs = xs + ys
        xp = xp + swiglu(rmsnorm(xp, norm_ffn_g[layer]), ffn_w1[layer], ffn_w3[layer], ffn_w2[layer])
        xs = xs + swiglu(rmsnorm(xs, norm_ffn_g[layer]), ffn_w1[layer], ffn_w3[layer], ffn_w2[layer])
    new_a_v_sample = jnp.stack(a_v_s)
    new_b_kv0_prompt = jnp.stack(kv_p[0])
    new_b_kv1_prompt = jnp.stack(kv_p[1])
    new_b_kv2_prompt = jnp.stack(kv_p[2])
    new_b_kv0_sample = jnp.stack(kv_s[0])
    new_b_kv1_sample = jnp.stack(kv_s[1])
    new_b_kv2_sample = jnp.stack(kv_s[2])
    new_c_pool_prompt = jnp.stack(pool_p)
    new_c_pool_sample = jnp.stack(pool_s)
    return (xp, xs, new_a_v_sample,
            new_b_kv0_prompt, new_b_kv1_prompt, new_b_kv2_prompt,
            new_b_kv0_sample, new_b_kv1_sample, new_b_kv2_sample,
            new_c_pool_prompt, new_c_pool_sample)
```

```python
import numpy as np
from contextlib import ExitStack
import concourse.bass as bass
import concourse.mybir as mybir
from concourse.bass_utils import run_bass_kernel_spmd

F32 = mybir.dt.float32
BF16 = mybir.dt.bfloat16
AF = mybir.ActivationFunctionType
ALU = mybir.AluOpType
EPS = 1e-6
SEQ = 2048
HALF = 1024
NSMP = 4
POOL_W = (2, 4, 8, 16)
DILS = (1, 4, 16)
NQ0 = 896
NQ = SEQ - NQ0
NOT = NQ + NSMP


class Cfg:
    def __init__(self, D=2048):
        self.D = D
        self.ND = D // 128
        self.H = D // 128
        self.AH = D // 128
        self.DFF = ((8 * D + 3 * 256 - 1) // (3 * 256)) * 256
        self.NF = self.DFF // 128
        self.CGW = D // 4
        self.NCG = self.ND // 4


class _Rec:
    def __init__(self):
        self.call = None

    def __getattr__(self, name):
        def f(*a, **k):
            self.call = (name, a, k)
            return self
        return f


def _capture(fn):
    r = _Rec()
    fn(r)
    name, a, k = r.call
    return lambda eng: getattr(eng, name)(*a, **k)


class Sched:
    NDMA = 12

    def __init__(self, nc, es):
        self.nc = nc
        self.eng = {'pe': None, 'act': None, 'dve': None, 'pool': None, 'sp': None}
        self.prog = {e: [] for e in self.eng}
        self.sem = {e: es.enter_context(nc.semaphore("s_" + e)) for e in self.eng}
        self.cnt = {e: 0 for e in self.eng}
        self.known = {e: {} for e in self.eng}
        self.lastw = {}
        self.readers = {}
        self.dsem = {q: [es.enter_context(nc.semaphore("d_%s%d" % (q, i))) for i in range(self.NDMA)]
                     for q in ('sp', 'pool')}
        self.dtot = {q: [0] * self.NDMA for q in ('sp', 'pool')}
        self.drr = {'sp': 0, 'pool': 0}

    def _semh(self, sk):
        return self.sem[sk[1]] if sk[0] == 'e' else self.dsem[sk[1]][sk[2]]

    def _deps(self, e, reads, writes):
        need = {}
        def add(d):
            if d is None:
                return
            sk, v = d
            if need.get(sk, 0) < v:
                need[sk] = v
        for k in reads:
            add(self.lastw.get(k))
        for k in writes:
            add(self.lastw.get(k))
            for sk, v in self.readers.get(k, {}).items():
                add((sk, v))
        for sk, v in need.items():
            if sk == ('e', 'pe') and e == 'pe':
                continue
            if self.known[e].get(sk, 0) >= v:
                continue
            self._wait(e, self._semh(sk), v)
            self.known[e][sk] = v

    def _wait(self, e, sem, v):
        self.prog[e].append(lambda eng, sem=sem, v=v: eng.wait_ge(sem, v))

    def emit_all(self):
        nc = self.nc
        prog = self.prog
        with nc.Block() as block:
            @block.tensor
            def _(eng):
                for f in prog['pe']:
                    f(eng)

            @block.scalar
            def _(eng):
                for f in prog['act']:
                    f(eng)

            @block.vector
            def _(eng):
                for f in prog['dve']:
                    f(eng)

            @block.gpsimd
            def _(eng):
                for f in prog['pool']:
                    f(eng)

            @block.sync
            def _(eng):
                for f in prog['sp']:
                    f(eng)

    def _record(self, me, reads, writes):
        for k in writes:
            self.lastw[k] = me
            self.readers[k] = {}
        for k in reads:
            r = self.readers.setdefault(k, {})
            if r.get(me[0], 0) < me[1]:
                r[me[0]] = me[1]

    stopped = False

    def op(self, e, fn, reads=(), writes=(), inc=True):
        if self.stopped:
            return
        self._deps(e, reads, writes)
        sem = self.sem[e]
        fn = _capture(fn)
        if inc:
            self.cnt[e] += 1
            self.prog[e].append(lambda eng, fn=fn, sem=sem: fn(eng).then_inc(sem, 1))
            me = (('e', e), self.cnt[e])
        else:
            self.prog[e].append(lambda eng, fn=fn: fn(eng))
            me = (('e', e), self.cnt[e] + 1)
        self._record(me, reads, writes)

    def dma(self, q, out, in_, reads=(), writes=()):
        if self.stopped:
            return
        self._deps(q, reads, writes)
        i = self.drr[q]
        self.drr[q] = (i + 1) % self.NDMA
        sk = ('d', q, i)
        if self.dtot[q][i] > 0 and self.known[q].get(sk, 0) < self.dtot[q][i]:
            self._wait(q, self.dsem[q][i], self.dtot[q][i])
            self.known[q][sk] = self.dtot[q][i]
        self.prog[q].append(lambda eng, out=out, in_=in_, sem=self.dsem[q][i]: eng.dma_start(out=out, in_=in_).then_inc(sem, 16))
        self.dtot[q][i] += 16
        self._record((sk, self.dtot[q][i]), reads, writes)

    def barrier(self, light=False):
        if self.stopped:
            return
        for e in self.eng:
            if light and e == 'pool':
                continue
            for e2 in self.eng:
                if e2 == e or (light and e2 == 'pool'):
                    continue
                v = self.cnt[e2]
                if v > 0 and self.known[e].get(('e', e2), 0) < v:
                    self._wait(e, self.sem[e2], v)
                    self.known[e][('e', e2)] = v
            if self.cnt[e] > 0 and e != 'pe' and self.known[e].get(('e', e), 0) < self.cnt[e]:
                self._wait(e, self.sem[e], self.cnt[e])
                self.known[e][('e', e)] = self.cnt[e]
            for q in ('sp', 'pool'):
                if light and q == 'pool':
                    continue
                for i in range(self.NDMA):
                    v = self.dtot[q][i]
                    sk = ('d', q, i)
                    if v > 0 and self.known[e].get(sk, 0) < v:
                        self._wait(e, self.dsem[q][i], v)
                        self.known[e][sk] = v
        if light:
            self.lastw = {k: v for k, v in self.lastw.items() if isinstance(k, tuple) and k[0] in ('slab', 'wc')}
            self.readers = {k: v for k, v in self.readers.items() if isinstance(k, tuple) and k[0] in ('slab', 'wc')}
        else:
            self.lastw = {}
            self.readers = {}


class _Stop(Exception):
    pass


class WT:
    def __init__(self, ap32, ap16, name):
        self.ap32, self.ap16, self.name = ap32, ap16, name

    def __getitem__(self, i):
        return WT(self.ap32[i], None if self.ap16 is None else self.ap16[i], self.name + "/" + str(i))


class Builder:
    stop = None

    def ck(self, k):
        if self.stop is not None and self.stop == k:
            self.S.barrier()
            self.S.stopped = True

    def __init__(self, cfg):
        self.cfg = cfg
        self.nc = bass.Bass("TRN2", target_bir_lowering=False)
        self.uid = 0

    def din(self, name, shape, dt=F32):
        return self.nc.dram_tensor(name, list(shape), dt, kind="ExternalInput").ap()

    def dout(self, name, shape, dt=F32):
        return self.nc.dram_tensor(name, list(shape), dt, kind="ExternalOutput").ap()

    def dscr(self, name, shape, dt):
        return self.nc.dram_tensor(name, list(shape), dt, kind="Internal").ap()

    def sb(self, es, name, shape, dt):
        return es.enter_context(self.nc.sbuf_tensor("sb_" + name, list(shape), dt))

    def next_ps(self):
        i = self.ps_rr
        self.ps_rr = (i + 1) % len(self.ps)
        return self.ps[i], ('ps', i)

    def next_pt(self):
        i = self.pt_rr
        self.pt_rr = (i + 1) % len(self.pt)
        return self.pt[i], ('pt', i)

    def load_slab(self, spec):
        W, r0, nk, c0, n = spec
        i = self.slab_rr
        self.slab_rr = (i + 1) % len(self.slabs)
        view = self.slabs[i][:, 0:nk * n].rearrange("p (k n) -> p k n", k=nk)
        src32 = W.ap32[r0 * 128:(r0 + nk) * 128, c0:c0 + n].rearrange("(k p) n -> p k n", p=128)
        if W.ap16 is None:
            self.S.dma('pool', view, src32, writes=[('slab', i)])
            return view, ('slab', i)
        src16 = W.ap16[r0 * 128:(r0 + nk) * 128, c0:c0 + n].rearrange("(k p) n -> p k n", p=128)
        key = (W.name, r0, nk, c0, n)
        ck = ('wc',) + key
        if key in self.wcached:
            self.S.dma('pool', view, src16, reads=[ck], writes=[('slab', i)])
        else:
            self.S.dma('pool', view, src32, writes=[('slab', i)])
            self.S.dma('sp', src16, view, reads=[('slab', i)], writes=[ck])
            self.wcached.add(key)
        return view, ('slab', i)

    def wslab(self, W, r0, nk, c0, n):
        return (W, r0, nk, c0, n)

    def norm_T(self, xg, tiles, gcol, hT, hkey, xkey='xg'):
        S, cfg = self.S, self.cfg
        D, ND = cfg.D, cfg.ND
        nst = D // 512 if D >= 512 else 1
        for t, (c0, rows) in enumerate(tiles):
            b = self.nrm_rr
            self.nrm_rr = (b + 1) % 2
            bx = b % len(self.xn)
            st, mv, xn = self.nst[b], self.nmv[b], self.xn[bx]
            kk = ('nrm', b)
            for c in range(nst):
                S.op('dve', lambda e, c=c: e.bn_stats(out=st[:rows, c * 6:(c + 1) * 6],
                                                      in_=xg[:rows, t, c * 512:(c + 1) * 512]),
                     reads=[(xkey, t)], writes=[kk])
            S.op('dve', lambda e: e.bn_aggr(out=mv[:rows, 0:2], in_=st[:rows, 0:nst * 6]), reads=[kk], writes=[kk])
            S.op('dve', lambda e: e.scalar_tensor_tensor(out=mv[:rows, 2:3], in0=mv[:rows, 0:1], scalar=mv[:rows, 0:1],
                                                         in1=mv[:rows, 1:2], op0=ALU.mult, op1=ALU.add),
                 reads=[kk], writes=[kk])
            S.op('act', lambda e: e.activation(out=mv[:rows, 3:4], in_=mv[:rows, 2:3], func=AF.Sqrt, bias=self.epsc[:rows, 0:1], scale=1.0), reads=[kk], writes=[kk])
            S.op('dve', lambda e: e.reciprocal(out=mv[:rows, 3:4], in_=mv[:rows, 3:4]), reads=[kk], writes=[kk])
            S.op('act', lambda e: e.activation(out=xn[:rows, :], in_=xg[:rows, t, :], func=AF.Identity,
                                               scale=mv[:rows, 3:4]),
                 reads=[kk, (xkey, t)], writes=[('xn', bx)])
            for d0 in range(0, ND, 8):
                nd = min(8, ND - d0)
                pt, pk = self.next_pt()
                for j in range(nd):
                    dc = d0 + j
                    S.op('pe', lambda e, j=j, dc=dc: e.transpose(out=pt[:, j * 128:j * 128 + rows],
                                                                 in_=xn[:rows, dc * 128:(dc + 1) * 128],
                                                                 identity=self.ident[:rows, :rows]),
                         reads=[('xn', bx)], writes=[pk], inc=(j == nd - 1))
                src = pt[:, 0:nd * 128].rearrange("p (a b) -> p a b", a=nd)[:, :, 0:rows]
                g3 = gcol[:, d0:d0 + nd].unsqueeze(2).to_broadcast([128, nd, rows])
                S.op('dve', lambda e, src=src, g3=g3, d0=d0, nd=nd: e.tensor_tensor(
                    out=hT[:, d0:d0 + nd, c0:c0 + rows], in0=src, in1=g3, op=ALU.mult),
                    reads=[pk], writes=[hkey])

    def fm_stage(self, W, c0, ncols, KC, actT, akey, ntok, evac):
        S = self.S
        for s0 in range(0, ncols, 512):
            w = min(512, ncols - s0)
            buf, bk = self.load_slab(self.wslab(W, 0, KC, c0 + s0, w))
            for j in range(w // 128):
                ps, pk = self.next_ps()
                for kc in range(KC):
                    S.op('pe', lambda e, kc=kc, j=j, ps=ps, buf=buf: e.matmul(
                        ps[:, 0:ntok], lhsT=buf[:, kc, j * 128:(j + 1) * 128], rhs=actT[:, kc, 0:ntok],
                        start=(kc == 0), stop=(kc == KC - 1)),
                        reads=[bk, akey], writes=[pk], inc=(kc == KC - 1))
                evac((s0 // 128) + j, ps, pk)

    def tm_stage(self, W, r0, c0, ncols, KC, KS, actT, akey, tiles, evac, colw=512):
        S = self.S
        nsub = KC // KS
        for s0 in range(0, ncols, colw):
            w = min(colw, ncols - s0)
            pss = [self.next_ps() for _ in tiles] if nsub > 1 else None
            for sub in range(nsub):
                buf, bk = self.load_slab(self.wslab(W, r0 + sub * KS, KS, c0 + s0, w))
                for t, (tc0, rows) in enumerate(tiles):
                    ps, pk = pss[t] if pss else self.next_ps()
                    for kc in range(KS):
                        first = (sub == 0 and kc == 0)
                        last = (sub == nsub - 1 and kc == KS - 1)
                        S.op('pe', lambda e, kc=kc, ps=ps, buf=buf, tc0=tc0, rows=rows, sub=sub, first=first, last=last:
                             e.matmul(ps[:rows, 0:w], lhsT=actT[:, sub * KS + kc, tc0:tc0 + rows], rhs=buf[:, kc, 0:w],
                                      start=first, stop=last),
                             reads=[bk, akey], writes=[pk], inc=(kc == KS - 1))
                    if sub == nsub - 1:
                        evac(t, s0, w, ps, pk)

    def resid_add(self, xg, xkey='xg'):
        S = self.S
        def evac(t, s0, w, ps, pk):
            rows = self.cur_tiles[t][1]
            S.op('dve', lambda e: e.tensor_tensor(out=xg[:rows, t, s0:s0 + w], in0=xg[:rows, t, s0:s0 + w],
                                                  in1=ps[:rows, 0:w], op=ALU.add),
                 reads=[pk, (xkey, t)], writes=[(xkey, t)])
        return evac

    def ffn(self, l, xg, tiles, ntok):
        S, cfg = self.S, self.cfg
        ND, NF = cfg.ND, cfg.NF
        self.cur_tiles = tiles
        hT = self.hT
        self.norm_T(xg, tiles, self.gcols[:, 4 + l, :], hT, 'hT')
        aT = self.mixb[:, 0:NF * 512].rearrange("p (a b) -> p a b", a=NF)
        sg = self.mixb[:, NF * 512:NF * 512 + 1024].rearrange("p (a b) -> p a b", a=2)
        W1, W3, W2 = self.w['ffn_w1'][l], self.w['ffn_w3'][l], self.w['ffn_w2'][l]
        for s0 in range(0, cfg.DFF, 512):
            w = min(512, cfg.DFF - s0)
            b1, k1 = self.load_slab(self.wslab(W1, 0, ND, s0, w))
            b3, k3 = self.load_slab(self.wslab(W3, 0, ND, s0, w))
            for j in range(w // 128):
                fc = s0 // 128 + j
                pg, pgk = self.next_ps()
                pu, puk = self.next_ps()
                for (buf, bk, ps, pk) in ((b1, k1, pg, pgk), (b3, k3, pu, puk)):
                    for kc in range(ND):
                        S.op('pe', lambda e, kc=kc, buf=buf, ps=ps, j=j: e.matmul(
                            ps[:, 0:ntok], lhsT=buf[:, kc, j * 128:(j + 1) * 128], rhs=hT[:, kc, 0:ntok],
                            start=(kc == 0), stop=(kc == ND - 1)),
                            reads=[bk, 'hT'], writes=[pk], inc=(kc == ND - 1))
                sb = fc % 2
                S.op('act', lambda e, sb=sb, pg=pg: e.activation(out=sg[:, sb, 0:ntok], in_=pg[:, 0:ntok], func=AF.Silu),
                     reads=[pgk], writes=[('sg', sb)])
                S.op('dve', lambda e, sb=sb, pu=pu, fc=fc: e.tensor_tensor(out=aT[:, fc, 0:ntok], in0=sg[:, sb, 0:ntok],
                                                                         in1=pu[:, 0:ntok], op=ALU.mult),
                     reads=[puk, ('sg', sb)], writes=['aT'])
        self.tm_stage(W2, 0, 0, cfg.D, NF, NF // 4, aT, 'aT', tiles, self.resid_add(xg))
        S.barrier(light=True)

    def setup_A(self, ia):
        S, cfg = self.S, self.cfg
        AH = cfg.AH
        wsn = self.mix[:, 0:AH * 128].rearrange("p (a b) -> p a b", a=AH)
        wsb = self.mixb[:, 2 * AH * 128:3 * AH * 128].rearrange("p (a b) -> p a b", a=AH)
        S.dma('sp', wsn, self.w['a_w_s'][ia].rearrange("h i j -> i h j"), writes=['wsn'])
        S.dma('sp', self.T2[:, :, :].rearrange("p h i -> p (h i)"),
              self.w['a_b_s'][ia:ia + 1].rearrange("o h i -> o (h i)").to_broadcast([128, AH * 128]), writes=['T2'])
        S.op('dve', lambda e: e.tensor_copy(out=wsb, in_=wsn), reads=['wsn'], writes=['wsb'])
        for h0 in range(0, AH, 4):
            nh = min(4, AH - h0)
            pt, pk = self.next_pt()
            for j in range(nh):
                S.op('pe', lambda e, j=j: e.transpose(out=pt[:, j * 128:(j + 1) * 128], in_=wsb[:, h0 + j, :],
                                                      identity=self.ident[:, :]),
                     reads=['wsb'], writes=[pk], inc=(j == nh - 1))
            src = pt[:, 0:nh * 128].rearrange("p (a b) -> p a b", a=nh)
            m3 = self.amask[:, 3, :].unsqueeze(1).to_broadcast([128, nh, 128])
            S.op('dve', lambda e, src=src, m3=m3, h0=h0, nh=nh: e.tensor_tensor(
                out=self.WsT[:, h0:h0 + nh, :], in0=src, in1=m3, op=ALU.mult), reads=[pk], writes=['WsT'])
            ps, psk = self.next_ps()
            S.op('pe', lambda e, ps=ps, h0=h0, nh=nh: e.matmul(
                ps[:, 0:nh * 128], lhsT=self.ones_bf[:, :], rhs=self.WsT[:, h0:h0 + nh, :].rearrange("p a b -> p (a b)"), start=True, stop=True),
                reads=['WsT'], writes=[psk])
            for j in range(nh):
                h = h0 + j
                S.op('dve', lambda e, ps=ps, j=j, h=h: e.scalar_tensor_tensor(
                    out=self.T2[:, h, :], in0=ps[:, j * 128:(j + 1) * 128], scalar=self.lncols[:, ia, 1, h:h + 1],
                    in1=self.T2[:, h, :], op0=ALU.mult, op1=ALU.add), reads=[psk, 'T2'], writes=['T2'])
        S.barrier()

    def mixer_A(self, ia, layer, xg, tiles, ntok, sample_out=None):
        S, cfg = self.S, self.cfg
        D, ND, AH = cfg.D, cfg.ND, cfg.AH
        AW = D
        self.cur_tiles = tiles
        hT = self.hT
        self.norm_T(xg, tiles, self.gcols[:, layer, :], hT, 'hT')
        uT = self.mixb[:, 0:AH * 512].rearrange("p (a b) -> p a b", a=AH)
        v = self.mix[:, AH * 256:AH * 256 + 4 * AW].rearrange("p (a b) -> p a b", a=4)
        vh = self.mixb[:, AH * 512 + 8 * AW:AH * 512 + 12 * AW].rearrange("p (a b) -> p a b", a=4)
        Win, Wout = self.w['a_w_in'][ia], self.w['a_w_out'][ia]

        def evac_u(fc, ps, pk):
            S.op('act', lambda e: e.activation(out=uT[:, fc, 0:ntok], in_=ps[:, 0:ntok], func=AF.Gelu_apprx_tanh),
                 reads=[pk], writes=['uT'])
        self.fm_stage(Win, 0, AW, ND, hT, 'hT', ntok, evac_u)

        nst = AW // 512
        for p0 in range(0, len(tiles), 4):
            sub = tiles[p0:p0 + 4]

            def evac_v(t, s0, w, ps, pk, sub=sub):
                rows = sub[t][1]
                S.op('act', lambda e: e.activation(out=v[:rows, t, s0:s0 + w], in_=ps[:rows, 0:w],
                                                   func=AF.Gelu_apprx_tanh), reads=[pk], writes=[('v', t)])
            self.tm_stage(Win, 0, AW, AW, ND, ND, hT, 'hT', sub, evac_v)
            for t, (c0, rows) in enumerate(sub):
                tt = p0 + t
                b = self.nrm_rr
                self.nrm_rr = (b + 1) % 2
                st, mv = self.nst[b], self.nmv[b]
                kk = ('nrm', b)
                for c in range(nst):
                    S.op('dve', lambda e, c=c: e.bn_stats(out=st[:rows, c * 6:(c + 1) * 6],
                                                          in_=v[:rows, t, c * 512:(c + 1) * 512]),
                         reads=[('v', t)], writes=[kk])
                S.op('dve', lambda e: e.bn_aggr(out=mv[:rows, 0:2], in_=st[:rows, 0:nst * 6]), reads=[kk], writes=[kk])
                S.op('act', lambda e: e.activation(out=mv[:rows, 3:4], in_=mv[:rows, 1:2], func=AF.Sqrt, bias=self.epsc[:rows, 0:1], scale=1.0), reads=[kk], writes=[kk])
                S.op('dve', lambda e: e.reciprocal(out=mv[:rows, 3:4], in_=mv[:rows, 3:4]), reads=[kk], writes=[kk])
                if sample_out is not None:
                    vo = self.xg[:rows, 1, :]
                    S.op('dve', lambda e: e.tensor_scalar(out=vo, in0=v[:rows, t, :], scalar1=mv[:rows, 0:1],
                                                          scalar2=mv[:rows, 3:4], op0=ALU.subtract, op1=ALU.mult),
                         reads=[kk, ('v', t)], writes=[('xg', 1)])
                    S.op('dve', lambda e: e.tensor_tensor(out=vo, in0=vo, in1=self.xg[:rows, 2, :], op=ALU.mult),
                         reads=[('xg', 1), ('xg', 2)], writes=[('xg', 1)])
                    S.op('dve', lambda e: e.tensor_tensor(out=vo, in0=vo, in1=self.xg[:rows, 3, :], op=ALU.add),
                         reads=[('xg', 1), ('xg', 3)], writes=[('xg', 1)])
                    S.dma('sp', sample_out, vo, reads=[('xg', 1)])
                S.op('dve', lambda e, tt=tt: e.tensor_scalar(out=vh[:rows, tt, :], in0=v[:rows, t, :],
                                                             scalar1=mv[:rows, 0:1], scalar2=mv[:rows, 3:4],
                                                             op0=ALU.subtract, op1=ALU.mult),
                     reads=[kk, ('v', t)], writes=[('vh', tt)])
        for t, (c0, rows) in enumerate(tiles):
            for h0 in range(0, AH, 4):
                nh = min(4, AH - h0)
                ps, pk = self.next_ps()
                for j in range(nh):
                    h = h0 + j
                    S.op('pe', lambda e, ps=ps, j=j, h=h: e.matmul(
                        ps[:, j * 128:j * 128 + rows], lhsT=vh[:rows, t, h * 128:(h + 1) * 128],
                        rhs=self.WsT[:rows, h, 0:rows], start=True, stop=True),
                        reads=[('vh', t), 'WsT'], writes=[pk], inc=(j == nh - 1))
                for j in range(nh):
                    h = h0 + j
                    tb = self.mtmp_rr
                    self.mtmp_rr = (tb + 1) % 2
                    tmp = self.mtmp[tb]
                    S.op('dve', lambda e, ps=ps, j=j, h=h, tmp=tmp: e.scalar_tensor_tensor(
                        out=tmp[:, 0:rows], in0=ps[:, j * 128:j * 128 + rows], scalar=self.lncols[:, ia, 0, h:h + 1],
                        in1=self.T2[:, h, 0:rows], op0=ALU.mult, op1=ALU.add),
                        reads=[pk, 'T2'], writes=[('mtmp', tb)])
                    S.op('dve', lambda e, h=h, tmp=tmp: e.tensor_tensor(
                        out=uT[:, h, c0:c0 + rows], in0=uT[:, h, c0:c0 + rows], in1=tmp[:, 0:rows], op=ALU.mult),
                        reads=[('mtmp', tb), 'uT'], writes=['uT'])
        self.tm_stage(Wout, 0, 0, D, AH, AH, uT, 'uT', tiles, self.resid_add(xg))
        S.barrier(light=True)

    def load_x(self, src, r0, tiles):
        for t, (c0, rows) in enumerate(tiles):
            self.S.dma('sp', self.xg[:rows, t, :], src[r0 + c0:r0 + c0 + rows, :], writes=[('xg', t)])

    def store_x(self, dst, r0, tiles, only=None):
        for t, (c0, rows) in enumerate(tiles):
            if only is not None and t not in only:
                continue
            self.S.dma('sp', dst[r0 + c0:r0 + c0 + rows, :], self.xg[:rows, t, :], reads=[('xg', t)])

    def build(self):
        cfg, nc = self.cfg, self.nc
        D, ND, H, AH, NF = cfg.D, cfg.ND, cfg.H, cfg.AH, cfg.NF
        w = {}
        self.w = w
        xA = self.din("xA", [HALF, D]); xB = self.din("xB", [HALF, D]); xS = self.din("xS", [NSMP, D])
        cache = [self.din("cache%d" % g, [(128, 512, 2048)[g], 2, H, 128]) for g in range(3)]
        state = self.din("state", [15, D])
        gcols_d = self.din("gcols", [128, 8, ND])
        lncols_d = self.din("lncols", [128, 2, 2, AH])
        lnrep_d = self.din("lnrep", [2, 2, D])
        qkg_d = self.din("qkg", [1, 6 * 128])
        g2_d = self.din("g2row", [1, D])
        cs_d = self.din("csrow", [1, D])
        amask_d = self.din("amask", [128, 5, 128])
        smask_d = self.din("smask", [128, 7, 4])
        bm_d = self.din("poolp", [128, 4, 4, 128])
        bss_d = self.din("poolss", [15, 4, 4])
        bsn_d = self.din("poolsn", [4, 4, 4])
        ident_d = self.din("ident", [128, 128])
        def wt(name, shape, cache=True):
            ap16 = self.dscr("wc_" + name, shape, BF16) if cache else None
            return WT(self.din(name, shape), ap16, name)
        self.wcached = set()
        w['a_w_in'] = wt("a_w_in", [2, D, 2 * D]); w['a_w_s'] = self.din("a_w_s", [2, AH, 128, 128])
        w['a_b_s'] = self.din("a_b_s", [2, AH, 128]); w['a_w_out'] = wt("a_w_out", [2, D, D])
        w['b_w_qkv'] = wt("b_w_qkv", [1, D, 9 * D], cache=False); w['b_w_out'] = wt("b_w_out", [1, D, D])
        w['c_w'] = wt("c_w", [1, 4, cfg.CGW, cfg.CGW])
        w['ffn_w1'] = wt("ffn_w1", [4, D, cfg.DFF]); w['ffn_w3'] = wt("ffn_w3", [4, D, cfg.DFF])
        w['ffn_w2'] = wt("ffn_w2", [4, cfg.DFF, D])
        yB = self.dout("yB", [HALF, D]); yS = self.dout("yS", [NSMP, D])
        avs = self.dout("avs", [2, NSMP, D])
        kvp = [self.dout("kvp%d" % g, [(128, 512, 1024)[g], 2, H, 128]) for g in range(3)]
        kvs = [self.dout("kvs%d" % g, [NSMP, 2, H, 128]) for g in range(3)]
        poolp = self.dout("poolpo", [15, D]); pools = self.dout("poolso", [15, D])
        hT1s = self.dscr("hT1s", [128, ND, SEQ + NSMP], BF16)
        xscr = self.dscr("xscr", [NQ - HALF + HALF + NSMP, D], F32)
        OTs = self.dscr("OTs", [128, H, NOT], BF16)

        with ExitStack() as es:
            S = Sched(nc, es)
            self.S = S
            self.ps = [es.enter_context(nc.psum_tensor("ps%d" % i, [128, 512], F32)) for i in range(6)]
            self.pt = [es.enter_context(nc.psum_tensor("pt%d" % i, [128, 1024], BF16)) for i in range(2)]
            self.ps_rr = self.pt_rr = self.slab_rr = self.nrm_rr = self.mtmp_rr = 0
            self.ident = self.sb(es, "ident", [128, 128], BF16)
            self.ones_bf = self.sb(es, "ones", [128, 128], BF16)
            self.amask = self.sb(es, "amask", [128, 5, 128], BF16)
            self.smask = self.sb(es, "smask", [128, 7, 4], BF16)
            self.gcols = self.sb(es, "gcols", [128, 8, ND], F32)
            self.lncols = self.sb(es, "lncols", [128, 2, 2, AH], F32)
            self.qkg = self.sb(es, "qkg", [128, 6, 128], F32)
            self.nst = [self.sb(es, "nst%d" % i, [128, 24], F32) for i in range(2)]
            self.nmv = [self.sb(es, "nmv%d" % i, [128, 4], F32) for i in range(2)]
            self.mtmp = [self.sb(es, "mtmp%d" % i, [128, 128], F32) for i in range(2)]
            self.epsc = self.sb(es, "epsc", [128, 1], F32)
            S.dma('pool', self.ident[:, :], ident_d[:, :], writes=['c0'])
            S.dma('pool', self.amask[:, :, :], amask_d[:, :, :], writes=['c1'])
            S.dma('pool', self.smask[:, :, :], smask_d[:, :, :], writes=['c2'])
            S.dma('sp', self.gcols[:, :, :], gcols_d[:, :, :], writes=['c3'])
            S.dma('sp', self.lncols[:, :, :, :], lncols_d[:, :, :, :], writes=['c4'])
            S.dma('sp', self.qkg[:, :, :].rearrange("p a b -> p (a b)"), qkg_d[0:1, :].to_broadcast([128, 6 * 128]),
                  writes=['c5'])
            S.op('dve', lambda e: e.memset(self.ones_bf[:, :], 1.0), writes=['c6'])
            S.op('dve', lambda e: e.memset(self.epsc[:, :], EPS), writes=['c7'])
            S.barrier()
            self.body(locals())
            S.stopped = False
            S.barrier()
            S.emit_all()
        return nc

    def body(self, L):
        cfg, nc, S, w = self.cfg, self.nc, self.S, self.w
        D, ND, H, AH, NF = cfg.D, cfg.ND, cfg.H, cfg.AH, cfg.NF
        xA, xB, xS, cache, state = L['xA'], L['xB'], L['xS'], L['cache'], L['state']
        lnrep_d, g2_d, cs_d, bm_d, bss_d, bsn_d = L['lnrep_d'], L['g2_d'], L['cs_d'], L['bm_d'], L['bss_d'], L['bsn_d']
        yB, yS, avs, kvp, kvs, poolp, pools = L['yB'], L['yS'], L['avs'], L['kvp'], L['kvs'], L['poolp'], L['pools']
        hT1s, xscr, OTs = L['hT1s'], L['xscr'], L['OTs']
        self.ck(0)
        if True:
            full = [(i * 128, 128) for i in range(4)]
            with ExitStack() as p1:
                self.slabs = [self.sb(p1, "slab%d" % i, [128, 8192], BF16) for i in range(4)]
                self.xn = [self.sb(p1, "xn%d" % i, [128, D], BF16) for i in range(2)]
                self.slab_rr = 0
                self.xg = self.sb(p1, "xg", [128, 4, D], F32)
                self.hT = self.sb(p1, "hT", [128, ND, 512], BF16)
                mixw = max(NF * 512 + 1024, AH * 512 + 12 * D, 2 * (ND * 256 + 3 * D + 2048 + 1024))
                self.mixb = self.sb(p1, "mix", [128, mixw], BF16)
                self.mix = self.mixb.bitcast(F32)
                self.WsT = self.sb(p1, "WsT", [128, AH, 128], BF16)
                self.T2 = self.sb(p1, "T2", [128, AH, 128], F32)
                self.setup_A(0)
                self.ck(1)
                groups = [('A', xA, 0, full, 0), ('A', xA, 512, full, 512), ('B', xB, 0, full, 1024),
                          ('B', xB, 512, full, 1536), ('S', xS, 0, [(0, NSMP)], SEQ)]
                for (kind, src, r0, tiles, hcol) in groups:
                    ntok = sum(r for _, r in tiles)
                    self.load_x(src, r0, tiles)
                    so = None
                    if kind == 'S':
                        S.dma('sp', self.xg[:NSMP, 2, :], lnrep_d[0, 0:1, :].to_broadcast([NSMP, D]), writes=[('xg', 2)])
                        S.dma('sp', self.xg[:NSMP, 3, :], lnrep_d[0, 1:2, :].to_broadcast([NSMP, D]), writes=[('xg', 3)])
                        so = avs[0, :, :]
                    self.mixer_A(0, 0, self.xg, tiles, ntok, sample_out=so)
                    self.ck(2)
                    self.ffn(0, self.xg, tiles, ntok)
                    self.ck(3)
                    self.norm_T(self.xg, tiles, self.gcols[:, 1, :], self.hT, 'hT')
                    S.dma('sp', hT1s[:, :, hcol:hcol + ntok], self.hT[:, :, 0:ntok], reads=['hT'])
                    if kind == 'A' and r0 == 512:
                        self.store_x(xscr, 0 - 384, tiles, only=[3])
                    elif kind == 'B':
                        self.store_x(xscr, 128 + r0, tiles)
                    elif kind == 'S':
                        self.store_x(xscr, 128 + HALF, tiles)
                    S.barrier(light=True)
            S.barrier()

            self.ck(4)
            with ExitStack() as p2:
                self.phase2(p2, hT1s, OTs, cache, kvp, kvs)
            S.barrier()

            self.ck(5)
            with ExitStack() as p3:
                self.slabs = [self.sb(p3, "slab3_%d" % i, [128, 8192], BF16) for i in range(4)]
                self.xn = [self.sb(p3, "xn3_%d" % i, [128, D], BF16) for i in range(1)]
                self.slab_rr = 0
                self.xg = self.sb(p3, "xg3", [128, 4, D], F32)
                self.hT = self.sb(p3, "hT3", [128, ND, 512], BF16)
                mixw = max(NF * 512 + 1024, AH * 512 + 12 * D, 2 * (ND * 256 + 3 * D + 2048 + 1024))
                self.mixb = self.sb(p3, "mix3", [128, mixw], BF16)
                self.mix = self.mixb.bitcast(F32)
                self.WsT = self.sb(p3, "WsT3", [128, AH, 128], BF16)
                self.T2 = self.sb(p3, "T23", [128, AH, 128], F32)
                self.h2halo = self.sb(p3, "h2halo", [128, D], F32)
                self.setup_A(1)
                OTg = self.mixb[:, 0:H * 512].rearrange("p (a b) -> p a b", a=H)

                def load_scr(tiles, srows):
                    for t, (c0, rows) in enumerate(tiles):
                        S.dma('sp', self.xg[:rows, t, :], xscr[srows[t]:srows[t] + rows, :], writes=[('xg', t)])

                def store_scr(tiles, srows, only=None):
                    for t, (c0, rows) in enumerate(tiles):
                        if only is None or t in only:
                            S.dma('sp', xscr[srows[t]:srows[t] + rows, :], self.xg[:rows, t, :], reads=[('xg', t)])

                def l1post(tiles, srows):
                    ntok = sum(r for _, r in tiles)
                    self.cur_tiles = tiles
                    load_scr(tiles, srows)
                    for t, (c0, rows) in enumerate(tiles):
                        S.dma('sp', OTg[:, :, c0:c0 + rows], OTs[:, :, srows[t]:srows[t] + rows], writes=['OTg'])
                    self.tm_stage(w['b_w_out'][0], 0, 0, D, H, H, OTg, 'OTg', tiles, self.resid_add(self.xg))
                    S.barrier(light=True)
                    self.ffn(1, self.xg, tiles, ntok)

                def rest(kind, tiles, first_b, pool_out, ydst, yrow):
                    ntok = sum(r for _, r in tiles)
                    self.cur_tiles = tiles
                    if kind == 'S':
                        self.pool_sample(state, pools, g2_d, cs_d, bss_d, bsn_d)
                    else:
                        self.pool_prompt('B', tiles, ntok, g2_d, cs_d, bm_d, first_b, pool_out)
                    self.ffn(2, self.xg, tiles, ntok)
                    so = None
                    if kind == 'S':
                        S.dma('sp', self.xg[:NSMP, 2, :], lnrep_d[1, 0:1, :].to_broadcast([NSMP, D]), writes=[('xg', 2)])
                        S.dma('sp', self.xg[:NSMP, 3, :], lnrep_d[1, 1:2, :].to_broadcast([NSMP, D]), writes=[('xg', 3)])
                        so = avs[1, :, :]
                    self.mixer_A(1, 3, self.xg, tiles, ntok, sample_out=so)
                    self.ffn(3, self.xg, tiles, ntok)
                    self.store_x(ydst, yrow, tiles)
                    S.barrier(light=True)

                rows_b0 = [128, 256, 384, 512]
                rows_b1 = [640, 768, 896, 1024]
                t_hs = [(0, 128), (128, NSMP)]
                t_s = [(0, NSMP)]
                l1post(full, rows_b0)
                store_scr(full, rows_b0)
                S.barrier(light=True)
                l1post(t_hs, [0, 128 + HALF])
                self.pool_prompt('H', t_hs, 132, g2_d, cs_d, bm_d, False, None)
                store_scr(t_hs, [0, 128 + HALF], only=[1])
                S.barrier(light=True)
                self.ck(6)
                load_scr(full, rows_b0)
                rest('B', full, True, None, yB, 0)
                l1post(full, rows_b1)
                rest('B', full, False, poolp, yB, 512)
                load_scr(t_s, [128 + HALF])
                rest('S', t_s, False, None, yS, 0)
            S.barrier()

    def pool_common(self, tiles, ntok, zT, csrep):
        S, cfg = self.S, self.cfg
        NCG, CGW = cfg.NCG, cfg.CGW
        for pg in range(4):
            buf, bk = self.load_slab(self.wslab(self.w['c_w'][0][pg], 0, cfg.NCG, 0, cfg.CGW))
            for t, (c0, rows) in enumerate(tiles):
                for s0 in range(0, CGW, 512):
                    wd = min(512, CGW - s0)
                    ps, pk = self.next_ps()
                    for kc in range(NCG):
                        S.op('pe', lambda e, kc=kc, ps=ps, buf=buf: e.matmul(
                            ps[:rows, 0:wd], lhsT=zT[:, pg * NCG + kc, c0:c0 + rows], rhs=buf[:, kc, s0:s0 + wd],
                            start=(kc == 0), stop=(kc == NCG - 1)),
                            reads=[bk, 'zT'], writes=[pk], inc=(kc == NCG - 1))
                    col = pg * CGW + s0
                    tb = self.mtmp_rr
                    self.mtmp_rr = (tb + 1) % 2
                    tmp = self.ptmp[tb]
                    S.op('dve', lambda e, ps=ps, tmp=tmp: e.tensor_tensor(out=tmp[:rows, 0:wd], in0=ps[:rows, 0:wd],
                                                                        in1=csrep[:rows, col:col + wd], op=ALU.mult),
                         reads=[pk, 'csrep'], writes=[('ptmp', tb)])
                    S.op('dve', lambda e, tmp=tmp, t=t: e.tensor_tensor(
                        out=self.xg[:rows, t, col:col + wd], in0=self.xg[:rows, t, col:col + wd], in1=tmp[:rows, 0:wd],
                        op=ALU.add), reads=[('ptmp', tb), ('xg', t)], writes=[('xg', t)])

    def pool_regions(self):
        cfg = self.cfg
        D, ND = cfg.D, cfg.ND
        zT = self.mixb[:, 0:ND * 512].rearrange("p (a b) -> p a b", a=ND)
        o = ND * 256
        g2rep = self.mix[:, o:o + D]
        csrep = self.mix[:, o + D:o + 2 * D]
        bm = self.mix[:, o + 2 * D:o + 2 * D + 2048].rearrange("p (a b c) -> p a b c", a=4, b=4)
        h2 = self.mix[:, o + 2 * D + 2048:o + 3 * D + 2048]
        pt0 = self.mix[:, o + 3 * D + 2048:o + 3 * D + 2048 + 512]
        pt1 = self.mix[:, o + 3 * D + 2560:o + 3 * D + 2560 + 512]
        self.ptmp = [pt0, pt1]
        return zT, g2rep, csrep, bm, h2

    def h2_rows(self, t, rows, g2rep, dst, dkey):
        S, cfg = self.S, self.cfg
        D = cfg.D
        nst = D // 512
        b = self.nrm_rr
        self.nrm_rr = (b + 1) % 2
        st, mv = self.nst[b], self.nmv[b]
        kk = ('nrm', b)
        for c in range(nst):
            S.op('dve', lambda e, c=c: e.bn_stats(out=st[:rows, c * 6:(c + 1) * 6],
                                                  in_=self.xg[:rows, t, c * 512:(c + 1) * 512]),
                 reads=[('xg', t)], writes=[kk])
        S.op('dve', lambda e: e.bn_aggr(out=mv[:rows, 0:2], in_=st[:rows, 0:nst * 6]), reads=[kk], writes=[kk])
        S.op('dve', lambda e: e.scalar_tensor_tensor(out=mv[:rows, 2:3], in0=mv[:rows, 0:1], scalar=mv[:rows, 0:1],
                                                     in1=mv[:rows, 1:2], op0=ALU.mult, op1=ALU.add),
             reads=[kk], writes=[kk])
        S.op('act', lambda e: e.activation(out=mv[:rows, 3:4], in_=mv[:rows, 2:3], func=AF.Sqrt, bias=self.epsc[:rows, 0:1], scale=1.0), reads=[kk], writes=[kk])
        S.op('dve', lambda e: e.reciprocal(out=mv[:rows, 3:4], in_=mv[:rows, 3:4]), reads=[kk], writes=[kk])
        S.op('dve', lambda e: e.scalar_tensor_tensor(out=dst[:rows, :], in0=self.xg[:rows, t, :], scalar=mv[:rows, 3:4],
                                                     in1=g2rep[:rows, :], op0=ALU.mult, op1=ALU.mult),
             reads=[kk, ('xg', t), 'g2rep'], writes=[dkey])

    def pool_prompt(self, kind, tiles, ntok, g2_d, cs_d, bm_d, first_b, pool_out):
        S, cfg = self.S, self.cfg
        D, ND, NCG = cfg.D, cfg.ND, cfg.NCG
        zT, g2rep, csrep, bm, h2 = self.pool_regions()
        S.dma('sp', g2rep, g2_d[0:1, :].to_broadcast([128, D]), writes=['g2rep'])
        if kind == 'H':
            self.h2_rows(0, 128, g2rep, h2, 'h2')
            S.dma('sp', self.h2halo[0:32, :], h2[96:128, :], reads=['h2'], writes=['h2halo'])
            return
        S.dma('sp', csrep, cs_d[0:1, :].to_broadcast([128, D]), writes=['csrep'])
        S.dma('sp', bm, bm_d[:, :, :, :], writes=['bm'])
        for t, (c0, rows) in enumerate(tiles):
            self.h2_rows(t, rows, g2rep, h2, 'h2')
            kc_, kp_ = (2, 3) if (first_b and t == 0) else (0, 1)
            for d0 in range(0, ND, 4):
                ps, pk = self.next_ps()
                for j in range(4):
                    dc = d0 + j
                    pg = dc // NCG
                    S.op('pe', lambda e, ps=ps, j=j, dc=dc, pg=pg: e.matmul(
                        ps[:, j * 128:(j + 1) * 128], lhsT=self.h2halo[0:32, dc * 128:(dc + 1) * 128],
                        rhs=bm[0:32, pg, kp_, :], start=True, stop=False),
                        reads=['h2halo', 'bm'], writes=[pk], inc=False)
                    S.op('pe', lambda e, ps=ps, j=j, dc=dc, pg=pg: e.matmul(
                        ps[:, j * 128:(j + 1) * 128], lhsT=h2[:, dc * 128:(dc + 1) * 128],
                        rhs=bm[:, pg, kc_, :], start=False, stop=True),
                        reads=['h2', 'bm'], writes=[pk], inc=(j == 3))
                src = ps[:, 0:512].rearrange("p (a b) -> p a b", a=4)
                S.op('act', lambda e, src=src, d0=d0: e.activation(out=zT[:, d0:d0 + 4, c0:c0 + rows], in_=src, func=AF.Identity),
                     reads=[pk], writes=['zT'])
            S.dma('sp', self.h2halo[0:32, :], h2[96:128, :], reads=['h2'], writes=['h2halo'])
            if pool_out is not None and t == len(tiles) - 1:
                S.dma('sp', pool_out[0:15, :], h2[113:128, :], reads=['h2'])
        self.pool_common(tiles, ntok, zT, csrep)
        S.barrier(light=True)

    def pool_sample(self, state, pools, g2_d, cs_d, bss_d, bsn_d):
        S, cfg = self.S, self.cfg
        D, ND, NCG = cfg.D, cfg.ND, cfg.NCG
        zT, g2rep, csrep, bm, h2 = self.pool_regions()
        tiles = [(0, NSMP)]
        S.dma('sp', g2rep, g2_d[0:1, :].to_broadcast([128, D]), writes=['g2rep'])
        S.dma('sp', csrep, cs_d[0:1, :].to_broadcast([128, D]), writes=['csrep'])
        bss = bm[0:15, 0, :, 0:4]
        bsn = bm[0:4, 1, :, 0:4]
        S.dma('sp', bss, bss_d[:, :, :], writes=['bm'])
        S.dma('sp', bsn, bsn_d[:, :, :], writes=['bm'])
        st15 = self.h2halo
        S.dma('sp', st15[0:15, :], state[:, :], writes=['h2halo'])
        S.dma('sp', pools[0:11, :], state[4:15, :])
        self.h2_rows(0, NSMP, g2rep, h2, 'h2')
        S.dma('sp', pools[11:15, :], h2[0:NSMP, :], reads=['h2'])
        for d0 in range(0, ND, 4):
            ps, pk = self.next_ps()
            for j in range(4):
                dc = d0 + j
                pg = dc // NCG
                S.op('pe', lambda e, ps=ps, j=j, dc=dc, pg=pg: e.matmul(
                    ps[:, j * 128:j * 128 + NSMP], lhsT=st15[0:15, dc * 128:(dc + 1) * 128], rhs=bss[:, pg, :],
                    start=True, stop=False), reads=['h2halo', 'bm'], writes=[pk], inc=False)
                S.op('pe', lambda e, ps=ps, j=j, dc=dc, pg=pg: e.matmul(
                    ps[:, j * 128:j * 128 + NSMP], lhsT=h2[0:NSMP, dc * 128:(dc + 1) * 128], rhs=bsn[:, pg, :],
                    start=False, stop=True), reads=['h2', 'bm'], writes=[pk], inc=(j == 3))
            src = ps[:, 0:512].rearrange("p (a b) -> p a b", a=4)[:, :, 0:NSMP]
            S.op('act', lambda e, src=src, d0=d0: e.activation(out=zT[:, d0:d0 + 4, 0:NSMP], in_=src, func=AF.Identity),
                 reads=[pk], writes=['zT'])
        self.pool_common(tiles, NSMP, zT, csrep)
        S.barrier(light=True)

    def phase2(self, p2, hT1s, OTs, cache, kvp, kvs):
        S, cfg, nc = self.S, self.cfg, self.nc
        D, ND, H = cfg.D, cfg.ND, cfg.H
        NTK = SEQ + NSMP
        scale = 128.0 ** -0.5
        self.slabs = [self.sb(p2, "slab2_%d" % i, [128, ND * 256], BF16) for i in range(5)]
        self.slab_rr = 0
        hT1 = self.sb(p2, "hT1", [128, ND, NTK], BF16)
        KT = self.sb(p2, "KT", [128, 2, NTK], BF16)
        QT = self.sb(p2, "QT", [128, 2, NOT], BF16)
        Vb = self.sb(p2, "Vb", [128, 16, 256], BF16)
        Vsb = self.sb(p2, "Vsb", [NSMP, 256], BF16)
        acc = self.sb(p2, "acc", [128, 2, 2, NOT], F32)
        OTb = self.sb(p2, "OTb", [128, 2, NOT], BF16)
        vf = [self.sb(p2, "vf%d" % i, [128, 256], F32) for i in range(2)]
        Pb = [self.sb(p2, "Pb%d" % i, [128, 256], BF16) for i in range(4)]
        kcb = [self.sb(p2, "kcb%d" % i, [128, 256], BF16) for i in range(4)]
        vcb = [self.sb(p2, "vcb%d" % i, [128, 256], BF16) for i in range(4)]
        KcT = [self.sb(p2, "KcT%d" % i, [128, 128], BF16) for i in range(4)]
        Ps = [self.sb(p2, "Ps%d" % i, [128, 8], BF16) for i in range(4)]
        NROT = {'kn': 2, 'vf': 2, 'P': 4, 'kc': 4, 'KcT': 4, 'Ps': 4}
        kraw = self.sb(p2, "kraw", [128, 17, 256], F32)
        ksq = self.sb(p2, "ksq", [128, 17, 256], F32)
        kbf = self.sb(p2, "kbf", [128, 17, 256], BF16)
        kss = self.sb(p2, "kss", [128, 34], F32)
        rr = {'kn': 0, 'vf': 0, 'P': 0, 'kc': 0, 'KcT': 0, 'Ps': 0}

        def nxt(name):
            i = rr[name]
            rr[name] = (i + 1) % NROT[name]
            return i

        for c0 in range(0, NTK, 1024):
            n = min(1024, NTK - c0)
            S.dma('sp', hT1[:, :, c0:c0 + n], hT1s[:, :, c0:c0 + n], writes=['hT1'])
        Wqkv = self.w['b_w_qkv'][0]
        self.ck(10)

        def proj(buf, bk, cols, rows, step=1):
            ps, pk = self.next_ps()
            for kc in range(ND):
                lhsT = hT1[:, kc, cols:cols + (rows - 1) * step + 1:step] if step > 1 else hT1[:, kc, cols:cols + rows]
                S.op('pe', lambda e, kc=kc, lhsT=lhsT, ps=ps: e.matmul(ps[:rows, 0:256], lhsT=lhsT, rhs=buf[:, kc, :],
                                                                      start=(kc == 0), stop=(kc == ND - 1)),
                     reads=[bk, 'hT1'], writes=[pk], inc=(kc == ND - 1))
            return ps, pk

        def qk_norm(ps, pk, rows, gi):
            i = nxt('kn')
            f, b_, st = knf[i], knb[i], qst[i]
            S.op('dve', lambda e: e.tensor_copy(out=f[:rows, :], in_=ps[:rows, 0:256]),
                 reads=[pk], writes=[('knf', i)])
            for h in range(2):
                S.op('dve', lambda e, h=h: e.bn_stats(out=st[:rows, 0:6], in_=f[:rows, h * 128:(h + 1) * 128]),
                     reads=[('knf', i)], writes=[('qst', i)])
                S.op('dve', lambda e: e.bn_aggr(out=st[:rows, 6:8], in_=st[:rows, 0:6]), reads=[('qst', i)],
                     writes=[('qst', i)])
                S.op('dve', lambda e: e.scalar_tensor_tensor(out=st[:rows, 0:1], in0=st[:rows, 6:7], scalar=st[:rows, 6:7],
                                                             in1=st[:rows, 7:8], op0=ALU.mult, op1=ALU.add),
                     reads=[('qst', i)], writes=[('qst', i)])
                S.op('act', lambda e: e.activation(out=st[:rows, 1:2], in_=st[:rows, 0:1], func=AF.Sqrt, bias=self.epsc[:rows, 0:1], scale=1.0), reads=[('qst', i)], writes=[('qst', i)])
                S.op('dve', lambda e: e.reciprocal(out=st[:rows, 1:2], in_=st[:rows, 1:2]), reads=[('qst', i)], writes=[('qst', i)])
                S.op('dve', lambda e, h=h: e.scalar_tensor_tensor(
                    out=f[:rows, h * 128:(h + 1) * 128], in0=f[:rows, h * 128:(h + 1) * 128], scalar=st[:rows, 1:2],
                    in1=self.qkg[:rows, gi, :], op0=ALU.mult, op1=ALU.mult),
                    reads=[('qst', i), ('knf', i)], writes=[('knf', i)])
            S.op('act', lambda e: e.activation(out=b_[:rows, :], in_=f[:rows, :], func=AF.Identity),
                 reads=[('knf', i)], writes=[('knb', i)])
            return i

        def to_T(i, rows, dst, dkey, col):
            pt, pk = self.next_pt()
            for h in range(2):
                S.op('pe', lambda e, h=h: e.transpose(out=pt[:, h * 128:h * 128 + rows],
                                                      in_=knb[i][:rows, h * 128:(h + 1) * 128],
                                                      identity=self.ident[:rows, :rows]),
                     reads=[('knb', i)], writes=[pk], inc=(h == 1))
            src = pt[:, 0:256].rearrange("p (a b) -> p a b", a=2)[:, :, 0:rows]
            S.op('act', lambda e: e.activation(out=dst[:, :, col:col + rows], in_=src, func=AF.Identity),
                 reads=[pk], writes=[dkey])

        S.op('dve', lambda e: e.memset(kraw[:, :, :], 0.0), writes=[('kraw', 0), ('kraw', 1)])

        def qk_pc(buf, bkey, tl, s0, R, gi, out_fn):
            n = len(tl)
            kr, kq_, kb_, ks_ = ('kraw', R), ('ksq', R), ('kbf', R), ('kss', R)
            for j, (cols, rows) in enumerate(tl):
                ps, pk = proj(buf, bkey, cols, rows)
                if j % 2 == 0:
                    S.op('act', lambda e, j=j, ps=ps, rows=rows: e.activation(out=kraw[:rows, s0 + j, :], in_=ps[:rows, 0:256],
                                                                            func=AF.Identity), reads=[pk], writes=[kr])
                else:
                    S.op('dve', lambda e, j=j, ps=ps, rows=rows: e.tensor_copy(out=kraw[:rows, s0 + j, :], in_=ps[:rows, 0:256]),
                         reads=[pk], writes=[kr])
            k3 = kraw[:, s0:s0 + n, :].rearrange("p t (h e) -> p (t h) e", h=2)
            q3 = ksq[:, s0:s0 + n, :].rearrange("p t (h e) -> p (t h) e", h=2)
            ss = kss[:, 2 * s0:2 * s0 + 2 * n]
            S.op('dve', lambda e: e.tensor_tensor(out=ksq[:, s0:s0 + n, :], in0=kraw[:, s0:s0 + n, :],
                                                  in1=kraw[:, s0:s0 + n, :], op=ALU.mult), reads=[kr], writes=[kq_])
            S.op('dve', lambda e: e.tensor_reduce(out=ss, in_=q3, axis=mybir.AxisListType.X, op=ALU.add),
                 reads=[kq_], writes=[ks_])
            S.op('act', lambda e: e.activation(out=ss, in_=ss, func=AF.Sqrt, bias=self.epsc[:, 0:1], scale=1.0 / 128.0),
                 reads=[ks_], writes=[ks_])
            S.op('dve', lambda e: e.reciprocal(out=ss, in_=ss), reads=[ks_], writes=[ks_])
            S.op('dve', lambda e: e.tensor_tensor(out=k3, in0=k3, in1=ss.unsqueeze(2).to_broadcast([128, 2 * n, 128]),
                                                  op=ALU.mult), reads=[ks_, kr], writes=[kr])
            S.op('dve', lambda e: e.tensor_tensor(out=k3, in0=k3,
                                                  in1=self.qkg[:, gi, :].unsqueeze(1).to_broadcast([128, 2 * n, 128]),
                                                  op=ALU.mult), reads=[kr], writes=[kr])
            S.op('act', lambda e: e.activation(out=kbf[:, s0:s0 + n, :], in_=kraw[:, s0:s0 + n, :], func=AF.Identity),
                 reads=[kr], writes=[kb_])
            for j, (cols, rows) in enumerate(tl):
                out_fn(s0 + j, cols, rows, kr)

        def qk_T(tl, s0, R, dstT, dkey, dcol):
            n = len(tl)
            kb_ = ('kbf', R)
            for j0 in range(0, n, 4):
                nj = min(4, n - j0)
                pt, ptk = self.next_pt()
                full = all(tl[j0 + jj][1] == 128 for jj in range(nj))
                for jj in range(nj):
                    rows = tl[j0 + jj][1]
                    for h in range(2):
                        S.op('pe', lambda e, jj=jj, h=h, rows=rows: e.transpose(
                            out=pt[:, (jj * 2 + h) * 128:(jj * 2 + h) * 128 + rows],
                            in_=kbf[:rows, s0 + j0 + jj, h * 128:(h + 1) * 128], identity=self.ident[:rows, :rows]),
                            reads=[kb_], writes=[ptk], inc=(jj == nj - 1 and h == 1))
                if full and all(dcol(tl[j0 + jj][0]) == dcol(tl[j0][0]) + 128 * jj for jj in range(nj)):
                    c0 = dcol(tl[j0][0])
                    for h in range(2):
                        src = pt[:, 0:nj * 256].rearrange("p (t h e) -> p t h e", t=nj, h=2)[:, :, h, :]
                        dst = dstT[:, h, c0:c0 + nj * 128].rearrange("p (t e) -> p t e", t=nj)
                        if h == 0:
                            S.op('act', lambda e, src=src, dst=dst: e.activation(out=dst, in_=src, func=AF.Identity),
                                 reads=[ptk], writes=[dkey, ptk])
                        else:
                            S.op('dve', lambda e, src=src, dst=dst: e.tensor_copy(out=dst, in_=src),
                                 reads=[ptk], writes=[dkey, ptk])
                else:
                    for jj in range(nj):
                        cols, rows = tl[j0 + jj]
                        src = pt[:, jj * 256:(jj + 1) * 256].rearrange("p (a b) -> p a b", a=2)[:, :, 0:rows]
                        S.op('act', lambda e, src=src, cols=cols, rows=rows: e.activation(
                            out=dstT[:, :, dcol(cols):dcol(cols) + rows], in_=src, func=AF.Identity),
                            reads=[ptk], writes=[dkey, ptk])

        for hb in range(H // 2):
            h0 = hb * 2
            S.op('dve', lambda e: e.memset(acc[:, :, :, :], 0.0), writes=['acc'])
            for g in range(3):
                dil = DILS[g]
                base = (g * 3) * D + h0 * 128
                bq, kq = self.load_slab(self.wslab(Wqkv, 0, ND, base, 256))
                bk_, kk_ = self.load_slab(self.wslab(Wqkv, 0, ND, base + D, 256))
                bv, kv_ = self.load_slab(self.wslab(Wqkv, 0, ND, base + 2 * D, 256))
                keep0 = (SEQ - (128, 512, 1024)[g])
                def k_out(slot, cols, rows, kr, g=g, h0=h0, keep0=keep0):
                    src = kraw[:rows, slot, :].rearrange("p (a b) -> p a b", a=2)
                    if rows == NSMP:
                        S.dma('sp', kvs[g][:, 0, h0:h0 + 2, :], src, reads=[kr])
                    elif cols >= keep0:
                        S.dma('sp', kvp[g][cols - keep0:cols - keep0 + 128, 0, h0:h0 + 2, :], src, reads=[kr])
                no_out = lambda slot, c, r, kr: None
                tka = [(ti * 128, 128) for ti in range(0, 9)]
                tkb = [(ti * 128, 128) for ti in range(9, 16)] + [(SEQ, NSMP)]
                tqa = [(ti * 128, 128) for ti in range(7, 12)]
                tqb = [(ti * 128, 128) for ti in range(12, 16)] + [(SEQ, NSMP)]
                kcol = lambda c: c
                qcol = lambda c: (NQ if c == SEQ else c - NQ0)
                qk_pc(bk_, kk_, tka, 0, 0, 3 + g, k_out)
                qk_pc(bk_, kk_, tkb, 9, 1, 3 + g, k_out)
                qk_T(tka, 0, 0, KT, 'KT', kcol)
                qk_pc(bq, kq, tqa, 0, 0, g, no_out)
                qk_T(tkb, 9, 1, KT, 'KT', kcol)
                qk_pc(bq, kq, tqb, 9, 1, g, no_out)
                if hb == 0 and g == 0:
                    self.ck(11)
                L = SEQ // dil
                nb = L // 128
                for p in range(16):
                    r, c = divmod(p, nb)
                    tok0 = r + dil * 128 * c
                    ps, pk = proj(bv, kv_, tok0, 128, step=dil)
                    S.op('act', lambda e, p=p, ps=ps: e.activation(out=Vb[:, p, :], in_=ps[:, 0:256], func=AF.Identity),
                         reads=[pk], writes=[('Vb', p)])
                    i0 = 0
                    while tok0 + dil * i0 < keep0:
                        i0 += 1
                        if i0 >= 128:
                            break
                    if i0 < 128 and i0 in (0, 64):
                        j = nxt('vf')
                        S.op('dve', lambda e, j=j, ps=ps: e.tensor_copy(out=vf[j][:, :], in_=ps[:, 0:256]),
                             reads=[pk], writes=[('vf', j), pk])
                        o0 = tok0 + dil * i0 - keep0
                        n = 128 - i0
                        dst = kvp[g][o0:o0 + dil * (n - 1) + 1:dil, 1, h0:h0 + 2, :] if dil > 1 else \
                            kvp[g][o0:o0 + n, 1, h0:h0 + 2, :]
                        S.dma('sp', dst, vf[j][i0:128, :].rearrange("p (a b) -> p a b", a=2), reads=[('vf', j)])
                ps, pk = proj(bv, kv_, SEQ, NSMP)
                S.op('act', lambda e, ps=ps: e.activation(out=Vsb[:, :], in_=ps[:NSMP, 0:256], func=AF.Identity),
                     reads=[pk], writes=['Vsb'])
                j = nxt('vf')
                S.op('dve', lambda e, j=j, ps=ps: e.tensor_copy(out=vf[j][:NSMP, :], in_=ps[:NSMP, 0:256]),
                     reads=[pk], writes=[('vf', j), pk])
                S.dma('sp', kvs[g][:, 1, h0:h0 + 2, :], vf[j][:NSMP, :].rearrange("p (a b) -> p a b", a=2),
                      reads=[('vf', j)])

                qk_T(tqa, 0, 0, QT, 'QT', qcol)
                qk_T(tqb, 9, 1, QT, 'QT', qcol)
                if hb == 0 and g == 0:
                    self.ck(13)
                blocks = []
                if g == 0:
                    for c in range(7, 16):
                        blocks.append((128 * (c - 7), 128, 0,
                                       [(128 * (c - 1), c - 1, 0 if c - 1 <= 7 else 2),
                                        (128 * c, c, 1 if c <= 7 else 3)]))
                elif g == 1:
                    for r in range(4):
                        for c in range(1, 4):
                            i0 = 96 if c == 1 else 0
                            q0 = r + 4 * (128 * c + i0) - NQ0
                            blocks.append((q0, 128 - i0, i0,
                                           [(r + 512 * (c - 1), r * 4 + c - 1, 0 if c - 1 <= 1 else 2),
                                            (r + 512 * c, r * 4 + c, 1 if c <= 1 else 3)]))
                else:
                    for r in range(16):
                        blocks.append((r, 72, 56, [(r, r, 4)]))
                def p_front(h, blk):
                    (q0, nq, i0, kbl) = blk
                    qs = QT[:, h, q0:q0 + dil * (nq - 1) + 1:dil] if dil > 1 else QT[:, h, q0:q0 + nq]
                    ps, pk = self.next_ps()
                    for bi, (k0, vt, mi) in enumerate(kbl):
                        ks = KT[:, h, k0:k0 + dil * 127 + 1:dil] if dil > 1 else KT[:, h, k0:k0 + 128]
                        S.op('pe', lambda e, ps=ps, bi=bi, ks=ks, qs=qs, nq=nq: e.matmul(
                            ps[:, bi * 128:bi * 128 + nq], lhsT=ks, rhs=qs, start=True, stop=True),
                            reads=['KT', 'QT'], writes=[pk], inc=(bi == len(kbl) - 1))
                    pi = nxt('P')
                    P = Pb[pi]
                    nb_ = len(kbl)
                    src = ps[:, 0:nb_ * 128].rearrange("p (a b) -> p a b", a=nb_)[:, :, 0:nq]
                    dstP = P[:, 0:nb_ * 128].rearrange("p (a b) -> p a b", a=nb_)[:, :, 0:nq]
                    S.op('act', lambda e, src=src, dstP=dstP: e.activation(out=dstP, in_=src, func=AF.Exp, scale=scale),
                         reads=[pk], writes=[('P', pi)])
                    for bi, (k0, vt, mi) in enumerate(kbl):
                        S.op('dve', lambda e, bi=bi, mi=mi, P=P, nq=nq, i0=i0: e.tensor_tensor(
                            out=P[:, bi * 128:bi * 128 + nq], in0=P[:, bi * 128:bi * 128 + nq],
                            in1=self.amask[:, mi, i0:i0 + nq], op=ALU.mult),
                            reads=[('P', pi)], writes=[('P', pi)])
                    return (h, blk, P, pi)

                def p_back(st):
                    (h, (q0, nq, i0, kbl), P, pi) = st
                    nb_ = len(kbl)
                    po, pok = self.next_ps()
                    for bi, (k0, vt, mi) in enumerate(kbl):
                        S.op('pe', lambda e, po=po, bi=bi, vt=vt, P=P, nq=nq, nb_=nb_, h=h: e.matmul(
                            po[:, 0:nq], lhsT=Vb[:, vt, h * 128:(h + 1) * 128], rhs=P[:, bi * 128:bi * 128 + nq],
                            start=(bi == 0), stop=(bi == nb_ - 1)),
                            reads=[('P', pi), ('Vb', vt)], writes=[pok], inc=False)
                    for bi, (k0, vt, mi) in enumerate(kbl):
                        S.op('pe', lambda e, po=po, bi=bi, P=P, nq=nq, nb_=nb_: e.matmul(
                            po[:, 256:256 + nq], lhsT=self.ones_bf[:, :], rhs=P[:, bi * 128:bi * 128 + nq],
                            start=(bi == 0), stop=(bi == nb_ - 1)),
                            reads=[('P', pi)], writes=[pok], inc=(bi == nb_ - 1))
                    src2 = po[:, 0:512].rearrange("p (a b) -> p a b", a=2)[:, :, 0:nq]
                    a2 = acc[:, :, h, q0:q0 + dil * (nq - 1) + 1:dil] if dil > 1 else acc[:, :, h, q0:q0 + nq]
                    S.op('dve', lambda e, src2=src2, a2=a2: e.tensor_tensor(out=a2, in0=a2, in1=src2, op=ALU.add),
                         reads=[pok, 'acc'], writes=['acc'])

                pend = []
                for h in range(2):
                    for blk in blocks:
                        pend.append(p_front(h, blk))
                        if len(pend) > 2:
                            p_back(pend.pop(0))
                while pend:
                    p_back(pend.pop(0))

                if hb == 0 and g == 0:
                    self.ck(14)
                nblk = 1 if g == 0 else 4

                def s_A(rho, h, ci):
                    pt, ptk = self.next_pt()
                    S.op('pe', lambda e, pt=pt, ci=ci, h=h: e.transpose(out=pt[:, 0:128], in_=kcb[ci][:, h * 128:(h + 1) * 128],
                                                                      identity=self.ident[:, :]),
                         reads=[('kcb', ci)], writes=[ptk])
                    ki = nxt('KcT')
                    S.op('act', lambda e, pt=pt, ki=ki: e.activation(out=KcT[ki][:, :], in_=pt[:, 0:128], func=AF.Identity),
                         reads=[ptk], writes=[('KcT', ki)])
                    return (rho, h, ci, ki)

                def s_B(st):
                    (rho, h, ci, ki) = st
                    ps, pk = self.next_ps()
                    S.op('pe', lambda e, ps=ps, ki=ki, h=h: e.matmul(ps[:, 0:NSMP], lhsT=KcT[ki][:, :],
                                                                   rhs=QT[:, h, NQ:NQ + NSMP], start=True, stop=True),
                         reads=[('KcT', ki), 'QT'], writes=[pk])
                    if rho == 0:
                        S.op('pe', lambda e, ps=ps, h=h: e.matmul(ps[0:NSMP, 8:8 + NSMP], lhsT=KT[:, h, SEQ:SEQ + NSMP],
                                                                rhs=QT[:, h, NQ:NQ + NSMP], start=True, stop=True),
                             reads=['KT', 'QT'], writes=[pk])
                    si = nxt('Ps')
                    Pq = Ps[si]
                    S.op('act', lambda e, ps=ps, Pq=Pq: e.activation(out=Pq[:, 0:NSMP], in_=ps[:, 0:NSMP], func=AF.Exp,
                                                                   scale=scale), reads=[pk], writes=[('Ps', si)])
                    mi = 0 if g == 0 else 1 + rho
                    S.op('dve', lambda e, Pq=Pq, mi=mi: e.tensor_tensor(out=Pq[:, 0:NSMP], in0=Pq[:, 0:NSMP],
                                                                       in1=self.smask[:, mi, :], op=ALU.mult),
                         reads=[('Ps', si)], writes=[('Ps', si)])
                    if rho == 0:
                        S.op('act', lambda e, ps=ps, Pq=Pq: e.activation(out=Pq[0:NSMP, 4:8], in_=ps[0:NSMP, 8:8 + NSMP],
                                                                       func=AF.Exp, scale=scale),
                             reads=[pk, ('Ps', si)], writes=[('Ps', si)])
                        mn = 5 if g == 0 else 6
                        S.op('dve', lambda e, Pq=Pq, mn=mn: e.tensor_tensor(out=Pq[0:NSMP, 4:8], in0=Pq[0:NSMP, 4:8],
                                                                           in1=self.smask[0:NSMP, mn, :], op=ALU.mult),
                             reads=[('Ps', si)], writes=[('Ps', si)])
                    return (rho, h, ci, si)

                def s_C(st):
                    (rho, h, ci, si) = st
                    Pq = Ps[si]
                    po, pok = self.next_ps()
                    S.op('pe', lambda e, po=po, ci=ci, h=h, Pq=Pq: e.matmul(
                        po[:, 0:NSMP], lhsT=vcb[ci][:, h * 128:(h + 1) * 128], rhs=Pq[:, 0:NSMP],
                        start=True, stop=(rho != 0)), reads=[('vcb', ci), ('Ps', si)], writes=[pok], inc=False)
                    if rho == 0:
                        S.op('pe', lambda e, po=po, h=h, Pq=Pq: e.matmul(
                            po[:, 0:NSMP], lhsT=Vsb[0:NSMP, h * 128:(h + 1) * 128], rhs=Pq[0:NSMP, 4:8],
                            start=False, stop=True), reads=['Vsb', ('Ps', si)], writes=[pok], inc=False)
                    S.op('pe', lambda e, po=po, Pq=Pq: e.matmul(
                        po[:, 256:256 + NSMP], lhsT=self.ones_bf[:, :], rhs=Pq[:, 0:NSMP],
                        start=True, stop=(rho != 0)), reads=[('Ps', si)], writes=[pok], inc=(rho != 0))
                    if rho == 0:
                        S.op('pe', lambda e, po=po, Pq=Pq: e.matmul(
                            po[:, 256:256 + NSMP], lhsT=self.ones_bf[0:NSMP, :], rhs=Pq[0:NSMP, 4:8],
                            start=False, stop=True), reads=[('Ps', si)], writes=[pok])
                    src2 = po[:, 0:512].rearrange("p (a b) -> p a b", a=2)[:, :, 0:NSMP]
                    a2 = acc[:, :, h, NQ:NQ + NSMP]
                    S.op('dve', lambda e, src2=src2, a2=a2: e.tensor_tensor(out=a2, in0=a2, in1=src2, op=ALU.add),
                         reads=[pok, 'acc'], writes=['acc'])

                sunits = []
                for rho in range(nblk):
                    ci = nxt('kc')
                    rsl = slice(0, 128) if g == 0 else slice(rho, rho + dil * 127 + 1, dil)
                    S.dma('pool', kcb[ci][:, :].rearrange("p (a b) -> p a b", a=2), cache[g][rsl, 0, h0:h0 + 2, :],
                          writes=[('kcb', ci)])
                    S.dma('pool', vcb[ci][:, :].rearrange("p (a b) -> p a b", a=2), cache[g][rsl, 1, h0:h0 + 2, :],
                          writes=[('vcb', ci)])
                    for h in range(2):
                        sunits.append((rho, h, ci))
                stA, stB = [], []
                n_u = len(sunits)
                for i in range(n_u + 2):
                    if i < n_u:
                        stA.append(s_A(*sunits[i]))
                    if 0 <= i - 1 < n_u:
                        stB.append(s_B(stA[i - 1]))
                    if 0 <= i - 2 < n_u:
                        s_C(stB[i - 2])
                if hb == 0:
                    self.ck(15 + g)
            S.op('dve', lambda e: e.tensor_scalar(out=acc[:, 1, :, :], in0=acc[:, 1, :, :], scalar1=1e-30, scalar2=None,
                                                  op0=ALU.max), reads=['acc'], writes=['acc'])
            S.op('dve', lambda e: e.reciprocal(out=acc[:, 1, :, :], in_=acc[:, 1, :, :]), reads=['acc'], writes=['acc'])
            S.op('dve', lambda e: e.tensor_tensor(out=OTb[:, :, :], in0=acc[:, 0, :, :], in1=acc[:, 1, :, :], op=ALU.mult),
                 reads=['acc'], writes=['OTb'])
            S.dma('sp', OTs[:, h0:h0 + 2, :], OTb[:, :, :], reads=['OTb'])
            if hb == 0:
                self.ck(18)


def host_consts(cfg, half):
    flag = 1.0 if half == 1 else 0.0
    k = np.arange(128)[:, None]
    q = np.arange(128)[None, :]
    U = (k >= q).astype(np.float32)
    Lm = (k <= q).astype(np.float32)
    L2 = Lm * np.where(k < 64, flag, 1.0)
    amask = np.stack([U * flag, Lm * flag, U, Lm, L2], axis=1).astype(np.float32)
    smask = np.zeros((128, 7, 4), np.float32)
    t = np.arange(4)[None, :]
    smask[:, 0, :] = (k >= t)
    for rho in range(4):
        smask[:, 1 + rho, rho] = 1.0
    smask[:4, 5, :] = (np.arange(4)[:, None] <= t)
    smask[:4, 6, :] = np.eye(4)
    bm = np.zeros((128, 4, 4, 128), np.float32)
    j = np.arange(128)[:, None]
    tt = np.arange(128)[None, :]
    for pg, wd in enumerate(POOL_W):
        cur = ((j <= tt) & (j >= tt - wd + 1)).astype(np.float32) / wd - (j == tt)
        prev = ((j - 32 >= tt - wd + 1) & (j < 32)).astype(np.float32) / wd
        bm[:, pg, 0, :] = cur
        bm[:, pg, 1, :] = prev
        if half == 1:
            bm[:, pg, 2, :] = cur
            bm[:, pg, 3, :] = prev
        else:
            cnt = np.minimum(wd, tt + 1).astype(np.float32)
            bm[:, pg, 2, :] = ((j <= tt) & (j >= tt - wd + 1)).astype(np.float32) / cnt - (j == tt)
            bm[:, pg, 3, :] = 0.0
    bss = np.zeros((15, 4, 4), np.float32)
    bsn = np.zeros((4, 4, 4), np.float32)
    for pg, wd in enumerate(POOL_W):
        for t_ in range(4):
            pos = 15 + t_
            for jj in range(pos - wd + 1, pos + 1):
                if jj < 15:
                    bss[jj, pg, t_] += 1.0 / wd
                else:
                    bsn[jj - 15, pg, t_] += 1.0 / wd
            bsn[t_, pg, t_] -= 1.0
    return amask, smask, bm, bss, bsn


_NC_CACHE = {}


def make_in_maps(cfg, inp, n_cores):
    D, ND, AH = cfg.D, cfg.ND, cfg.AH
    f = lambda a: np.ascontiguousarray(np.asarray(a, dtype=np.float32))
    gcols = np.concatenate([f(inp['norm_mix_g']), f(inp['norm_ffn_g'])], axis=0)
    gcols = np.ascontiguousarray(gcols.reshape(8, ND, 128).transpose(2, 0, 1))
    ln = np.stack([f(inp['a_ln_g']), f(inp['a_ln_b'])], axis=1)
    lncols = np.ascontiguousarray(ln.reshape(2, 2, AH, 128).transpose(3, 0, 1, 2))
    qkg = np.concatenate([f(inp['b_q_g'])[0].reshape(-1), f(inp['b_k_g'])[0].reshape(-1)])[None, :]
    qkg = np.ascontiguousarray(qkg)
    shared = {
        'gcols': gcols, 'lncols': lncols, 'lnrep': np.ascontiguousarray(ln), 'qkg': qkg,
        'g2row': f(inp['norm_mix_g'])[2:3], 'csrow': f(inp['c_scale'])[0:1],
        'ident': np.eye(128, dtype=np.float32),
    }
    for k_ in ('a_w_in', 'a_w_s', 'a_b_s', 'a_w_out', 'b_w_qkv', 'b_w_out', 'c_w', 'ffn_w1', 'ffn_w3', 'ffn_w2'):
        shared[k_] = f(inp[k_])
    xp, xs = f(inp['x_prompt']), f(inp['x_sample'])
    caches = [f(inp[k_]) for k_ in ('cache_b_kv0', 'cache_b_kv1', 'cache_b_kv2')]
    st = f(inp['state_c_pool'])
    consts = {h: host_consts(cfg, h) for h in (0, 1)}
    maps = []
    for c in range(n_cores):
        s, h = divmod(c, 2)
        m = dict(shared)
        m['xB'] = np.ascontiguousarray(xp[s, h * HALF:(h + 1) * HALF])
        m['xA'] = np.ascontiguousarray(xp[s, 0:HALF]) if h == 1 else np.zeros((HALF, D), np.float32)
        m['xS'] = np.ascontiguousarray(xs[c])
        for g in range(3):
            m['cache%d' % g] = np.ascontiguousarray(caches[g][0, c])
        m['state'] = np.ascontiguousarray(st[0, c])
        am, sm, bm, bss, bsn = consts[h]
        m['amask'], m['smask'], m['poolp'], m['poolss'], m['poolsn'] = am, sm, bm, bss, bsn
        maps.append(m)
    return maps


def assemble(cfg, res, n_cores):
    D, H = cfg.D, cfg.H
    nseq = n_cores // 2
    y_prompt = np.stack([np.concatenate([res[2 * s]['yB'], res[2 * s + 1]['yB']], axis=0) for s in range(nseq)])
    y_sample = np.stack([res[c]['yS'] for c in range(n_cores)])
    av = np.stack([res[c]['avs'] for c in range(n_cores)], axis=1)
    kv0p = np.stack([res[2 * s + 1]['kvp0'] for s in range(nseq)])[None]
    kv1p = np.stack([res[2 * s + 1]['kvp1'] for s in range(nseq)])[None]
    kv2p = np.stack([np.concatenate([res[2 * s]['kvp2'], res[2 * s + 1]['kvp2']], axis=0) for s in range(nseq)])[None]
    kvs = [np.stack([res[c]['kvs%d' % g] for c in range(n_cores)])[None] for g in range(3)]
    pp = np.stack([res[2 * s + 1]['poolpo'] for s in range(nseq)])[None]
    psm = np.stack([res[c]['poolso'] for c in range(n_cores)])[None]
    outs = (y_prompt, y_sample, av, kv0p, kv1p, kv2p, kvs[0], kvs[1], kvs[2], pp, psm)
    return tuple(np.ascontiguousarray(o.astype(np.float32)) for o in outs)


def kernel(**inputs):
    cfg = Cfg(2048)
    n_cores = 8
    nc = Builder(cfg).build()
    maps = make_in_maps(cfg, inputs, n_cores)
    res = run_bass_kernel_spmd(nc, maps, core_ids=list(range(n_cores)))
    return assemble(cfg, res.results, n_cores)
```

```python
import numpy as np
from contextlib import ExitStack
import concourse.bass as bass
import concourse.mybir as mybir
from concourse.bass_utils import run_bass_kernel_spmd

F32 = mybir.dt.float32
BF16 = mybir.dt.bfloat16
AF = mybir.ActivationFunctionType
ALU = mybir.AluOpType
EPS = 1e-6
SEQ = 2048
HALF = 1024
NSMP = 4
POOL_W = (2, 4, 8, 16)
DILS = (1, 4, 16)
NQ0 = 896
NQ = SEQ - NQ0
NOT = NQ + NSMP


class Cfg:
    def __init__(self, D=2048):
        self.D = D
        self.ND = D // 128
        self.H = D // 128
        self.AH = D // 128
        self.DFF = ((8 * D + 3 * 256 - 1) // (3 * 256)) * 256
        self.NF = self.DFF // 128
        self.CGW = D // 4
        self.NCG = self.ND // 4


class _Rec:
    def __init__(self):
        self.call = None

    def __getattr__(self, name):
        def f(*a, **k):
            self.call = (name, a, k)
            return self
        return f


def _capture(fn):
    r = _Rec()
    fn(r)
    name, a, k = r.call
    return lambda eng: getattr(eng, name)(*a, **k)


class Sched:
    NDMA = 12

    def __init__(self, nc, es):
        self.nc = nc
        self.eng = {'pe': None, 'act': None, 'dve': None, 'pool': None, 'sp': None}
        self.prog = {e: [] for e in self.eng}
        self.sem = {e: es.enter_context(nc.semaphore("s_" + e)) for e in self.eng}
        self.cnt = {e: 0 for e in self.eng}
        self.known = {e: {} for e in self.eng}
        self.lastw = {}
        self.readers = {}
        self.dsem = {q: [es.enter_context(nc.semaphore("d_%s%d" % (q, i))) for i in range(self.NDMA)]
                     for q in ('sp', 'pool')}
        self.dtot = {q: [0] * self.NDMA for q in ('sp', 'pool')}
        self.drr = {'sp': 0, 'pool': 0}

    def _semh(self, sk):
        return self.sem[sk[1]] if sk[0] == 'e' else self.dsem[sk[1]][sk[2]]

    def _deps(self, e, reads, writes):
        need = {}
        def add(d):
            if d is None:
                return
            sk, v = d
            if need.get(sk, 0) < v:
                need[sk] = v
        for k in reads:
            add(self.lastw.get(k))
        for k in writes:
            add(self.lastw.get(k))
            for sk, v in self.readers.get(k, {}).items():
                add((sk, v))
        for sk, v in need.items():
            if sk == ('e', 'pe') and e == 'pe':
                continue
            if self.known[e].get(sk, 0) >= v:
                continue
            self._wait(e, self._semh(sk), v)
            self.known[e][sk] = v

    def _wait(self, e, sem, v):
        self.prog[e].append(lambda eng, sem=sem, v=v: eng.wait_ge(sem, v))

    def emit_all(self):
        nc = self.nc
        prog = self.prog
        with nc.Block() as block:
            @block.tensor
            def _(eng):
                for f in prog['pe']:
                    f(eng)

            @block.scalar
            def _(eng):
                for f in prog['act']:
                    f(eng)

            @block.vector
            def _(eng):
                for f in prog['dve']:
                    f(eng)

            @block.gpsimd
            def _(eng):
                for f in prog['pool']:
                    f(eng)

            @block.sync
            def _(eng):
                for f in prog['sp']:
                    f(eng)

    def _record(self, me, reads, writes):
        for k in writes:
            self.lastw[k] = me
            self.readers[k] = {}
        for k in reads:
            r = self.readers.setdefault(k, {})
            if r.get(me[0], 0) < me[1]:
                r[me[0]] = me[1]

    stopped = False

    def op(self, e, fn, reads=(), writes=(), inc=True):
        if self.stopped:
            return
        self._deps(e, reads, writes)
        sem = self.sem[e]
        fn = _capture(fn)
        if inc:
            self.cnt[e] += 1
            self.prog[e].append(lambda eng, fn=fn, sem=sem: fn(eng).then_inc(sem, 1))
            me = (('e', e), self.cnt[e])
        else:
            self.prog[e].append(lambda eng, fn=fn: fn(eng))
            me = (('e', e), self.cnt[e] + 1)
        self._record(me, reads, writes)

    def dma(self, q, out, in_, reads=(), writes=()):
        if self.stopped:
            return
        self._deps(q, reads, writes)
        i = self.drr[q]
        self.drr[q] = (i + 1) % self.NDMA
        sk = ('d', q, i)
        if self.dtot[q][i] > 0 and self.known[q].get(sk, 0) < self.dtot[q][i]:
            self._wait(q, self.dsem[q][i], self.dtot[q][i])
            self.known[q][sk] = self.dtot[q][i]
        self.prog[q].append(lambda eng, out=out, in_=in_, sem=self.dsem[q][i]: eng.dma_start(out=out, in_=in_).then_inc(sem, 16))
        self.dtot[q][i] += 16
        self._record((sk, self.dtot[q][i]), reads, writes)

    def barrier(self, light=False):
        if self.stopped:
            return
        for e in self.eng:
            if light and e == 'pool':
                continue
            for e2 in self.eng:
                if e2 == e or (light and e2 == 'pool'):
                    continue
                v = self.cnt[e2]
                if v > 0 and self.known[e].get(('e', e2), 0) < v:
                    self._wait(e, self.sem[e2], v)
                    self.known[e][('e', e2)] = v
            if self.cnt[e] > 0 and e != 'pe' and self.known[e].get(('e', e), 0) < self.cnt[e]:
                self._wait(e, self.sem[e], self.cnt[e])
                self.known[e][('e', e)] = self.cnt[e]
            for q in ('sp', 'pool'):
                if light and q == 'pool':
                    continue
                for i in range(self.NDMA):
                    v = self.dtot[q][i]
                    sk = ('d', q, i)
                    if v > 0 and self.known[e].get(sk, 0) < v:
                        self._wait(e, self.dsem[q][i], v)
                        self.known[e][sk] = v
        if light:
            self.lastw = {k: v for k, v in self.lastw.items() if isinstance(k, tuple) and k[0] in ('slab', 'wc')}
            self.readers = {k: v for k, v in self.readers.items() if isinstance(k, tuple) and k[0] in ('slab', 'wc')}
        else:
            self.lastw = {}
            self.readers = {}


class _Stop(Exception):
    pass


class WT:
    def __init__(self, ap32, ap16, name):
        self.ap32, self.ap16, self.name = ap32, ap16, name

    def __getitem__(self, i):
        return WT(self.ap32[i], None if self.ap16 is None else self.ap16[i], self.name + "/" + str(i))


class Builder:
    stop = None

    def ck(self, k):
        if self.stop is not None and self.stop == k:
            self.S.barrier()
            self.S.stopped = True

    def __init__(self, cfg):
        self.cfg = cfg
        self.nc = bass.Bass("TRN2", target_bir_lowering=False)
        self.uid = 0

    def din(self, name, shape, dt=F32):
        return self.nc.dram_tensor(name, list(shape), dt, kind="ExternalInput").ap()

    def dout(self, name, shape, dt=F32):
        return self.nc.dram_tensor(name, list(shape), dt, kind="ExternalOutput").ap()

    def dscr(self, name, shape, dt):
        return self.nc.dram_tensor(name, list(shape), dt, kind="Internal").ap()

    def sb(self, es, name, shape, dt):
        return es.enter_context(self.nc.sbuf_tensor("sb_" + name, list(shape), dt))

    def next_ps(self):
        i = self.ps_rr
        self.ps_rr = (i + 1) % len(self.ps)
        return self.ps[i], ('ps', i)

    def next_pt(self):
        i = self.pt_rr
        self.pt_rr = (i + 1) % len(self.pt)
        return self.pt[i], ('pt', i)

    def load_slab(self, spec):
        W, r0, nk, c0, n = spec
        i = self.slab_rr
        self.slab_rr = (i + 1) % len(self.slabs)
        view = self.slabs[i][:, 0:nk * n].rearrange("p (k n) -> p k n", k=nk)
        src32 = W.ap32[r0 * 128:(r0 + nk) * 128, c0:c0 + n].rearrange("(k p) n -> p k n", p=128)
        if W.ap16 is None:
            self.S.dma('pool', view, src32, writes=[('slab', i)])
            return view, ('slab', i)
        src16 = W.ap16[r0 * 128:(r0 + nk) * 128, c0:c0 + n].rearrange("(k p) n -> p k n", p=128)
        key = (W.name, r0, nk, c0, n)
        ck = ('wc',) + key
        if key in self.wcached:
            self.S.dma('pool', view, src16, reads=[ck], writes=[('slab', i)])
        else:
            self.S.dma('pool', view, src32, writes=[('slab', i)])
            self.S.dma('sp', src16, view, reads=[('slab', i)], writes=[ck])
            self.wcached.add(key)
        return view, ('slab', i)

    def wslab(self, W, r0, nk, c0, n):
        return (W, r0, nk, c0, n)

    def norm_T(self, xg, tiles, gcol, hT, hkey, xkey='xg'):
        S, cfg = self.S, self.cfg
        D, ND = cfg.D, cfg.ND
        nst = D // 512 if D >= 512 else 1
        for t, (c0, rows) in enumerate(tiles):
            b = self.nrm_rr
            self.nrm_rr = (b + 1) % 2
            st, mv, xn = self.nst[b], self.nmv[b], self.xn[b]
            kk = ('nrm', b)
            for c in range(nst):
                S.op('dve', lambda e, c=c: e.bn_stats(out=st[:rows, c * 6:(c + 1) * 6],
                                                      in_=xg[:rows, t, c * 512:(c + 1) * 512]),
                     reads=[(xkey, t)], writes=[kk])
            S.op('dve', lambda e: e.bn_aggr(out=mv[:rows, 0:2], in_=st[:rows, 0:nst * 6]), reads=[kk], writes=[kk])
            S.op('dve', lambda e: e.scalar_tensor_tensor(out=mv[:rows, 2:3], in0=mv[:rows, 0:1], scalar=mv[:rows, 0:1],
                                                         in1=mv[:rows, 1:2], op0=ALU.mult, op1=ALU.add),
                 reads=[kk], writes=[kk])
            S.op('act', lambda e: e.activation(out=mv[:rows, 3:4], in_=mv[:rows, 2:3], func=AF.Sqrt, bias=self.epsc[:rows, 0:1], scale=1.0), reads=[kk], writes=[kk])
            S.op('dve', lambda e: e.reciprocal(out=mv[:rows, 3:4], in_=mv[:rows, 3:4]), reads=[kk], writes=[kk])
            S.op('act', lambda e: e.activation(out=xn[:rows, :], in_=xg[:rows, t, :], func=AF.Identity,
                                               scale=mv[:rows, 3:4]),
                 reads=[kk, (xkey, t)], writes=[('xn', b)])
            for d0 in range(0, ND, 8):
                nd = min(8, ND - d0)
                pt, pk = self.next_pt()
                for j in range(nd):
                    dc = d0 + j
                    S.op('pe', lambda e, j=j, dc=dc: e.transpose(out=pt[:, j * 128:j * 128 + rows],
                                                                 in_=xn[:rows, dc * 128:(dc + 1) * 128],
                                                                 identity=self.ident[:rows, :rows]),
                         reads=[('xn', b)], writes=[pk], inc=(j == nd - 1))
                src = pt[:, 0:nd * 128].rearrange("p (a b) -> p a b", a=nd)[:, :, 0:rows]
                g3 = gcol[:, d0:d0 + nd].unsqueeze(2).to_broadcast([128, nd, rows])
                S.op('dve', lambda e, src=src, g3=g3, d0=d0, nd=nd: e.tensor_tensor(
                    out=hT[:, d0:d0 + nd, c0:c0 + rows], in0=src, in1=g3, op=ALU.mult),
                    reads=[pk], writes=[hkey])

    def fm_stage(self, W, c0, ncols, KC, actT, akey, ntok, evac):
        S = self.S
        for s0 in range(0, ncols, 512):
            w = min(512, ncols - s0)
            buf, bk = self.load_slab(self.wslab(W, 0, KC, c0 + s0, w))
            for j in range(w // 128):
                ps, pk = self.next_ps()
                for kc in range(KC):
                    S.op('pe', lambda e, kc=kc, j=j, ps=ps, buf=buf: e.matmul(
                        ps[:, 0:ntok], lhsT=buf[:, kc, j * 128:(j + 1) * 128], rhs=actT[:, kc, 0:ntok],
                        start=(kc == 0), stop=(kc == KC - 1)),
                        reads=[bk, akey], writes=[pk], inc=(kc == KC - 1))
                evac((s0 // 128) + j, ps, pk)

    def tm_stage(self, W, r0, c0, ncols, KC, KS, actT, akey, tiles, evac, colw=512):
        S = self.S
        nsub = KC // KS
        for s0 in range(0, ncols, colw):
            w = min(colw, ncols - s0)
            pss = [self.next_ps() for _ in tiles] if nsub > 1 else None
            for sub in range(nsub):
                buf, bk = self.load_slab(self.wslab(W, r0 + sub * KS, KS, c0 + s0, w))
                for t, (tc0, rows) in enumerate(tiles):
                    ps, pk = pss[t] if pss else self.next_ps()
                    for kc in range(KS):
                        first = (sub == 0 and kc == 0)
                        last = (sub == nsub - 1 and kc == KS - 1)
                        S.op('pe', lambda e, kc=kc, ps=ps, buf=buf, tc0=tc0, rows=rows, sub=sub, first=first, last=last:
                             e.matmul(ps[:rows, 0:w], lhsT=actT[:, sub * KS + kc, tc0:tc0 + rows], rhs=buf[:, kc, 0:w],
                                      start=first, stop=last),
                             reads=[bk, akey], writes=[pk], inc=(kc == KS - 1))
                    if sub == nsub - 1:
                        evac(t, s0, w, ps, pk)

    def resid_add(self, xg, xkey='xg'):
        S = self.S
        def evac(t, s0, w, ps, pk):
            rows = self.cur_tiles[t][1]
            S.op('dve', lambda e: e.tensor_tensor(out=xg[:rows, t, s0:s0 + w], in0=xg[:rows, t, s0:s0 + w],
                                                  in1=ps[:rows, 0:w], op=ALU.add),
                 reads=[pk, (xkey, t)], writes=[(xkey, t)])
        return evac

    def ffn(self, l, xg, tiles, ntok):
        S, cfg = self.S, self.cfg
        ND, NF = cfg.ND, cfg.NF
        self.cur_tiles = tiles
        hT = self.hT
        self.norm_T(xg, tiles, self.gcols[:, 4 + l, :], hT, 'hT')
        aT = self.mixb[:, 0:NF * 512].rearrange("p (a b) -> p a b", a=NF)
        sg = self.mixb[:, NF * 512:NF * 512 + 1024].rearrange("p (a b) -> p a b", a=2)
        W1, W3, W2 = self.w['ffn_w1'][l], self.w['ffn_w3'][l], self.w['ffn_w2'][l]
        for s0 in range(0, cfg.DFF, 512):
            w = min(512, cfg.DFF - s0)
            b1, k1 = self.load_slab(self.wslab(W1, 0, ND, s0, w))
            b3, k3 = self.load_slab(self.wslab(W3, 0, ND, s0, w))
            for j in range(w // 128):
                fc = s0 // 128 + j
                pg, pgk = self.next_ps()
                pu, puk = self.next_ps()
                for (buf, bk, ps, pk) in ((b1, k1, pg, pgk), (b3, k3, pu, puk)):
                    for kc in range(ND):
                        S.op('pe', lambda e, kc=kc, buf=buf, ps=ps, j=j: e.matmul(
                            ps[:, 0:ntok], lhsT=buf[:, kc, j * 128:(j + 1) * 128], rhs=hT[:, kc, 0:ntok],
                            start=(kc == 0), stop=(kc == ND - 1)),
                            reads=[bk, 'hT'], writes=[pk], inc=(kc == ND - 1))
                sb = fc % 2
                S.op('act', lambda e, sb=sb, pg=pg: e.activation(out=sg[:, sb, 0:ntok], in_=pg[:, 0:ntok], func=AF.Silu),
                     reads=[pgk], writes=[('sg', sb)])
                S.op('dve', lambda e, sb=sb, pu=pu, fc=fc: e.tensor_tensor(out=aT[:, fc, 0:ntok], in0=sg[:, sb, 0:ntok],
                                                                         in1=pu[:, 0:ntok], op=ALU.mult),
                     reads=[puk, ('sg', sb)], writes=['aT'])
        self.tm_stage(W2, 0, 0, cfg.D, NF, NF // 4, aT, 'aT', tiles, self.resid_add(xg))
        S.barrier(light=True)

    def setup_A(self, ia):
        S, cfg = self.S, self.cfg
        AH = cfg.AH
        wsn = self.mix[:, 0:AH * 128].rearrange("p (a b) -> p a b", a=AH)
        wsb = self.mixb[:, 2 * AH * 128:3 * AH * 128].rearrange("p (a b) -> p a b", a=AH)
        S.dma('sp', wsn, self.w['a_w_s'][ia].rearrange("h i j -> i h j"), writes=['wsn'])
        S.dma('sp', self.T2[:, :, :].rearrange("p h i -> p (h i)"),
              self.w['a_b_s'][ia:ia + 1].rearrange("o h i -> o (h i)").to_broadcast([128, AH * 128]), writes=['T2'])
        S.op('dve', lambda e: e.tensor_copy(out=wsb, in_=wsn), reads=['wsn'], writes=['wsb'])
        for h0 in range(0, AH, 4):
            nh = min(4, AH - h0)
            pt, pk = self.next_pt()
            for j in range(nh):
                S.op('pe', lambda e, j=j: e.transpose(out=pt[:, j * 128:(j + 1) * 128], in_=wsb[:, h0 + j, :],
                                                      identity=self.ident[:, :]),
                     reads=['wsb'], writes=[pk], inc=(j == nh - 1))
            src = pt[:, 0:nh * 128].rearrange("p (a b) -> p a b", a=nh)
            m3 = self.amask[:, 3, :].unsqueeze(1).to_broadcast([128, nh, 128])
            S.op('dve', lambda e, src=src, m3=m3, h0=h0, nh=nh: e.tensor_tensor(
                out=self.WsT[:, h0:h0 + nh, :], in0=src, in1=m3, op=ALU.mult), reads=[pk], writes=['WsT'])
            ps, psk = self.next_ps()
            S.op('pe', lambda e, ps=ps, h0=h0, nh=nh: e.matmul(
                ps[:, 0:nh * 128], lhsT=self.ones_bf[:, :], rhs=self.WsT[:, h0:h0 + nh, :].rearrange("p a b -> p (a b)"), start=True, stop=True),
                reads=['WsT'], writes=[psk])
            for j in range(nh):
                h = h0 + j
                S.op('dve', lambda e, ps=ps, j=j, h=h: e.scalar_tensor_tensor(
                    out=self.T2[:, h, :], in0=ps[:, j * 128:(j + 1) * 128], scalar=self.lncols[:, ia, 1, h:h + 1],
                    in1=self.T2[:, h, :], op0=ALU.mult, op1=ALU.add), reads=[psk, 'T2'], writes=['T2'])
        S.barrier()

    def mixer_A(self, ia, layer, xg, tiles, ntok, sample_out=None):
        S, cfg = self.S, self.cfg
        D, ND, AH = cfg.D, cfg.ND, cfg.AH
        AW = D
        self.cur_tiles = tiles
        hT = self.hT
        self.norm_T(xg, tiles, self.gcols[:, layer, :], hT, 'hT')
        uT = self.mixb[:, 0:AH * 512].rearrange("p (a b) -> p a b", a=AH)
        VP = self.vpass
        v = self.mix[:, AH * 256:AH * 256 + VP * AW].rearrange("p (a b) -> p a b", a=VP)
        vh = self.mixb[:, AH * 512 + 2 * VP * AW:AH * 512 + (2 * VP + 4) * AW].rearrange("p (a b) -> p a b", a=4)
        Win, Wout = self.w['a_w_in'][ia], self.w['a_w_out'][ia]

        def evac_u(fc, ps, pk):
            S.op('act', lambda e: e.activation(out=uT[:, fc, 0:ntok], in_=ps[:, 0:ntok], func=AF.Gelu_apprx_tanh),
                 reads=[pk], writes=['uT'])
        self.fm_stage(Win, 0, AW, ND, hT, 'hT', ntok, evac_u)

        nst = AW // 512
        for p0 in range(0, len(tiles), VP):
            sub = tiles[p0:p0 + VP]

            def evac_v(t, s0, w, ps, pk, sub=sub):
                rows = sub[t][1]
                S.op('act', lambda e: e.activation(out=v[:rows, t, s0:s0 + w], in_=ps[:rows, 0:w],
                                                   func=AF.Gelu_apprx_tanh), reads=[pk], writes=[('v', t)])
            self.tm_stage(Win, 0, AW, AW, ND, ND, hT, 'hT', sub, evac_v)
            for t, (c0, rows) in enumerate(sub):
                tt = p0 + t
                b = self.nrm_rr
                self.nrm_rr = (b + 1) % 2
                st, mv = self.nst[b], self.nmv[b]
                kk = ('nrm', b)
                for c in range(nst):
                    S.op('dve', lambda e, c=c: e.bn_stats(out=st[:rows, c * 6:(c + 1) * 6],
                                                          in_=v[:rows, t, c * 512:(c + 1) * 512]),
                         reads=[('v', t)], writes=[kk])
                S.op('dve', lambda e: e.bn_aggr(out=mv[:rows, 0:2], in_=st[:rows, 0:nst * 6]), reads=[kk], writes=[kk])
                S.op('act', lambda e: e.activation(out=mv[:rows, 3:4], in_=mv[:rows, 1:2], func=AF.Sqrt, bias=self.epsc[:rows, 0:1], scale=1.0), reads=[kk], writes=[kk])
                S.op('dve', lambda e: e.reciprocal(out=mv[:rows, 3:4], in_=mv[:rows, 3:4]), reads=[kk], writes=[kk])
                if sample_out is not None:
                    vo = self.xg[:rows, 1, :]
                    S.op('dve', lambda e: e.tensor_scalar(out=vo, in0=v[:rows, t, :], scalar1=mv[:rows, 0:1],
                                                          scalar2=mv[:rows, 3:4], op0=ALU.subtract, op1=ALU.mult),
                         reads=[kk, ('v', t)], writes=[('xg', 1)])
                    S.op('dve', lambda e: e.tensor_tensor(out=vo, in0=vo, in1=self.xg[:rows, 2, :], op=ALU.mult),
                         reads=[('xg', 1), ('xg', 2)], writes=[('xg', 1)])
                    S.op('dve', lambda e: e.tensor_tensor(out=vo, in0=vo, in1=self.xg[:rows, 3, :], op=ALU.add),
                         reads=[('xg', 1), ('xg', 3)], writes=[('xg', 1)])
                    S.dma('sp', sample_out, vo, reads=[('xg', 1)])
                S.op('dve', lambda e, tt=tt: e.tensor_scalar(out=vh[:rows, tt, :], in0=v[:rows, t, :],
                                                             scalar1=mv[:rows, 0:1], scalar2=mv[:rows, 3:4],
                                                             op0=ALU.subtract, op1=ALU.mult),
                     reads=[kk, ('v', t)], writes=[('vh', tt)])
        for t, (c0, rows) in enumerate(tiles):
            for h0 in range(0, AH, 4):
                nh = min(4, AH - h0)
                ps, pk = self.next_ps()
                for j in range(nh):
                    h = h0 + j
                    S.op('pe', lambda e, ps=ps, j=j, h=h: e.matmul(
                        ps[:, j * 128:j * 128 + rows], lhsT=vh[:rows, t, h * 128:(h + 1) * 128],
                        rhs=self.WsT[:rows, h, 0:rows], start=True, stop=True),
                        reads=[('vh', t), 'WsT'], writes=[pk], inc=(j == nh - 1))
                for j in range(nh):
                    h = h0 + j
                    tb = self.mtmp_rr
                    self.mtmp_rr = (tb + 1) % 2
                    tmp = self.mtmp[tb]
                    S.op('dve', lambda e, ps=ps, j=j, h=h, tmp=tmp: e.scalar_tensor_tensor(
                        out=tmp[:, 0:rows], in0=ps[:, j * 128:j * 128 + rows], scalar=self.lncols[:, ia, 0, h:h + 1],
                        in1=self.T2[:, h, 0:rows], op0=ALU.mult, op1=ALU.add),
                        reads=[pk, 'T2'], writes=[('mtmp', tb)])
                    S.op('dve', lambda e, h=h, tmp=tmp: e.tensor_tensor(
                        out=uT[:, h, c0:c0 + rows], in0=uT[:, h, c0:c0 + rows], in1=tmp[:, 0:rows], op=ALU.mult),
                        reads=[('mtmp', tb), 'uT'], writes=['uT'])
        self.tm_stage(Wout, 0, 0, D, AH, AH, uT, 'uT', tiles, self.resid_add(xg))
        S.barrier(light=True)

    def load_x(self, src, r0, tiles):
        for t, (c0, rows) in enumerate(tiles):
            self.S.dma('sp', self.xg[:rows, t, :], src[r0 + c0:r0 + c0 + rows, :], writes=[('xg', t)])

    def store_x(self, dst, r0, tiles, only=None):
        for t, (c0, rows) in enumerate(tiles):
            if only is not None and t not in only:
                continue
            self.S.dma('sp', dst[r0 + c0:r0 + c0 + rows, :], self.xg[:rows, t, :], reads=[('xg', t)])

    def build(self):
        cfg, nc = self.cfg, self.nc
        D, ND, H, AH, NF = cfg.D, cfg.ND, cfg.H, cfg.AH, cfg.NF
        w = {}
        self.w = w
        xA = self.din("xA", [HALF, D]); xB = self.din("xB", [HALF, D]); xS = self.din("xS", [NSMP, D])
        cache = [self.din("cache%d" % g, [(128, 512, 2048)[g], 2, H, 128]) for g in range(3)]
        state = self.din("state", [15, D])
        gcols_d = self.din("gcols", [128, 8, ND])
        lncols_d = self.din("lncols", [128, 2, 2, AH])
        lnrep_d = self.din("lnrep", [2, 2, D])
        qkg_d = self.din("qkg", [1, 6 * 128])
        g2_d = self.din("g2row", [1, D])
        cs_d = self.din("csrow", [1, D])
        amask_d = self.din("amask", [128, 5, 128])
        smask_d = self.din("smask", [128, 7, 4])
        bm_d = self.din("poolp", [128, 4, 4, 128])
        bss_d = self.din("poolss", [15, 4, 4])
        bsn_d = self.din("poolsn", [4, 4, 4])
        ident_d = self.din("ident", [128, 128])
        def wt(name, shape, cache=True):
            ap16 = self.dscr("wc_" + name, shape, BF16) if cache else None
            return WT(self.din(name, shape), ap16, name)
        self.wcached = set()
        w['a_w_in'] = wt("a_w_in", [2, D, 2 * D]); w['a_w_s'] = self.din("a_w_s", [2, AH, 128, 128])
        w['a_b_s'] = self.din("a_b_s", [2, AH, 128]); w['a_w_out'] = wt("a_w_out", [2, D, D])
        w['b_w_qkv'] = wt("b_w_qkv", [1, D, 9 * D], cache=False); w['b_w_out'] = wt("b_w_out", [1, D, D])
        w['c_w'] = wt("c_w", [1, 4, cfg.CGW, cfg.CGW])
        w['ffn_w1'] = wt("ffn_w1", [4, D, cfg.DFF]); w['ffn_w3'] = wt("ffn_w3", [4, D, cfg.DFF])
        w['ffn_w2'] = wt("ffn_w2", [4, cfg.DFF, D])
        yB = self.dout("yB", [HALF, D]); yS = self.dout("yS", [NSMP, D])
        avs = self.dout("avs", [2, NSMP, D])
        kvp = [self.dout("kvp%d" % g, [(128, 512, 1024)[g], 2, H, 128]) for g in range(3)]
        kvs = [self.dout("kvs%d" % g, [NSMP, 2, H, 128]) for g in range(3)]
        poolp = self.dout("poolpo", [15, D]); pools = self.dout("poolso", [15, D])
        hT1s = self.dscr("hT1s", [128, ND, SEQ + NSMP], BF16)
        xscr = self.dscr("xscr", [NQ - HALF + HALF + NSMP, D], F32)
        OTs = self.dscr("OTs", [128, H, NOT], BF16)

        with ExitStack() as es:
            S = Sched(nc, es)
            self.S = S
            self.ps = [es.enter_context(nc.psum_tensor("ps%d" % i, [128, 512], F32)) for i in range(6)]
            self.pt = [es.enter_context(nc.psum_tensor("pt%d" % i, [128, 1024], BF16)) for i in range(2)]
            self.ps_rr = self.pt_rr = self.slab_rr = self.nrm_rr = self.mtmp_rr = 0
            self.ident = self.sb(es, "ident", [128, 128], BF16)
            self.ones_bf = self.sb(es, "ones", [128, 128], BF16)
            self.amask = self.sb(es, "amask", [128, 5, 128], BF16)
            self.smask = self.sb(es, "smask", [128, 7, 4], BF16)
            self.gcols = self.sb(es, "gcols", [128, 8, ND], F32)
            self.lncols = self.sb(es, "lncols", [128, 2, 2, AH], F32)
            self.qkg = self.sb(es, "qkg", [128, 6, 128], F32)
            self.nst = [self.sb(es, "nst%d" % i, [128, 24], F32) for i in range(2)]
            self.nmv = [self.sb(es, "nmv%d" % i, [128, 4], F32) for i in range(2)]
            self.mtmp = [self.sb(es, "mtmp%d" % i, [128, 128], F32) for i in range(2)]
            self.epsc = self.sb(es, "epsc", [128, 1], F32)
            S.dma('pool', self.ident[:, :], ident_d[:, :], writes=['c0'])
            S.dma('pool', self.amask[:, :, :], amask_d[:, :, :], writes=['c1'])
            S.dma('pool', self.smask[:, :, :], smask_d[:, :, :], writes=['c2'])
            S.dma('sp', self.gcols[:, :, :], gcols_d[:, :, :], writes=['c3'])
            S.dma('sp', self.lncols[:, :, :, :], lncols_d[:, :, :, :], writes=['c4'])
            S.dma('sp', self.qkg[:, :, :].rearrange("p a b -> p (a b)"), qkg_d[0:1, :].to_broadcast([128, 6 * 128]),
                  writes=['c5'])
            S.op('dve', lambda e: e.memset(self.ones_bf[:, :], 1.0), writes=['c6'])
            S.op('dve', lambda e: e.memset(self.epsc[:, :], EPS), writes=['c7'])
            S.barrier()
            self.body(locals())
            S.stopped = False
            S.barrier()
            S.emit_all()
        return nc

    def body(self, L):
        cfg, nc, S, w = self.cfg, self.nc, self.S, self.w
        D, ND, H, AH, NF = cfg.D, cfg.ND, cfg.H, cfg.AH, cfg.NF
        xA, xB, xS, cache, state = L['xA'], L['xB'], L['xS'], L['cache'], L['state']
        lnrep_d, g2_d, cs_d, bm_d, bss_d, bsn_d = L['lnrep_d'], L['g2_d'], L['cs_d'], L['bm_d'], L['bss_d'], L['bsn_d']
        yB, yS, avs, kvp, kvs, poolp, pools = L['yB'], L['yS'], L['avs'], L['kvp'], L['kvs'], L['poolp'], L['pools']
        hT1s, xscr, OTs = L['hT1s'], L['xscr'], L['OTs']
        self.ck(0)
        if True:
            full = [(i * 128, 128) for i in range(4)]
            with ExitStack() as p1:
                self.slabs = [self.sb(p1, "slab%d" % i, [128, 8192], BF16) for i in range(4)]
                self.xn = [self.sb(p1, "xn%d" % i, [128, D], BF16) for i in range(2)]
                self.slab_rr = 0
                self.xg = self.sb(p1, "xg", [128, 4, D], F32)
                self.hT = self.sb(p1, "hT", [128, ND, 512], BF16)
                self.vpass = 4
                mixw = max(NF * 512 + 1024, AH * 512 + 12 * D, 2 * (ND * 256 + 3 * D + 2048 + 1024))
                self.mixb = self.sb(p1, "mix", [128, mixw], BF16)
                self.mix = self.mixb.bitcast(F32)
                self.WsT = self.sb(p1, "WsT", [128, AH, 128], BF16)
                self.T2 = self.sb(p1, "T2", [128, AH, 128], F32)
                self.setup_A(0)
                self.ck(1)
                groups = [('A', xA, 0, full, 0), ('A', xA, 512, full, 512), ('B', xB, 0, full, 1024),
                          ('B', xB, 512, full, 1536), ('S', xS, 0, [(0, NSMP)], SEQ)]
                for (kind, src, r0, tiles, hcol) in groups:
                    ntok = sum(r for _, r in tiles)
                    self.load_x(src, r0, tiles)
                    so = None
                    if kind == 'S':
                        S.dma('sp', self.xg[:NSMP, 2, :], lnrep_d[0, 0:1, :].to_broadcast([NSMP, D]), writes=[('xg', 2)])
                        S.dma('sp', self.xg[:NSMP, 3, :], lnrep_d[0, 1:2, :].to_broadcast([NSMP, D]), writes=[('xg', 3)])
                        so = avs[0, :, :]
                    self.mixer_A(0, 0, self.xg, tiles, ntok, sample_out=so)
                    self.ck(2)
                    self.ffn(0, self.xg, tiles, ntok)
                    self.ck(3)
                    self.norm_T(self.xg, tiles, self.gcols[:, 1, :], self.hT, 'hT')
                    S.dma('sp', hT1s[:, :, hcol:hcol + ntok], self.hT[:, :, 0:ntok], reads=['hT'])
                    if kind == 'A' and r0 == 512:
                        self.store_x(xscr, 0 - 384, tiles, only=[3])
                    elif kind == 'B':
                        self.store_x(xscr, 128 + r0, tiles)
                    elif kind == 'S':
                        self.store_x(xscr, 128 + HALF, tiles)
                    S.barrier(light=True)
            S.barrier()

            self.ck(4)
            with ExitStack() as p2:
                self.phase2(p2, hT1s, OTs, cache, kvp, kvs)
            S.barrier()

            self.ck(5)
            with ExitStack() as p3:
                self.slabs = [self.sb(p3, "slab3_%d" % i, [128, 8192], BF16) for i in range(4)]
                self.xn = [self.sb(p3, "xn3_%d" % i, [128, D], BF16) for i in range(2)]
                self.slab_rr = 0
                self.vpass = 2
                self.xg = self.sb(p3, "xg3", [128, 4, D], F32)
                self.hT = self.sb(p3, "hT3", [128, ND, 512], BF16)
                mixw = max(NF * 512 + 1024, AH * 512 + 8 * D, 2 * (ND * 256 + 3 * D + 2048 + 1024))
                self.mixb = self.sb(p3, "mix3", [128, mixw], BF16)
                self.mix = self.mixb.bitcast(F32)
                self.WsT = self.sb(p3, "WsT3", [128, AH, 128], BF16)
                self.T2 = self.sb(p3, "T23", [128, AH, 128], F32)
                self.h2halo = self.sb(p3, "h2halo", [128, D], F32)
                self.setup_A(1)
                OTg = self.mixb[:, 0:H * 512].rearrange("p (a b) -> p a b", a=H)

                def load_scr(tiles, srows):
                    for t, (c0, rows) in enumerate(tiles):
                        S.dma('sp', self.xg[:rows, t, :], xscr[srows[t]:srows[t] + rows, :], writes=[('xg', t)])

                def store_scr(tiles, srows, only=None):
                    for t, (c0, rows) in enumerate(tiles):
                        if only is None or t in only:
                            S.dma('sp', xscr[srows[t]:srows[t] + rows, :], self.xg[:rows, t, :], reads=[('xg', t)])

                def l1post(tiles, srows):
                    ntok = sum(r for _, r in tiles)
                    self.cur_tiles = tiles
                    load_scr(tiles, srows)
                    for t, (c0, rows) in enumerate(tiles):
                        S.dma('sp', OTg[:, :, c0:c0 + rows], OTs[:, :, srows[t]:srows[t] + rows], writes=['OTg'])
                    self.tm_stage(w['b_w_out'][0], 0, 0, D, H, H, OTg, 'OTg', tiles, self.resid_add(self.xg))
                    S.barrier(light=True)
                    self.ffn(1, self.xg, tiles, ntok)

                def rest(kind, tiles, first_b, pool_out, ydst, yrow):
                    ntok = sum(r for _, r in tiles)
                    self.cur_tiles = tiles
                    if kind == 'S':
                        self.pool_sample(state, pools, g2_d, cs_d, bss_d, bsn_d)
                    else:
                        self.pool_prompt('B', tiles, ntok, g2_d, cs_d, bm_d, first_b, pool_out)
                    self.ffn(2, self.xg, tiles, ntok)
                    so = None
                    if kind == 'S':
                        S.dma('sp', self.xg[:NSMP, 2, :], lnrep_d[1, 0:1, :].to_broadcast([NSMP, D]), writes=[('xg', 2)])
                        S.dma('sp', self.xg[:NSMP, 3, :], lnrep_d[1, 1:2, :].to_broadcast([NSMP, D]), writes=[('xg', 3)])
                        so = avs[1, :, :]
                    self.mixer_A(1, 3, self.xg, tiles, ntok, sample_out=so)
                    self.ffn(3, self.xg, tiles, ntok)
                    self.store_x(ydst, yrow, tiles)
                    S.barrier(light=True)

                rows_b0 = [128, 256, 384, 512]
                rows_b1 = [640, 768, 896, 1024]
                t_hs = [(0, 128), (128, NSMP)]
                t_s = [(0, NSMP)]
                l1post(full, rows_b0)
                store_scr(full, rows_b0)
                S.barrier(light=True)
                l1post(t_hs, [0, 128 + HALF])
                self.pool_prompt('H', t_hs, 132, g2_d, cs_d, bm_d, False, None)
                store_scr(t_hs, [0, 128 + HALF], only=[1])
                S.barrier(light=True)
                self.ck(6)
                load_scr(full, rows_b0)
                rest('B', full, True, None, yB, 0)
                l1post(full, rows_b1)
                rest('B', full, False, poolp, yB, 512)
                load_scr(t_s, [128 + HALF])
                rest('S', t_s, False, None, yS, 0)
            S.barrier()

    def pool_common(self, tiles, ntok, zT, csrep):
        S, cfg = self.S, self.cfg
        NCG, CGW = cfg.NCG, cfg.CGW
        for pg in range(4):
            buf, bk = self.load_slab(self.wslab(self.w['c_w'][0][pg], 0, cfg.NCG, 0, cfg.CGW))
            for t, (c0, rows) in enumerate(tiles):
                for s0 in range(0, CGW, 512):
                    wd = min(512, CGW - s0)
                    ps, pk = self.next_ps()
                    for kc in range(NCG):
                        S.op('pe', lambda e, kc=kc, ps=ps, buf=buf: e.matmul(
                            ps[:rows, 0:wd], lhsT=zT[:, pg * NCG + kc, c0:c0 + rows], rhs=buf[:, kc, s0:s0 + wd],
                            start=(kc == 0), stop=(kc == NCG - 1)),
                            reads=[bk, 'zT'], writes=[pk], inc=(kc == NCG - 1))
                    col = pg * CGW + s0
                    tb = self.mtmp_rr
                    self.mtmp_rr = (tb + 1) % 2
                    tmp = self.ptmp[tb]
                    S.op('dve', lambda e, ps=ps, tmp=tmp: e.tensor_tensor(out=tmp[:rows, 0:wd], in0=ps[:rows, 0:wd],
                                                                        in1=csrep[:rows, col:col + wd], op=ALU.mult),
                         reads=[pk, 'csrep'], writes=[('ptmp', tb)])
                    S.op('dve', lambda e, tmp=tmp, t=t: e.tensor_tensor(
                        out=self.xg[:rows, t, col:col + wd], in0=self.xg[:rows, t, col:col + wd], in1=tmp[:rows, 0:wd],
                        op=ALU.add), reads=[('ptmp', tb), ('xg', t)], writes=[('xg', t)])

    def pool_regions(self):
        cfg = self.cfg
        D, ND = cfg.D, cfg.ND
        zT = self.mixb[:, 0:ND * 512].rearrange("p (a b) -> p a b", a=ND)
        o = ND * 256
        g2rep = self.mix[:, o:o + D]
        csrep = self.mix[:, o + D:o + 2 * D]
        bm = self.mix[:, o + 2 * D:o + 2 * D + 2048].rearrange("p (a b c) -> p a b c", a=4, b=4)
        h2 = self.mix[:, o + 2 * D + 2048:o + 3 * D + 2048]
        pt0 = self.mix[:, o + 3 * D + 2048:o + 3 * D + 2048 + 512]
        pt1 = self.mix[:, o + 3 * D + 2560:o + 3 * D + 2560 + 512]
        self.ptmp = [pt0, pt1]
        return zT, g2rep, csrep, bm, h2

    def h2_rows(self, t, rows, g2rep, dst, dkey):
        S, cfg = self.S, self.cfg
        D = cfg.D
        nst = D // 512
        b = self.nrm_rr
        self.nrm_rr = (b + 1) % 2
        st, mv = self.nst[b], self.nmv[b]
        kk = ('nrm', b)
        for c in range(nst):
            S.op('dve', lambda e, c=c: e.bn_stats(out=st[:rows, c * 6:(c + 1) * 6],
                                                  in_=self.xg[:rows, t, c * 512:(c + 1) * 512]),
                 reads=[('xg', t)], writes=[kk])
        S.op('dve', lambda e: e.bn_aggr(out=mv[:rows, 0:2], in_=st[:rows, 0:nst * 6]), reads=[kk], writes=[kk])
        S.op('dve', lambda e: e.scalar_tensor_tensor(out=mv[:rows, 2:3], in0=mv[:rows, 0:1], scalar=mv[:rows, 0:1],
                                                     in1=mv[:rows, 1:2], op0=ALU.mult, op1=ALU.add),
             reads=[kk], writes=[kk])
        S.op('act', lambda e: e.activation(out=mv[:rows, 3:4], in_=mv[:rows, 2:3], func=AF.Sqrt, bias=self.epsc[:rows, 0:1], scale=1.0), reads=[kk], writes=[kk])
        S.op('dve', lambda e: e.reciprocal(out=mv[:rows, 3:4], in_=mv[:rows, 3:4]), reads=[kk], writes=[kk])
        S.op('dve', lambda e: e.scalar_tensor_tensor(out=dst[:rows, :], in0=self.xg[:rows, t, :], scalar=mv[:rows, 3:4],
                                                     in1=g2rep[:rows, :], op0=ALU.mult, op1=ALU.mult),
             reads=[kk, ('xg', t), 'g2rep'], writes=[dkey])

    def pool_prompt(self, kind, tiles, ntok, g2_d, cs_d, bm_d, first_b, pool_out):
        S, cfg = self.S, self.cfg
        D, ND, NCG = cfg.D, cfg.ND, cfg.NCG
        zT, g2rep, csrep, bm, h2 = self.pool_regions()
        S.dma('sp', g2rep, g2_d[0:1, :].to_broadcast([128, D]), writes=['g2rep'])
        if kind == 'H':
            self.h2_rows(0, 128, g2rep, h2, 'h2')
            S.dma('sp', self.h2halo[0:32, :], h2[96:128, :], reads=['h2'], writes=['h2halo'])
            return
        S.dma('sp', csrep, cs_d[0:1, :].to_broadcast([128, D]), writes=['csrep'])
        S.dma('sp', bm, bm_d[:, :, :, :], writes=['bm'])
        for t, (c0, rows) in enumerate(tiles):
            self.h2_rows(t, rows, g2rep, h2, 'h2')
            kc_, kp_ = (2, 3) if (first_b and t == 0) else (0, 1)
            for d0 in range(0, ND, 4):
                ps, pk = self.next_ps()
                for j in range(4):
                    dc = d0 + j
                    pg = dc // NCG
                    S.op('pe', lambda e, ps=ps, j=j, dc=dc, pg=pg: e.matmul(
                        ps[:, j * 128:(j + 1) * 128], lhsT=self.h2halo[0:32, dc * 128:(dc + 1) * 128],
                        rhs=bm[0:32, pg, kp_, :], start=True, stop=False),
                        reads=['h2halo', 'bm'], writes=[pk], inc=False)
                    S.op('pe', lambda e, ps=ps, j=j, dc=dc, pg=pg: e.matmul(
                        ps[:, j * 128:(j + 1) * 128], lhsT=h2[:, dc * 128:(dc + 1) * 128],
                        rhs=bm[:, pg, kc_, :], start=False, stop=True),
                        reads=['h2', 'bm'], writes=[pk], inc=(j == 3))
                src = ps[:, 0:512].rearrange("p (a b) -> p a b", a=4)
                S.op('act', lambda e, src=src, d0=d0: e.activation(out=zT[:, d0:d0 + 4, c0:c0 + rows], in_=src, func=AF.Identity),
                     reads=[pk], writes=['zT'])
            S.dma('sp', self.h2halo[0:32, :], h2[96:128, :], reads=['h2'], writes=['h2halo'])
            if pool_out is not None and t == len(tiles) - 1:
                S.dma('sp', pool_out[0:15, :], h2[113:128, :], reads=['h2'])
        self.pool_common(tiles, ntok, zT, csrep)
        S.barrier(light=True)

    def pool_sample(self, state, pools, g2_d, cs_d, bss_d, bsn_d):
        S, cfg = self.S, self.cfg
        D, ND, NCG = cfg.D, cfg.ND, cfg.NCG
        zT, g2rep, csrep, bm, h2 = self.pool_regions()
        tiles = [(0, NSMP)]
        S.dma('sp', g2rep, g2_d[0:1, :].to_broadcast([128, D]), writes=['g2rep'])
        S.dma('sp', csrep, cs_d[0:1, :].to_broadcast([128, D]), writes=['csrep'])
        bss = bm[0:15, 0, :, 0:4]
        bsn = bm[0:4, 1, :, 0:4]
        S.dma('sp', bss, bss_d[:, :, :], writes=['bm'])
        S.dma('sp', bsn, bsn_d[:, :, :], writes=['bm'])
        st15 = self.h2halo
        S.dma('sp', st15[0:15, :], state[:, :], writes=['h2halo'])
        S.dma('sp', pools[0:11, :], state[4:15, :])
        self.h2_rows(0, NSMP, g2rep, h2, 'h2')
        S.dma('sp', pools[11:15, :], h2[0:NSMP, :], reads=['h2'])
        for d0 in range(0, ND, 4):
            ps, pk = self.next_ps()
            for j in range(4):
                dc = d0 + j
                pg = dc // NCG
                S.op('pe', lambda e, ps=ps, j=j, dc=dc, pg=pg: e.matmul(
                    ps[:, j * 128:j * 128 + NSMP], lhsT=st15[0:15, dc * 128:(dc + 1) * 128], rhs=bss[:, pg, :],
                    start=True, stop=False), reads=['h2halo', 'bm'], writes=[pk], inc=False)
                S.op('pe', lambda e, ps=ps, j=j, dc=dc, pg=pg: e.matmul(
                    ps[:, j * 128:j * 128 + NSMP], lhsT=h2[0:NSMP, dc * 128:(dc + 1) * 128], rhs=bsn[:, pg, :],
                    start=False, stop=True), reads=['h2', 'bm'], writes=[pk], inc=(j == 3))
            src = ps[:, 0:512].rearrange("p (a b) -> p a b", a=4)[:, :, 0:NSMP]
            S.op('act', lambda e, src=src, d0=d0: e.activation(out=zT[:, d0:d0 + 4, 0:NSMP], in_=src, func=AF.Identity),
                 reads=[pk], writes=['zT'])
        self.pool_common(tiles, NSMP, zT, csrep)
        S.barrier(light=True)

    def phase2(self, p2, hT1s, OTs, cache, kvp, kvs):
        S, cfg, nc = self.S, self.cfg, self.nc
        D, ND, H = cfg.D, cfg.ND, cfg.H
        NTK = SEQ + NSMP
        scale = 128.0 ** -0.5
        self.slabs = [self.sb(p2, "slab2_%d" % i, [128, ND * 256], BF16) for i in range(5)]
        self.slab_rr = 0
        hT1 = self.sb(p2, "hT1", [128, ND, NTK], BF16)
        KT = self.sb(p2, "KT", [128, 2, NTK], BF16)
        QT = self.sb(p2, "QT", [128, 2, NOT], BF16)
        Vb = self.sb(p2, "Vb", [128, 16, 256], BF16)
        Vsb = self.sb(p2, "Vsb", [NSMP, 256], BF16)
        acc = self.sb(p2, "acc", [128, 2, 2, NOT], F32)
        OTb = self.sb(p2, "OTb", [128, 2, NOT], BF16)
        vf = [self.sb(p2, "vf%d" % i, [128, 256], F32) for i in range(2)]
        Pb = [self.sb(p2, "Pb%d" % i, [128, 256], BF16) for i in range(4)]
        kcb = [self.sb(p2, "kcb%d" % i, [128, 256], BF16) for i in range(4)]
        vcb = [self.sb(p2, "vcb%d" % i, [128, 256], BF16) for i in range(4)]
        KcT = [self.sb(p2, "KcT%d" % i, [128, 128], BF16) for i in range(4)]
        Ps = [self.sb(p2, "Ps%d" % i, [128, 8], BF16) for i in range(4)]
        NROT = {'kn': 2, 'vf': 2, 'P': 4, 'kc': 4, 'KcT': 4, 'Ps': 4}
        kraw = self.sb(p2, "kraw", [128, 17, 256], F32)
        ksq = self.sb(p2, "ksq", [128, 17, 256], F32)
        kbf = self.sb(p2, "kbf", [128, 17, 256], BF16)
        kss = self.sb(p2, "kss", [128, 34], F32)
        rr = {'kn': 0, 'vf': 0, 'P': 0, 'kc': 0, 'KcT': 0, 'Ps': 0}

        def nxt(name):
            i = rr[name]
            rr[name] = (i + 1) % NROT[name]
            return i

        for c0 in range(0, NTK, 1024):
            n = min(1024, NTK - c0)
            S.dma('sp', hT1[:, :, c0:c0 + n], hT1s[:, :, c0:c0 + n], writes=['hT1'])
        Wqkv = self.w['b_w_qkv'][0]
        self.ck(10)

        def proj(buf, bk, cols, rows, step=1):
            ps, pk = self.next_ps()
            for kc in range(ND):
                lhsT = hT1[:, kc, cols:cols + (rows - 1) * step + 1:step] if step > 1 else hT1[:, kc, cols:cols + rows]
                S.op('pe', lambda e, kc=kc, lhsT=lhsT, ps=ps: e.matmul(ps[:rows, 0:256], lhsT=lhsT, rhs=buf[:, kc, :],
                                                                      start=(kc == 0), stop=(kc == ND - 1)),
                     reads=[bk, 'hT1'], writes=[pk], inc=(kc == ND - 1))
            return ps, pk

        def qk_norm(ps, pk, rows, gi):
            i = nxt('kn')
            f, b_, st = knf[i], knb[i], qst[i]
            S.op('dve', lambda e: e.tensor_copy(out=f[:rows, :], in_=ps[:rows, 0:256]),
                 reads=[pk], writes=[('knf', i)])
            for h in range(2):
                S.op('dve', lambda e, h=h: e.bn_stats(out=st[:rows, 0:6], in_=f[:rows, h * 128:(h + 1) * 128]),
                     reads=[('knf', i)], writes=[('qst', i)])
                S.op('dve', lambda e: e.bn_aggr(out=st[:rows, 6:8], in_=st[:rows, 0:6]), reads=[('qst', i)],
                     writes=[('qst', i)])
                S.op('dve', lambda e: e.scalar_tensor_tensor(out=st[:rows, 0:1], in0=st[:rows, 6:7], scalar=st[:rows, 6:7],
                                                             in1=st[:rows, 7:8], op0=ALU.mult, op1=ALU.add),
                     reads=[('qst', i)], writes=[('qst', i)])
                S.op('act', lambda e: e.activation(out=st[:rows, 1:2], in_=st[:rows, 0:1], func=AF.Sqrt, bias=self.epsc[:rows, 0:1], scale=1.0), reads=[('qst', i)], writes=[('qst', i)])
                S.op('dve', lambda e: e.reciprocal(out=st[:rows, 1:2], in_=st[:rows, 1:2]), reads=[('qst', i)], writes=[('qst', i)])
                S.op('dve', lambda e, h=h: e.scalar_tensor_tensor(
                    out=f[:rows, h * 128:(h + 1) * 128], in0=f[:rows, h * 128:(h + 1) * 128], scalar=st[:rows, 1:2],
                    in1=self.qkg[:rows, gi, :], op0=ALU.mult, op1=ALU.mult),
                    reads=[('qst', i), ('knf', i)], writes=[('knf', i)])
            S.op('act', lambda e: e.activation(out=b_[:rows, :], in_=f[:rows, :], func=AF.Identity),
                 reads=[('knf', i)], writes=[('knb', i)])
            return i

        def to_T(i, rows, dst, dkey, col):
            pt, pk = self.next_pt()
            for h in range(2):
                S.op('pe', lambda e, h=h: e.transpose(out=pt[:, h * 128:h * 128 + rows],
                                                      in_=knb[i][:rows, h * 128:(h + 1) * 128],
                                                      identity=self.ident[:rows, :rows]),
                     reads=[('knb', i)], writes=[pk], inc=(h == 1))
            src = pt[:, 0:256].rearrange("p (a b) -> p a b", a=2)[:, :, 0:rows]
            S.op('act', lambda e: e.activation(out=dst[:, :, col:col + rows], in_=src, func=AF.Identity),
                 reads=[pk], writes=[dkey])

        S.op('dve', lambda e: e.memset(kraw[:, :, :], 0.0), writes=[('kraw', 0), ('kraw', 1)])

        def qk_pc(buf, bkey, tl, s0, R, gi, out_fn):
            n = len(tl)
            kr, kq_, kb_, ks_ = ('kraw', R), ('ksq', R), ('kbf', R), ('kss', R)
            for j, (cols, rows) in enumerate(tl):
                ps, pk = proj(buf, bkey, cols, rows)
                if j % 2 == 0:
                    S.op('act', lambda e, j=j, ps=ps, rows=rows: e.activation(out=kraw[:rows, s0 + j, :], in_=ps[:rows, 0:256],
                                                                            func=AF.Identity), reads=[pk], writes=[kr])
                else:
                    S.op('dve', lambda e, j=j, ps=ps, rows=rows: e.tensor_copy(out=kraw[:rows, s0 + j, :], in_=ps[:rows, 0:256]),
                         reads=[pk], writes=[kr])
            k3 = kraw[:, s0:s0 + n, :].rearrange("p t (h e) -> p (t h) e", h=2)
            q3 = ksq[:, s0:s0 + n, :].rearrange("p t (h e) -> p (t h) e", h=2)
            ss = kss[:, 2 * s0:2 * s0 + 2 * n]
            S.op('dve', lambda e: e.tensor_tensor(out=ksq[:, s0:s0 + n, :], in0=kraw[:, s0:s0 + n, :],
                                                  in1=kraw[:, s0:s0 + n, :], op=ALU.mult), reads=[kr], writes=[kq_])
            S.op('dve', lambda e: e.tensor_reduce(out=ss, in_=q3, axis=mybir.AxisListType.X, op=ALU.add),
                 reads=[kq_], writes=[ks_])
            S.op('act', lambda e: e.activation(out=ss, in_=ss, func=AF.Sqrt, bias=self.epsc[:, 0:1], scale=1.0 / 128.0),
                 reads=[ks_], writes=[ks_])
            S.op('dve', lambda e: e.reciprocal(out=ss, in_=ss), reads=[ks_], writes=[ks_])
            S.op('dve', lambda e: e.tensor_tensor(out=k3, in0=k3, in1=ss.unsqueeze(2).to_broadcast([128, 2 * n, 128]),
                                                  op=ALU.mult), reads=[ks_, kr], writes=[kr])
            S.op('dve', lambda e: e.tensor_tensor(out=k3, in0=k3,
                                                  in1=self.qkg[:, gi, :].unsqueeze(1).to_broadcast([128, 2 * n, 128]),
                                                  op=ALU.mult), reads=[kr], writes=[kr])
            S.op('act', lambda e: e.activation(out=kbf[:, s0:s0 + n, :], in_=kraw[:, s0:s0 + n, :], func=AF.Identity),
                 reads=[kr], writes=[kb_])
            for j, (cols, rows) in enumerate(tl):
                out_fn(s0 + j, cols, rows, kr)

        def qk_T(tl, s0, R, dstT, dkey, dcol):
            n = len(tl)
            kb_ = ('kbf', R)
            for j0 in range(0, n, 4):
                nj = min(4, n - j0)
                pt, ptk = self.next_pt()
                full = all(tl[j0 + jj][1] == 128 for jj in range(nj))
                for jj in range(nj):
                    rows = tl[j0 + jj][1]
                    for h in range(2):
                        S.op('pe', lambda e, jj=jj, h=h, rows=rows: e.transpose(
                            out=pt[:, (jj * 2 + h) * 128:(jj * 2 + h) * 128 + rows],
                            in_=kbf[:rows, s0 + j0 + jj, h * 128:(h + 1) * 128], identity=self.ident[:rows, :rows]),
                            reads=[kb_], writes=[ptk], inc=(jj == nj - 1 and h == 1))
                if full and all(dcol(tl[j0 + jj][0]) == dcol(tl[j0][0]) + 128 * jj for jj in range(nj)):
                    c0 = dcol(tl[j0][0])
                    for h in range(2):
                        src = pt[:, 0:nj * 256].rearrange("p (t h e) -> p t h e", t=nj, h=2)[:, :, h, :]
                        dst = dstT[:, h, c0:c0 + nj * 128].rearrange("p (t e) -> p t e", t=nj)
                        if h == 0:
                            S.op('act', lambda e, src=src, dst=dst: e.activation(out=dst, in_=src, func=AF.Identity),
                                 reads=[ptk], writes=[dkey, ptk])
                        else:
                            S.op('dve', lambda e, src=src, dst=dst: e.tensor_copy(out=dst, in_=src),
                                 reads=[ptk], writes=[dkey, ptk])
                else:
                    for jj in range(nj):
                        cols, rows = tl[j0 + jj]
                        src = pt[:, jj * 256:(jj + 1) * 256].rearrange("p (a b) -> p a b", a=2)[:, :, 0:rows]
                        S.op('act', lambda e, src=src, cols=cols, rows=rows: e.activation(
                            out=dstT[:, :, dcol(cols):dcol(cols) + rows], in_=src, func=AF.Identity),
                            reads=[ptk], writes=[dkey, ptk])

        for hb in range(H // 2):
            h0 = hb * 2
            S.op('dve', lambda e: e.memset(acc[:, :, :, :], 0.0), writes=['acc'])
            for g in range(3):
                dil = DILS[g]
                base = (g * 3) * D + h0 * 128
                bq, kq = self.load_slab(self.wslab(Wqkv, 0, ND, base, 256))
                bk_, kk_ = self.load_slab(self.wslab(Wqkv, 0, ND, base + D, 256))
                bv, kv_ = self.load_slab(self.wslab(Wqkv, 0, ND, base + 2 * D, 256))
                keep0 = (SEQ - (128, 512, 1024)[g])
                def k_out(slot, cols, rows, kr, g=g, h0=h0, keep0=keep0):
                    src = kraw[:rows, slot, :].rearrange("p (a b) -> p a b", a=2)
                    if rows == NSMP:
                        S.dma('sp', kvs[g][:, 0, h0:h0 + 2, :], src, reads=[kr])
                    elif cols >= keep0:
                        S.dma('sp', kvp[g][cols - keep0:cols - keep0 + 128, 0, h0:h0 + 2, :], src, reads=[kr])
                no_out = lambda slot, c, r, kr: None
                tka = [(ti * 128, 128) for ti in range(0, 9)]
                tkb = [(ti * 128, 128) for ti in range(9, 16)] + [(SEQ, NSMP)]
                tqa = [(ti * 128, 128) for ti in range(7, 12)]
                tqb = [(ti * 128, 128) for ti in range(12, 16)] + [(SEQ, NSMP)]
                kcol = lambda c: c
                qcol = lambda c: (NQ if c == SEQ else c - NQ0)
                qk_pc(bk_, kk_, tka, 0, 0, 3 + g, k_out)
                qk_pc(bk_, kk_, tkb, 9, 1, 3 + g, k_out)
                qk_T(tka, 0, 0, KT, 'KT', kcol)
                qk_pc(bq, kq, tqa, 0, 0, g, no_out)
                qk_T(tkb, 9, 1, KT, 'KT', kcol)
                qk_pc(bq, kq, tqb, 9, 1, g, no_out)
                if hb == 0 and g == 0:
                    self.ck(11)
                L = SEQ // dil
                nb = L // 128
                for p in range(16):
                    r, c = divmod(p, nb)
                    tok0 = r + dil * 128 * c
                    ps, pk = proj(bv, kv_, tok0, 128, step=dil)
                    S.op('act', lambda e, p=p, ps=ps: e.activation(out=Vb[:, p, :], in_=ps[:, 0:256], func=AF.Identity),
                         reads=[pk], writes=[('Vb', p)])
                    i0 = 0
                    while tok0 + dil * i0 < keep0:
                        i0 += 1
                        if i0 >= 128:
                            break
                    if i0 < 128 and i0 in (0, 64):
                        j = nxt('vf')
                        S.op('dve', lambda e, j=j, ps=ps: e.tensor_copy(out=vf[j][:, :], in_=ps[:, 0:256]),
                             reads=[pk], writes=[('vf', j), pk])
                        o0 = tok0 + dil * i0 - keep0
                        n = 128 - i0
                        dst = kvp[g][o0:o0 + dil * (n - 1) + 1:dil, 1, h0:h0 + 2, :] if dil > 1 else \
                            kvp[g][o0:o0 + n, 1, h0:h0 + 2, :]
                        S.dma('sp', dst, vf[j][i0:128, :].rearrange("p (a b) -> p a b", a=2), reads=[('vf', j)])
                ps, pk = proj(bv, kv_, SEQ, NSMP)
                S.op('act', lambda e, ps=ps: e.activation(out=Vsb[:, :], in_=ps[:NSMP, 0:256], func=AF.Identity),
                     reads=[pk], writes=['Vsb'])
                j = nxt('vf')
                S.op('dve', lambda e, j=j, ps=ps: e.tensor_copy(out=vf[j][:NSMP, :], in_=ps[:NSMP, 0:256]),
                     reads=[pk], writes=[('vf', j), pk])
                S.dma('sp', kvs[g][:, 1, h0:h0 + 2, :], vf[j][:NSMP, :].rearrange("p (a b) -> p a b", a=2),
                      reads=[('vf', j)])

                qk_T(tqa, 0, 0, QT, 'QT', qcol)
                qk_T(tqb, 9, 1, QT, 'QT', qcol)
                if hb == 0 and g == 0:
                    self.ck(13)
                blocks = []
                if g == 0:
                    for c in range(7, 16):
                        blocks.append((128 * (c - 7), 128, 0,
                                       [(128 * (c - 1), c - 1, 0 if c - 1 <= 7 else 2),
                                        (128 * c, c, 1 if c <= 7 else 3)]))
                elif g == 1:
                    for r in range(4):
                        for c in range(1, 4):
                            i0 = 96 if c == 1 else 0
                            q0 = r + 4 * (128 * c + i0) - NQ0
                            blocks.append((q0, 128 - i0, i0,
                                           [(r + 512 * (c - 1), r * 4 + c - 1, 0 if c - 1 <= 1 else 2),
                                            (r + 512 * c, r * 4 + c, 1 if c <= 1 else 3)]))
                else:
                    for r in range(16):
                        blocks.append((r, 72, 56, [(r, r, 4)]))
                def p_front(h, blk):
                    (q0, nq, i0, kbl) = blk
                    qs = QT[:, h, q0:q0 + dil * (nq - 1) + 1:dil] if dil > 1 else QT[:, h, q0:q0 + nq]
                    ps, pk = self.next_ps()
                    for bi, (k0, vt, mi) in enumerate(kbl):
                        ks = KT[:, h, k0:k0 + dil * 127 + 1:dil] if dil > 1 else KT[:, h, k0:k0 + 128]
                        S.op('pe', lambda e, ps=ps, bi=bi, ks=ks, qs=qs, nq=nq: e.matmul(
                            ps[:, bi * 128:bi * 128 + nq], lhsT=ks, rhs=qs, start=True, stop=True),
                            reads=['KT', 'QT'], writes=[pk], inc=(bi == len(kbl) - 1))
                    pi = nxt('P')
                    P = Pb[pi]
                    nb_ = len(kbl)
                    src = ps[:, 0:nb_ * 128].rearrange("p (a b) -> p a b", a=nb_)[:, :, 0:nq]
                    dstP = P[:, 0:nb_ * 128].rearrange("p (a b) -> p a b", a=nb_)[:, :, 0:nq]
                    S.op('act', lambda e, src=src, dstP=dstP: e.activation(out=dstP, in_=src, func=AF.Exp, scale=scale),
                         reads=[pk], writes=[('P', pi)])
                    for bi, (k0, vt, mi) in enumerate(kbl):
                        S.op('dve', lambda e, bi=bi, mi=mi, P=P, nq=nq, i0=i0: e.tensor_tensor(
                            out=P[:, bi * 128:bi * 128 + nq], in0=P[:, bi * 128:bi * 128 + nq],
                            in1=self.amask[:, mi, i0:i0 + nq], op=ALU.mult),
                            reads=[('P', pi)], writes=[('P', pi)])
                    return (h, blk, P, pi)

                def p_back(st):
                    (h, (q0, nq, i0, kbl), P, pi) = st
                    nb_ = len(kbl)
                    po, pok = self.next_ps()
                    for bi, (k0, vt, mi) in enumerate(kbl):
                        S.op('pe', lambda e, po=po, bi=bi, vt=vt, P=P, nq=nq, nb_=nb_, h=h: e.matmul(
                            po[:, 0:nq], lhsT=Vb[:, vt, h * 128:(h + 1) * 128], rhs=P[:, bi * 128:bi * 128 + nq],
                            start=(bi == 0), stop=(bi == nb_ - 1)),
                            reads=[('P', pi), ('Vb', vt)], writes=[pok], inc=False)
                    for bi, (k0, vt, mi) in enumerate(kbl):
                        S.op('pe', lambda e, po=po, bi=bi, P=P, nq=nq, nb_=nb_: e.matmul(
                            po[:, 256:256 + nq], lhsT=self.ones_bf[:, :], rhs=P[:, bi * 128:bi * 128 + nq],
                            start=(bi == 0), stop=(bi == nb_ - 1)),
                            reads=[('P', pi)], writes=[pok], inc=(bi == nb_ - 1))
                    src2 = po[:, 0:512].rearrange("p (a b) -> p a b", a=2)[:, :, 0:nq]
                    a2 = acc[:, :, h, q0:q0 + dil * (nq - 1) + 1:dil] if dil > 1 else acc[:, :, h, q0:q0 + nq]
                    S.op('dve', lambda e, src2=src2, a2=a2: e.tensor_tensor(out=a2, in0=a2, in1=src2, op=ALU.add),
                         reads=[pok, 'acc'], writes=['acc'])

                pend = []
                for h in range(2):
                    for blk in blocks:
                        pend.append(p_front(h, blk))
                        if len(pend) > 2:
                            p_back(pend.pop(0))
                while pend:
                    p_back(pend.pop(0))

                if hb == 0 and g == 0:
                    self.ck(14)
                nblk = 1 if g == 0 else 4

                def s_A(rho, h, ci):
                    pt, ptk = self.next_pt()
                    S.op('pe', lambda e, pt=pt, ci=ci, h=h: e.transpose(out=pt[:, 0:128], in_=kcb[ci][:, h * 128:(h + 1) * 128],
                                                                      identity=self.ident[:, :]),
                         reads=[('kcb', ci)], writes=[ptk])
                    ki = nxt('KcT')
                    S.op('act', lambda e, pt=pt, ki=ki: e.activation(out=KcT[ki][:, :], in_=pt[:, 0:128], func=AF.Identity),
                         reads=[ptk], writes=[('KcT', ki)])
                    return (rho, h, ci, ki)

                def s_B(st):
                    (rho, h, ci, ki) = st
                    ps, pk = self.next_ps()
                    S.op('pe', lambda e, ps=ps, ki=ki, h=h: e.matmul(ps[:, 0:NSMP], lhsT=KcT[ki][:, :],
                                                                   rhs=QT[:, h, NQ:NQ + NSMP], start=True, stop=True),
                         reads=[('KcT', ki), 'QT'], writes=[pk])
                    if rho == 0:
                        S.op('pe', lambda e, ps=ps, h=h: e.matmul(ps[0:NSMP, 8:8 + NSMP], lhsT=KT[:, h, SEQ:SEQ + NSMP],
                                                                rhs=QT[:, h, NQ:NQ + NSMP], start=True, stop=True),
                             reads=['KT', 'QT'], writes=[pk])
                    si = nxt('Ps')
                    Pq = Ps[si]
                    S.op('act', lambda e, ps=ps, Pq=Pq: e.activation(out=Pq[:, 0:NSMP], in_=ps[:, 0:NSMP], func=AF.Exp,
                                                                   scale=scale), reads=[pk], writes=[('Ps', si)])
                    mi = 0 if g == 0 else 1 + rho
                    S.op('dve', lambda e, Pq=Pq, mi=mi: e.tensor_tensor(out=Pq[:, 0:NSMP], in0=Pq[:, 0:NSMP],
                                                                       in1=self.smask[:, mi, :], op=ALU.mult),
                         reads=[('Ps', si)], writes=[('Ps', si)])
                    if rho == 0:
                        S.op('act', lambda e, ps=ps, Pq=Pq: e.activation(out=Pq[0:NSMP, 4:8], in_=ps[0:NSMP, 8:8 + NSMP],
                                                                       func=AF.Exp, scale=scale),
                             reads=[pk, ('Ps', si)], writes=[('Ps', si)])
                        mn = 5 if g == 0 else 6
                        S.op('dve', lambda e, Pq=Pq, mn=mn: e.tensor_tensor(out=Pq[0:NSMP, 4:8], in0=Pq[0:NSMP, 4:8],
                                                                           in1=self.smask[0:NSMP, mn, :], op=ALU.mult),
                             reads=[('Ps', si)], writes=[('Ps', si)])
                    return (rho, h, ci, si)

                def s_C(st):
                    (rho, h, ci, si) = st
                    Pq = Ps[si]
                    po, pok = self.next_ps()
                    S.op('pe', lambda e, po=po, ci=ci, h=h, Pq=Pq: e.matmul(
                        po[:, 0:NSMP], lhsT=vcb[ci][:, h * 128:(h + 1) * 128], rhs=Pq[:, 0:NSMP],
                        start=True, stop=(rho != 0)), reads=[('vcb', ci), ('Ps', si)], writes=[pok], inc=False)
                    if rho == 0:
                        S.op('pe', lambda e, po=po, h=h, Pq=Pq: e.matmul(
                            po[:, 0:NSMP], lhsT=Vsb[0:NSMP, h * 128:(h + 1) * 128], rhs=Pq[0:NSMP, 4:8],
                            start=False, stop=True), reads=['Vsb', ('Ps', si)], writes=[pok], inc=False)
                    S.op('pe', lambda e, po=po, Pq=Pq: e.matmul(
                        po[:, 256:256 + NSMP], lhsT=self.ones_bf[:, :], rhs=Pq[:, 0:NSMP],
                        start=True, stop=(rho != 0)), reads=[('Ps', si)], writes=[pok], inc=(rho != 0))
                    if rho == 0:
                        S.op('pe', lambda e, po=po, Pq=Pq: e.matmul(
                            po[:, 256:256 + NSMP], lhsT=self.ones_bf[0:NSMP, :], rhs=Pq[0:NSMP, 4:8],
                            start=False, stop=True), reads=[('Ps', si)], writes=[pok])
                    src2 = po[:, 0:512].rearrange("p (a b) -> p a b", a=2)[:, :, 0:NSMP]
                    a2 = acc[:, :, h, NQ:NQ + NSMP]
                    S.op('dve', lambda e, src2=src2, a2=a2: e.tensor_tensor(out=a2, in0=a2, in1=src2, op=ALU.add),
                         reads=[pok, 'acc'], writes=['acc'])

                sunits = []
                for rho in range(nblk):
                    ci = nxt('kc')
                    rsl = slice(0, 128) if g == 0 else slice(rho, rho + dil * 127 + 1, dil)
                    S.dma('pool', kcb[ci][:, :].rearrange("p (a b) -> p a b", a=2), cache[g][rsl, 0, h0:h0 + 2, :],
                          writes=[('kcb', ci)])
                    S.dma('pool', vcb[ci][:, :].rearrange("p (a b) -> p a b", a=2), cache[g][rsl, 1, h0:h0 + 2, :],
                          writes=[('vcb', ci)])
                    for h in range(2):
                        sunits.append((rho, h, ci))
                stA, stB = [], []
                n_u = len(sunits)
                for i in range(n_u + 2):
                    if i < n_u:
                        stA.append(s_A(*sunits[i]))
                    if 0 <= i - 1 < n_u:
                        stB.append(s_B(stA[i - 1]))
                    if 0 <= i - 2 < n_u:
                        s_C(stB[i - 2])
                if hb == 0:
                    self.ck(15 + g)
            S.op('dve', lambda e: e.tensor_scalar(out=acc[:, 1, :, :], in0=acc[:, 1, :, :], scalar1=1e-30, scalar2=None,
                                                  op0=ALU.max), reads=['acc'], writes=['acc'])
            S.op('dve', lambda e: e.reciprocal(out=acc[:, 1, :, :], in_=acc[:, 1, :, :]), reads=['acc'], writes=['acc'])
            S.op('dve', lambda e: e.tensor_tensor(out=OTb[:, :, :], in0=acc[:, 0, :, :], in1=acc[:, 1, :, :], op=ALU.mult),
                 reads=['acc'], writes=['OTb'])
            S.dma('sp', OTs[:, h0:h0 + 2, :], OTb[:, :, :], reads=['OTb'])
            if hb == 0:
                self.ck(18)


def host_consts(cfg, half):
    flag = 1.0 if half == 1 else 0.0
    k = np.arange(128)[:, None]
    q = np.arange(128)[None, :]
    U = (k >= q).astype(np.float32)
    Lm = (k <= q).astype(np.float32)
    L2 = Lm * np.where(k < 64, flag, 1.0)
    amask = np.stack([U * flag, Lm * flag, U, Lm, L2], axis=1).astype(np.float32)
    smask = np.zeros((128, 7, 4), np.float32)
    t = np.arange(4)[None, :]
    smask[:, 0, :] = (k >= t)
    for rho in range(4):
        smask[:, 1 + rho, rho] = 1.0
    smask[:4, 5, :] = (np.arange(4)[:, None] <= t)
    smask[:4, 6, :] = np.eye(4)
    bm = np.zeros((128, 4, 4, 128), np.float32)
    j = np.arange(128)[:, None]
    tt = np.arange(128)[None, :]
    for pg, wd in enumerate(POOL_W):
        cur = ((j <= tt) & (j >= tt - wd + 1)).astype(np.float32) / wd - (j == tt)
        prev = ((j - 32 >= tt - wd + 1) & (j < 32)).astype(np.float32) / wd
        bm[:, pg, 0, :] = cur
        bm[:, pg, 1, :] = prev
        if half == 1:
            bm[:, pg, 2, :] = cur
            bm[:, pg, 3, :] = prev
        else:
            cnt = np.minimum(wd, tt + 1).astype(np.float32)
            bm[:, pg, 2, :] = ((j <= tt) & (j >= tt - wd + 1)).astype(np.float32) / cnt - (j == tt)
            bm[:, pg, 3, :] = 0.0
    bss = np.zeros((15, 4, 4), np.float32)
    bsn = np.zeros((4, 4, 4), np.float32)
    for pg, wd in enumerate(POOL_W):
        for t_ in range(4):
            pos = 15 + t_
            for jj in range(pos - wd + 1, pos + 1):
                if jj < 15:
                    bss[jj, pg, t_] += 1.0 / wd
                else:
                    bsn[jj - 15, pg, t_] += 1.0 / wd
            bsn[t_, pg, t_] -= 1.0
    return amask, smask, bm, bss, bsn


_NC_CACHE = {}


def make_in_maps(cfg, inp, n_cores):
    D, ND, AH = cfg.D, cfg.ND, cfg.AH
    f = lambda a: np.ascontiguousarray(np.asarray(a, dtype=np.float32))
    gcols = np.concatenate([f(inp['norm_mix_g']), f(inp['norm_ffn_g'])], axis=0)
    gcols = np.ascontiguousarray(gcols.reshape(8, ND, 128).transpose(2, 0, 1))
    ln = np.stack([f(inp['a_ln_g']), f(inp['a_ln_b'])], axis=1)
    lncols = np.ascontiguousarray(ln.reshape(2, 2, AH, 128).transpose(3, 0, 1, 2))
    qkg = np.concatenate([f(inp['b_q_g'])[0].reshape(-1), f(inp['b_k_g'])[0].reshape(-1)])[None, :]
    qkg = np.ascontiguousarray(qkg)
    shared = {
        'gcols': gcols, 'lncols': lncols, 'lnrep': np.ascontiguousarray(ln), 'qkg': qkg,
        'g2row': f(inp['norm_mix_g'])[2:3], 'csrow': f(inp['c_scale'])[0:1],
        'ident': np.eye(128, dtype=np.float32),
    }
    for k_ in ('a_w_in', 'a_w_s', 'a_b_s', 'a_w_out', 'b_w_qkv', 'b_w_out', 'c_w', 'ffn_w1', 'ffn_w3', 'ffn_w2'):
        shared[k_] = f(inp[k_])
    xp, xs = f(inp['x_prompt']), f(inp['x_sample'])
    caches = [f(inp[k_]) for k_ in ('cache_b_kv0', 'cache_b_kv1', 'cache_b_kv2')]
    st = f(inp['state_c_pool'])
    consts = {h: host_consts(cfg, h) for h in (0, 1)}
    maps = []
    for c in range(n_cores):
        s, h = divmod(c, 2)
        m = dict(shared)
        m['xB'] = np.ascontiguousarray(xp[s, h * HALF:(h + 1) * HALF])
        m['xA'] = np.ascontiguousarray(xp[s, 0:HALF]) if h == 1 else np.zeros((HALF, D), np.float32)
        m['xS'] = np.ascontiguousarray(xs[c])
        for g in range(3):
            m['cache%d' % g] = np.ascontiguousarray(caches[g][0, c])
        m['state'] = np.ascontiguousarray(st[0, c])
        am, sm, bm, bss, bsn = consts[h]
        m['amask'], m['smask'], m['poolp'], m['poolss'], m['poolsn'] = am, sm, bm, bss, bsn
        maps.append(m)
    return maps


def assemble(cfg, res, n_cores):
    D, H = cfg.D, cfg.H
    nseq = n_cores // 2
    y_prompt = np.stack([np.concatenate([res[2 * s]['yB'], res[2 * s + 1]['yB']], axis=0) for s in range(nseq)])
    y_sample = np.stack([res[c]['yS'] for c in range(n_cores)])
    av = np.stack([res[c]['avs'] for c in range(n_cores)], axis=1)
    kv0p = np.stack([res[2 * s + 1]['kvp0'] for s in range(nseq)])[None]
    kv1p = np.stack([res[2 * s + 1]['kvp1'] for s in range(nseq)])[None]
    kv2p = np.stack([np.concatenate([res[2 * s]['kvp2'], res[2 * s + 1]['kvp2']], axis=0) for s in range(nseq)])[None]
    kvs = [np.stack([res[c]['kvs%d' % g] for c in range(n_cores)])[None] for g in range(3)]
    pp = np.stack([res[2 * s + 1]['poolpo'] for s in range(nseq)])[None]
    psm = np.stack([res[c]['poolso'] for c in range(n_cores)])[None]
    outs = (y_prompt, y_sample, av, kv0p, kv1p, kv2p, kvs[0], kvs[1], kvs[2], pp, psm)
    return tuple(np.ascontiguousarray(o.astype(np.float32)) for o in outs)


def kernel(**inputs):
    cfg = Cfg(2048)
    n_cores = 8
    nc = Builder(cfg).build()
    maps = make_in_maps(cfg, inputs, n_cores)
    res = run_bass_kernel_spmd(nc, maps, core_ids=list(range(n_cores)))
    return assemble(cfg, res.results, n_cores)
```

```python
import numpy as np
from contextlib import ExitStack
import concourse.bass as bass
import concourse.mybir as mybir
from concourse.bass_utils import run_bass_kernel_spmd

F32 = mybir.dt.float32
BF16 = mybir.dt.bfloat16
AF = mybir.ActivationFunctionType
ALU = mybir.AluOpType
EPS = 1e-6
SEQ = 2048
HALF = 1024
NSMP = 4
POOL_W = (2, 4, 8, 16)
DILS = (1, 4, 16)
NQ0 = 896
NQ = SEQ - NQ0
NOT = NQ + NSMP


class Cfg:
    def __init__(self, D=2048):
        self.D = D
        self.ND = D // 128
        self.H = D // 128
        self.AH = D // 128
        self.DFF = ((8 * D + 3 * 256 - 1) // (3 * 256)) * 256
        self.NF = self.DFF // 128
        self.CGW = D // 4
        self.NCG = self.ND // 4


class _Rec:
    def __init__(self):
        self.call = None

    def __getattr__(self, name):
        def f(*a, **k):
            self.call = (name, a, k)
            return self
        return f


def _capture(fn):
    r = _Rec()
    fn(r)
    name, a, k = r.call
    return lambda eng: getattr(eng, name)(*a, **k)


class Sched:
    NDMA = 12

    def __init__(self, nc, es):
        self.nc = nc
        self.eng = {'pe': None, 'act': None, 'dve': None, 'pool': None, 'sp': None}
        self.prog = {e: [] for e in self.eng}
        self.sem = {e: es.enter_context(nc.semaphore("s_" + e)) for e in self.eng}
        self.cnt = {e: 0 for e in self.eng}
        self.known = {e: {} for e in self.eng}
        self.lastw = {}
        self.readers = {}
        self.dsem = {q: [es.enter_context(nc.semaphore("d_%s%d" % (q, i))) for i in range(self.NDMA)]
                     for q in ('sp', 'pool')}
        self.dtot = {q: [0] * self.NDMA for q in ('sp', 'pool')}
        self.drr = {'sp': 0, 'pool': 0}

    def _semh(self, sk):
        return self.sem[sk[1]] if sk[0] == 'e' else self.dsem[sk[1]][sk[2]]

    def _deps(self, e, reads, writes):
        need = {}
        def add(d):
            if d is None:
                return
            sk, v = d
            if need.get(sk, 0) < v:
                need[sk] = v
        for k in reads:
            add(self.lastw.get(k))
        for k in writes:
            add(self.lastw.get(k))
            for sk, v in self.readers.get(k, {}).items():
                add((sk, v))
        for sk, v in need.items():
            if sk == ('e', 'pe') and e == 'pe':
                continue
            if self.known[e].get(sk, 0) >= v:
                continue
            self._wait(e, self._semh(sk), v)
            self.known[e][sk] = v

    def _wait(self, e, sem, v):
        self.prog[e].append(lambda eng, sem=sem, v=v: eng.wait_ge(sem, v))

    def emit_all(self):
        nc = self.nc
        prog = self.prog
        with nc.Block() as block:
            @block.tensor
            def _(eng):
                for f in prog['pe']:
                    f(eng)

            @block.scalar
            def _(eng):
                for f in prog['act']:
                    f(eng)

            @block.vector
            def _(eng):
                for f in prog['dve']:
                    f(eng)

            @block.gpsimd
            def _(eng):
                for f in prog['pool']:
                    f(eng)

            @block.sync
            def _(eng):
                for f in prog['sp']:
                    f(eng)

    def _record(self, me, reads, writes):
        for k in writes:
            self.lastw[k] = me
            self.readers[k] = {}
        for k in reads:
            r = self.readers.setdefault(k, {})
            if r.get(me[0], 0) < me[1]:
                r[me[0]] = me[1]

    stopped = False

    def op(self, e, fn, reads=(), writes=(), inc=True):
        if self.stopped:
            return
        self._deps(e, reads, writes)
        sem = self.sem[e]
        fn = _capture(fn)
        if inc:
            self.cnt[e] += 1
            self.prog[e].append(lambda eng, fn=fn, sem=sem: fn(eng).then_inc(sem, 1))
            me = (('e', e), self.cnt[e])
        else:
            self.prog[e].append(lambda eng, fn=fn: fn(eng))
            me = (('e', e), self.cnt[e] + 1)
        self._record(me, reads, writes)

    def dma(self, q, out, in_, reads=(), writes=()):
        if self.stopped:
            return
        self._deps(q, reads, writes)
        i = self.drr[q]
        self.drr[q] = (i + 1) % self.NDMA
        sk = ('d', q, i)
        if self.dtot[q][i] > 0 and self.known[q].get(sk, 0) < self.dtot[q][i]:
            self._wait(q, self.dsem[q][i], self.dtot[q][i])
            self.known[q][sk] = self.dtot[q][i]
        self.prog[q].append(lambda eng, out=out, in_=in_, sem=self.dsem[q][i]: eng.dma_start(out=out, in_=in_).then_inc(sem, 16))
        self.dtot[q][i] += 16
        self._record((sk, self.dtot[q][i]), reads, writes)

    def barrier(self, light=False):
        if self.stopped:
            return
        for e in self.eng:
            if light and e == 'pool':
                continue
            for e2 in self.eng:
                if e2 == e or (light and e2 == 'pool'):
                    continue
                v = self.cnt[e2]
                if v > 0 and self.known[e].get(('e', e2), 0) < v:
                    self._wait(e, self.sem[e2], v)
                    self.known[e][('e', e2)] = v
            if self.cnt[e] > 0 and e != 'pe' and self.known[e].get(('e', e), 0) < self.cnt[e]:
                self._wait(e, self.sem[e], self.cnt[e])
                self.known[e][('e', e)] = self.cnt[e]
            for q in ('sp', 'pool'):
                if light and q == 'pool':
                    continue
                for i in range(self.NDMA):
                    v = self.dtot[q][i]
                    sk = ('d', q, i)
                    if v > 0 and self.known[e].get(sk, 0) < v:
                        self._wait(e, self.dsem[q][i], v)
                        self.known[e][sk] = v
        if light:
            self.lastw = {k: v for k, v in self.lastw.items() if isinstance(k, tuple) and k[0] in ('slab', 'wc')}
            self.readers = {k: v for k, v in self.readers.items() if isinstance(k, tuple) and k[0] in ('slab', 'wc')}
        else:
            self.lastw = {}
            self.readers = {}


class _Stop(Exception):
    pass


class WT:
    def __init__(self, ap32, ap16, name):
        self.ap32, self.ap16, self.name = ap32, ap16, name

    def __getitem__(self, i):
        return WT(self.ap32[i], None if self.ap16 is None else self.ap16[i], self.name + "/" + str(i))


class Builder:
    stop = None

    def ck(self, k):
        if self.stop is not None and self.stop == k:
            self.S.barrier()
            self.S.stopped = True

    def __init__(self, cfg):
        self.cfg = cfg
        self.nc = bass.Bass("TRN2", target_bir_lowering=False)
        self.uid = 0

    def din(self, name, shape, dt=F32):
        return self.nc.dram_tensor(name, list(shape), dt, kind="ExternalInput").ap()

    def dout(self, name, shape, dt=F32):
        return self.nc.dram_tensor(name, list(shape), dt, kind="ExternalOutput").ap()

    def dscr(self, name, shape, dt):
        return self.nc.dram_tensor(name, list(shape), dt, kind="Internal").ap()

    def sb(self, es, name, shape, dt):
        return es.enter_context(self.nc.sbuf_tensor("sb_" + name, list(shape), dt))

    def next_ps(self):
        i = self.ps_rr
        self.ps_rr = (i + 1) % len(self.ps)
        return self.ps[i], ('ps', i)

    def next_pt(self):
        i = self.pt_rr
        self.pt_rr = (i + 1) % len(self.pt)
        return self.pt[i], ('pt', i)

    def load_slab(self, spec):
        W, r0, nk, c0, n = spec
        i = self.slab_rr
        self.slab_rr = (i + 1) % len(self.slabs)
        view = self.slabs[i][:, 0:nk * n].rearrange("p (k n) -> p k n", k=nk)
        src32 = W.ap32[r0 * 128:(r0 + nk) * 128, c0:c0 + n].rearrange("(k p) n -> p k n", p=128)
        if W.ap16 is None:
            self.S.dma('pool', view, src32, writes=[('slab', i)])
            return view, ('slab', i)
        src16 = W.ap16[r0 * 128:(r0 + nk) * 128, c0:c0 + n].rearrange("(k p) n -> p k n", p=128)
        key = (W.name, r0, nk, c0, n)
        ck = ('wc',) + key
        if key in self.wcached:
            self.S.dma('pool', view, src16, reads=[ck], writes=[('slab', i)])
        else:
            self.S.dma('pool', view, src32, writes=[('slab', i)])
            self.S.dma('sp', src16, view, reads=[('slab', i)], writes=[ck])
            self.wcached.add(key)
        return view, ('slab', i)

    def wslab(self, W, r0, nk, c0, n):
        return (W, r0, nk, c0, n)

    def norm_T(self, xg, tiles, gcol, hT, hkey, xkey='xg'):
        S, cfg = self.S, self.cfg
        D, ND = cfg.D, cfg.ND
        nst = D // 512 if D >= 512 else 1
        for t, (c0, rows) in enumerate(tiles):
            b = self.nrm_rr
            self.nrm_rr = (b + 1) % 2
            st, mv, xn = self.nst[b], self.nmv[b], self.xn[b]
            kk = ('nrm', b)
            for c in range(nst):
                S.op('dve', lambda e, c=c: e.bn_stats(out=st[:rows, c * 6:(c + 1) * 6],
                                                      in_=xg[:rows, t, c * 512:(c + 1) * 512]),
                     reads=[(xkey, t)], writes=[kk])
            S.op('dve', lambda e: e.bn_aggr(out=mv[:rows, 0:2], in_=st[:rows, 0:nst * 6]), reads=[kk], writes=[kk])
            S.op('dve', lambda e: e.scalar_tensor_tensor(out=mv[:rows, 2:3], in0=mv[:rows, 0:1], scalar=mv[:rows, 0:1],
                                                         in1=mv[:rows, 1:2], op0=ALU.mult, op1=ALU.add),
                 reads=[kk], writes=[kk])
            S.op('act', lambda e: e.activation(out=mv[:rows, 3:4], in_=mv[:rows, 2:3], func=AF.Sqrt, bias=self.epsc[:rows, 0:1], scale=1.0), reads=[kk], writes=[kk])
            S.op('dve', lambda e: e.reciprocal(out=mv[:rows, 3:4], in_=mv[:rows, 3:4]), reads=[kk], writes=[kk])
            S.op('act', lambda e: e.activation(out=xn[:rows, :], in_=xg[:rows, t, :], func=AF.Identity,
                                               scale=mv[:rows, 3:4]),
                 reads=[kk, (xkey, t)], writes=[('xn', b)])
            for d0 in range(0, ND, 8):
                nd = min(8, ND - d0)
                pt, pk = self.next_pt()
                for j in range(nd):
                    dc = d0 + j
                    S.op('pe', lambda e, j=j, dc=dc: e.transpose(out=pt[:, j * 128:j * 128 + rows],
                                                                 in_=xn[:rows, dc * 128:(dc + 1) * 128],
                                                                 identity=self.ident[:rows, :rows]),
                         reads=[('xn', b)], writes=[pk], inc=(j == nd - 1))
                src = pt[:, 0:nd * 128].rearrange("p (a b) -> p a b", a=nd)[:, :, 0:rows]
                g3 = gcol[:, d0:d0 + nd].unsqueeze(2).to_broadcast([128, nd, rows])
                S.op('dve', lambda e, src=src, g3=g3, d0=d0, nd=nd: e.tensor_tensor(
                    out=hT[:, d0:d0 + nd, c0:c0 + rows], in0=src, in1=g3, op=ALU.mult),
                    reads=[pk], writes=[hkey])

    def fm_stage(self, W, c0, ncols, KC, actT, akey, ntok, evac):
        S = self.S
        for s0 in range(0, ncols, 512):
            w = min(512, ncols - s0)
            buf, bk = self.load_slab(self.wslab(W, 0, KC, c0 + s0, w))
            for j in range(w // 128):
                ps, pk = self.next_ps()
                for kc in range(KC):
                    S.op('pe', lambda e, kc=kc, j=j, ps=ps, buf=buf: e.matmul(
                        ps[:, 0:ntok], lhsT=buf[:, kc, j * 128:(j + 1) * 128], rhs=actT[:, kc, 0:ntok],
                        start=(kc == 0), stop=(kc == KC - 1)),
                        reads=[bk, akey], writes=[pk], inc=(kc == KC - 1))
                evac((s0 // 128) + j, ps, pk)

    def tm_stage(self, W, r0, c0, ncols, KC, KS, actT, akey, tiles, evac, colw=512):
        S = self.S
        nsub = KC // KS
        for s0 in range(0, ncols, colw):
            w = min(colw, ncols - s0)
            pss = [self.next_ps() for _ in tiles] if nsub > 1 else None
            for sub in range(nsub):
                buf, bk = self.load_slab(self.wslab(W, r0 + sub * KS, KS, c0 + s0, w))
                for t, (tc0, rows) in enumerate(tiles):
                    ps, pk = pss[t] if pss else self.next_ps()
                    for kc in range(KS):
                        first = (sub == 0 and kc == 0)
                        last = (sub == nsub - 1 and kc == KS - 1)
                        S.op('pe', lambda e, kc=kc, ps=ps, buf=buf, tc0=tc0, rows=rows, sub=sub, first=first, last=last:
                             e.matmul(ps[:rows, 0:w], lhsT=actT[:, sub * KS + kc, tc0:tc0 + rows], rhs=buf[:, kc, 0:w],
                                      start=first, stop=last),
                             reads=[bk, akey], writes=[pk], inc=(kc == KS - 1))
                    if sub == nsub - 1:
                        evac(t, s0, w, ps, pk)

    def resid_add(self, xg, xkey='xg'):
        S = self.S
        def evac(t, s0, w, ps, pk):
            rows = self.cur_tiles[t][1]
            S.op('dve', lambda e: e.tensor_tensor(out=xg[:rows, t, s0:s0 + w], in0=xg[:rows, t, s0:s0 + w],
                                                  in1=ps[:rows, 0:w], op=ALU.add),
                 reads=[pk, (xkey, t)], writes=[(xkey, t)])
        return evac

    def ffn(self, l, xg, tiles, ntok):
        S, cfg = self.S, self.cfg
        ND, NF = cfg.ND, cfg.NF
        self.cur_tiles = tiles
        hT = self.hT
        self.norm_T(xg, tiles, self.gcols[:, 4 + l, :], hT, 'hT')
        aT = self.mixb[:, 0:NF * 512].rearrange("p (a b) -> p a b", a=NF)
        sg = self.mixb[:, NF * 512:NF * 512 + 1024].rearrange("p (a b) -> p a b", a=2)
        W1, W3, W2 = self.w['ffn_w1'][l], self.w['ffn_w3'][l], self.w['ffn_w2'][l]
        for s0 in range(0, cfg.DFF, 512):
            w = min(512, cfg.DFF - s0)
            b1, k1 = self.load_slab(self.wslab(W1, 0, ND, s0, w))
            b3, k3 = self.load_slab(self.wslab(W3, 0, ND, s0, w))
            for j in range(w // 128):
                fc = s0 // 128 + j
                pg, pgk = self.next_ps()
                pu, puk = self.next_ps()
                for (buf, bk, ps, pk) in ((b1, k1, pg, pgk), (b3, k3, pu, puk)):
                    for kc in range(ND):
                        S.op('pe', lambda e, kc=kc, buf=buf, ps=ps, j=j: e.matmul(
                            ps[:, 0:ntok], lhsT=buf[:, kc, j * 128:(j + 1) * 128], rhs=hT[:, kc, 0:ntok],
                            start=(kc == 0), stop=(kc == ND - 1)),
                            reads=[bk, 'hT'], writes=[pk], inc=(kc == ND - 1))
                sb = fc % 2
                S.op('act', lambda e, sb=sb, pg=pg: e.activation(out=sg[:, sb, 0:ntok], in_=pg[:, 0:ntok], func=AF.Silu),
                     reads=[pgk], writes=[('sg', sb)])
                S.op('dve', lambda e, sb=sb, pu=pu, fc=fc: e.tensor_tensor(out=aT[:, fc, 0:ntok], in0=sg[:, sb, 0:ntok],
                                                                         in1=pu[:, 0:ntok], op=ALU.mult),
                     reads=[puk, ('sg', sb)], writes=['aT'])
        self.tm_stage(W2, 0, 0, cfg.D, NF, NF // 4, aT, 'aT', tiles, self.resid_add(xg))
        S.barrier(light=True)

    def setup_A(self, ia):
        S, cfg = self.S, self.cfg
        AH = cfg.AH
        wsn = self.mix[:, 0:AH * 128].rearrange("p (a b) -> p a b", a=AH)
        wsb = self.mixb[:, 2 * AH * 128:3 * AH * 128].rearrange("p (a b) -> p a b", a=AH)
        S.dma('sp', wsn, self.w['a_w_s'][ia].rearrange("h i j -> i h j"), writes=['wsn'])
        S.dma('sp', self.T2[:, :, :].rearrange("p h i -> p (h i)"),
              self.w['a_b_s'][ia:ia + 1].rearrange("o h i -> o (h i)").to_broadcast([128, AH * 128]), writes=['T2'])
        S.op('dve', lambda e: e.tensor_copy(out=wsb, in_=wsn), reads=['wsn'], writes=['wsb'])
        for h0 in range(0, AH, 4):
            nh = min(4, AH - h0)
            pt, pk = self.next_pt()
            for j in range(nh):
                S.op('pe', lambda e, j=j: e.transpose(out=pt[:, j * 128:(j + 1) * 128], in_=wsb[:, h0 + j, :],
                                                      identity=self.ident[:, :]),
                     reads=['wsb'], writes=[pk], inc=(j == nh - 1))
            src = pt[:, 0:nh * 128].rearrange("p (a b) -> p a b", a=nh)
            m3 = self.amask[:, 3, :].unsqueeze(1).to_broadcast([128, nh, 128])
            S.op('dve', lambda e, src=src, m3=m3, h0=h0, nh=nh: e.tensor_tensor(
                out=self.WsT[:, h0:h0 + nh, :], in0=src, in1=m3, op=ALU.mult), reads=[pk], writes=['WsT'])
            ps, psk = self.next_ps()
            S.op('pe', lambda e, ps=ps, h0=h0, nh=nh: e.matmul(
                ps[:, 0:nh * 128], lhsT=self.ones_bf[:, :], rhs=self.WsT[:, h0:h0 + nh, :].rearrange("p a b -> p (a b)"), start=True, stop=True),
                reads=['WsT'], writes=[psk])
            for j in range(nh):
                h = h0 + j
                S.op('dve', lambda e, ps=ps, j=j, h=h: e.scalar_tensor_tensor(
                    out=self.T2[:, h, :], in0=ps[:, j * 128:(j + 1) * 128], scalar=self.lncols[:, ia, 1, h:h + 1],
                    in1=self.T2[:, h, :], op0=ALU.mult, op1=ALU.add), reads=[psk, 'T2'], writes=['T2'])
        S.barrier()

    def mixer_A(self, ia, layer, xg, tiles, ntok, sample_out=None):
        S, cfg = self.S, self.cfg
        D, ND, AH = cfg.D, cfg.ND, cfg.AH
        AW = D
        self.cur_tiles = tiles
        hT = self.hT
        self.norm_T(xg, tiles, self.gcols[:, layer, :], hT, 'hT')
        uT = self.mixb[:, 0:AH * 512].rearrange("p (a b) -> p a b", a=AH)
        v = self.mix[:, AH * 256:AH * 256 + 2 * AW].rearrange("p (a b) -> p a b", a=2)
        vh = self.mixb[:, AH * 512 + 4 * AW:AH * 512 + 8 * AW].rearrange("p (a b) -> p a b", a=4)
        Win, Wout = self.w['a_w_in'][ia], self.w['a_w_out'][ia]

        def evac_u(fc, ps, pk):
            S.op('act', lambda e: e.activation(out=uT[:, fc, 0:ntok], in_=ps[:, 0:ntok], func=AF.Gelu_apprx_tanh),
                 reads=[pk], writes=['uT'])
        self.fm_stage(Win, 0, AW, ND, hT, 'hT', ntok, evac_u)

        nst = AW // 512
        for p0 in range(0, len(tiles), 2):
            sub = tiles[p0:p0 + 2]

            def evac_v(t, s0, w, ps, pk, sub=sub):
                rows = sub[t][1]
                S.op('act', lambda e: e.activation(out=v[:rows, t, s0:s0 + w], in_=ps[:rows, 0:w],
                                                   func=AF.Gelu_apprx_tanh), reads=[pk], writes=[('v', t)])
            self.tm_stage(Win, 0, AW, AW, ND, ND, hT, 'hT', sub, evac_v)
            for t, (c0, rows) in enumerate(sub):
                tt = p0 + t
                b = self.nrm_rr
                self.nrm_rr = (b + 1) % 2
                st, mv = self.nst[b], self.nmv[b]
                kk = ('nrm', b)
                for c in range(nst):
                    S.op('dve', lambda e, c=c: e.bn_stats(out=st[:rows, c * 6:(c + 1) * 6],
                                                          in_=v[:rows, t, c * 512:(c + 1) * 512]),
                         reads=[('v', t)], writes=[kk])
                S.op('dve', lambda e: e.bn_aggr(out=mv[:rows, 0:2], in_=st[:rows, 0:nst * 6]), reads=[kk], writes=[kk])
                S.op('act', lambda e: e.activation(out=mv[:rows, 3:4], in_=mv[:rows, 1:2], func=AF.Sqrt, bias=self.epsc[:rows, 0:1], scale=1.0), reads=[kk], writes=[kk])
                S.op('dve', lambda e: e.reciprocal(out=mv[:rows, 3:4], in_=mv[:rows, 3:4]), reads=[kk], writes=[kk])
                if sample_out is not None:
                    vo = self.xg[:rows, 1, :]
                    S.op('dve', lambda e: e.tensor_scalar(out=vo, in0=v[:rows, t, :], scalar1=mv[:rows, 0:1],
                                                          scalar2=mv[:rows, 3:4], op0=ALU.subtract, op1=ALU.mult),
                         reads=[kk, ('v', t)], writes=[('xg', 1)])
                    S.op('dve', lambda e: e.tensor_tensor(out=vo, in0=vo, in1=self.xg[:rows, 2, :], op=ALU.mult),
                         reads=[('xg', 1), ('xg', 2)], writes=[('xg', 1)])
                    S.op('dve', lambda e: e.tensor_tensor(out=vo, in0=vo, in1=self.xg[:rows, 3, :], op=ALU.add),
                         reads=[('xg', 1), ('xg', 3)], writes=[('xg', 1)])
                    S.dma('sp', sample_out, vo, reads=[('xg', 1)])
                S.op('dve', lambda e, tt=tt: e.tensor_scalar(out=vh[:rows, tt, :], in0=v[:rows, t, :],
                                                             scalar1=mv[:rows, 0:1], scalar2=mv[:rows, 3:4],
                                                             op0=ALU.subtract, op1=ALU.mult),
                     reads=[kk, ('v', t)], writes=[('vh', tt)])
        for t, (c0, rows) in enumerate(tiles):
            for h0 in range(0, AH, 4):
                nh = min(4, AH - h0)
                ps, pk = self.next_ps()
                for j in range(nh):
                    h = h0 + j
                    S.op('pe', lambda e, ps=ps, j=j, h=h: e.matmul(
                        ps[:, j * 128:j * 128 + rows], lhsT=vh[:rows, t, h * 128:(h + 1) * 128],
                        rhs=self.WsT[:rows, h, 0:rows], start=True, stop=True),
                        reads=[('vh', t), 'WsT'], writes=[pk], inc=(j == nh - 1))
                for j in range(nh):
                    h = h0 + j
                    tb = self.mtmp_rr
                    self.mtmp_rr = (tb + 1) % 2
                    tmp = self.mtmp[tb]
                    S.op('dve', lambda e, ps=ps, j=j, h=h, tmp=tmp: e.scalar_tensor_tensor(
                        out=tmp[:, 0:rows], in0=ps[:, j * 128:j * 128 + rows], scalar=self.lncols[:, ia, 0, h:h + 1],
                        in1=self.T2[:, h, 0:rows], op0=ALU.mult, op1=ALU.add),
                        reads=[pk, 'T2'], writes=[('mtmp', tb)])
                    S.op('dve', lambda e, h=h, tmp=tmp: e.tensor_tensor(
                        out=uT[:, h, c0:c0 + rows], in0=uT[:, h, c0:c0 + rows], in1=tmp[:, 0:rows], op=ALU.mult),
                        reads=[('mtmp', tb), 'uT'], writes=['uT'])
        self.tm_stage(Wout, 0, 0, D, AH, AH, uT, 'uT', tiles, self.resid_add(xg))
        S.barrier(light=True)

    def load_x(self, src, r0, tiles):
        for t, (c0, rows) in enumerate(tiles):
            self.S.dma('sp', self.xg[:rows, t, :], src[r0 + c0:r0 + c0 + rows, :], writes=[('xg', t)])

    def store_x(self, dst, r0, tiles, only=None):
        for t, (c0, rows) in enumerate(tiles):
            if only is not None and t not in only:
                continue
            self.S.dma('sp', dst[r0 + c0:r0 + c0 + rows, :], self.xg[:rows, t, :], reads=[('xg', t)])

    def build(self):
        cfg, nc = self.cfg, self.nc
        D, ND, H, AH, NF = cfg.D, cfg.ND, cfg.H, cfg.AH, cfg.NF
        w = {}
        self.w = w
        xA = self.din("xA", [HALF, D]); xB = self.din("xB", [HALF, D]); xS = self.din("xS", [NSMP, D])
        cache = [self.din("cache%d" % g, [(128, 512, 2048)[g], 2, H, 128]) for g in range(3)]
        state = self.din("state", [15, D])
        gcols_d = self.din("gcols", [128, 8, ND])
        lncols_d = self.din("lncols", [128, 2, 2, AH])
        lnrep_d = self.din("lnrep", [2, 2, D])
        qkg_d = self.din("qkg", [1, 6 * 128])
        g2_d = self.din("g2row", [1, D])
        cs_d = self.din("csrow", [1, D])
        amask_d = self.din("amask", [128, 5, 128])
        smask_d = self.din("smask", [128, 7, 4])
        bm_d = self.din("poolp", [128, 4, 4, 128])
        bss_d = self.din("poolss", [15, 4, 4])
        bsn_d = self.din("poolsn", [4, 4, 4])
        ident_d = self.din("ident", [128, 128])
        def wt(name, shape, cache=True):
            ap16 = self.dscr("wc_" + name, shape, BF16) if cache else None
            return WT(self.din(name, shape), ap16, name)
        self.wcached = set()
        w['a_w_in'] = wt("a_w_in", [2, D, 2 * D]); w['a_w_s'] = self.din("a_w_s", [2, AH, 128, 128])
        w['a_b_s'] = self.din("a_b_s", [2, AH, 128]); w['a_w_out'] = wt("a_w_out", [2, D, D])
        w['b_w_qkv'] = wt("b_w_qkv", [1, D, 9 * D], cache=False); w['b_w_out'] = wt("b_w_out", [1, D, D])
        w['c_w'] = wt("c_w", [1, 4, cfg.CGW, cfg.CGW])
        w['ffn_w1'] = wt("ffn_w1", [4, D, cfg.DFF]); w['ffn_w3'] = wt("ffn_w3", [4, D, cfg.DFF])
        w['ffn_w2'] = wt("ffn_w2", [4, cfg.DFF, D])
        yB = self.dout("yB", [HALF, D]); yS = self.dout("yS", [NSMP, D])
        avs = self.dout("avs", [2, NSMP, D])
        kvp = [self.dout("kvp%d" % g, [(128, 512, 1024)[g], 2, H, 128]) for g in range(3)]
        kvs = [self.dout("kvs%d" % g, [NSMP, 2, H, 128]) for g in range(3)]
        poolp = self.dout("poolpo", [15, D]); pools = self.dout("poolso", [15, D])
        hT1s = self.dscr("hT1s", [128, ND, SEQ + NSMP], BF16)
        xscr = self.dscr("xscr", [NQ - HALF + HALF + NSMP, D], F32)
        OTs = self.dscr("OTs", [128, H, NOT], BF16)

        with ExitStack() as es:
            S = Sched(nc, es)
            self.S = S
            self.ps = [es.enter_context(nc.psum_tensor("ps%d" % i, [128, 512], F32)) for i in range(6)]
            self.pt = [es.enter_context(nc.psum_tensor("pt%d" % i, [128, 1024], BF16)) for i in range(2)]
            self.ps_rr = self.pt_rr = self.slab_rr = self.nrm_rr = self.mtmp_rr = 0
            self.ident = self.sb(es, "ident", [128, 128], BF16)
            self.ones_bf = self.sb(es, "ones", [128, 128], BF16)
            self.amask = self.sb(es, "amask", [128, 5, 128], BF16)
            self.smask = self.sb(es, "smask", [128, 7, 4], BF16)
            self.gcols = self.sb(es, "gcols", [128, 8, ND], F32)
            self.lncols = self.sb(es, "lncols", [128, 2, 2, AH], F32)
            self.qkg = self.sb(es, "qkg", [128, 6, 128], F32)
            self.nst = [self.sb(es, "nst%d" % i, [128, 24], F32) for i in range(2)]
            self.nmv = [self.sb(es, "nmv%d" % i, [128, 4], F32) for i in range(2)]
            self.mtmp = [self.sb(es, "mtmp%d" % i, [128, 128], F32) for i in range(2)]
            self.epsc = self.sb(es, "epsc", [128, 1], F32)
            S.dma('pool', self.ident[:, :], ident_d[:, :], writes=['c0'])
            S.dma('pool', self.amask[:, :, :], amask_d[:, :, :], writes=['c1'])
            S.dma('pool', self.smask[:, :, :], smask_d[:, :, :], writes=['c2'])
            S.dma('sp', self.gcols[:, :, :], gcols_d[:, :, :], writes=['c3'])
            S.dma('sp', self.lncols[:, :, :, :], lncols_d[:, :, :, :], writes=['c4'])
            S.dma('sp', self.qkg[:, :, :].rearrange("p a b -> p (a b)"), qkg_d[0:1, :].to_broadcast([128, 6 * 128]),
                  writes=['c5'])
            S.op('dve', lambda e: e.memset(self.ones_bf[:, :], 1.0), writes=['c6'])
            S.op('dve', lambda e: e.memset(self.epsc[:, :], EPS), writes=['c7'])
            S.barrier()
            self.body(locals())
            S.stopped = False
            S.barrier()
            S.emit_all()
        return nc

    def body(self, L):
        cfg, nc, S, w = self.cfg, self.nc, self.S, self.w
        D, ND, H, AH, NF = cfg.D, cfg.ND, cfg.H, cfg.AH, cfg.NF
        xA, xB, xS, cache, state = L['xA'], L['xB'], L['xS'], L['cache'], L['state']
        lnrep_d, g2_d, cs_d, bm_d, bss_d, bsn_d = L['lnrep_d'], L['g2_d'], L['cs_d'], L['bm_d'], L['bss_d'], L['bsn_d']
        yB, yS, avs, kvp, kvs, poolp, pools = L['yB'], L['yS'], L['avs'], L['kvp'], L['kvs'], L['poolp'], L['pools']
        hT1s, xscr, OTs = L['hT1s'], L['xscr'], L['OTs']
        self.ck(0)
        if True:
            full = [(i * 128, 128) for i in range(4)]
            with ExitStack() as p1:
                self.slabs = [self.sb(p1, "slab%d" % i, [128, 8192], BF16) for i in range(4)]
                self.xn = [self.sb(p1, "xn%d" % i, [128, D], BF16) for i in range(2)]
                self.slab_rr = 0
                self.xg = self.sb(p1, "xg", [128, 4, D], F32)
                self.hT = self.sb(p1, "hT", [128, ND, 512], BF16)
                mixw = max(NF * 512 + 1024, AH * 512 + 8 * D, 2 * (ND * 256 + 3 * D + 2048 + 1024))
                self.mixb = self.sb(p1, "mix", [128, mixw], BF16)
                self.mix = self.mixb.bitcast(F32)
                self.WsT = self.sb(p1, "WsT", [128, AH, 128], BF16)
                self.T2 = self.sb(p1, "T2", [128, AH, 128], F32)
                self.setup_A(0)
                self.ck(1)
                groups = [('A', xA, 0, full, 0), ('A', xA, 512, full, 512), ('B', xB, 0, full, 1024),
                          ('B', xB, 512, full, 1536), ('S', xS, 0, [(0, NSMP)], SEQ)]
                for (kind, src, r0, tiles, hcol) in groups:
                    ntok = sum(r for _, r in tiles)
                    self.load_x(src, r0, tiles)
                    so = None
                    if kind == 'S':
                        S.dma('sp', self.xg[:NSMP, 2, :], lnrep_d[0, 0:1, :].to_broadcast([NSMP, D]), writes=[('xg', 2)])
                        S.dma('sp', self.xg[:NSMP, 3, :], lnrep_d[0, 1:2, :].to_broadcast([NSMP, D]), writes=[('xg', 3)])
                        so = avs[0, :, :]
                    self.mixer_A(0, 0, self.xg, tiles, ntok, sample_out=so)
                    self.ck(2)
                    self.ffn(0, self.xg, tiles, ntok)
                    self.ck(3)
                    self.norm_T(self.xg, tiles, self.gcols[:, 1, :], self.hT, 'hT')
                    S.dma('sp', hT1s[:, :, hcol:hcol + ntok], self.hT[:, :, 0:ntok], reads=['hT'])
                    if kind == 'A' and r0 == 512:
                        self.store_x(xscr, 0 - 384, tiles, only=[3])
                    elif kind == 'B':
                        self.store_x(xscr, 128 + r0, tiles)
                    elif kind == 'S':
                        self.store_x(xscr, 128 + HALF, tiles)
                    S.barrier(light=True)
            S.barrier()

            self.ck(4)
            with ExitStack() as p2:
                self.phase2(p2, hT1s, OTs, cache, kvp, kvs)
            S.barrier()

            self.ck(5)
            with ExitStack() as p3:
                self.slabs = [self.sb(p3, "slab3_%d" % i, [128, 8192], BF16) for i in range(4)]
                self.xn = [self.sb(p3, "xn3_%d" % i, [128, D], BF16) for i in range(2)]
                self.slab_rr = 0
                self.xg = self.sb(p3, "xg3", [128, 4, D], F32)
                self.hT = self.sb(p3, "hT3", [128, ND, 512], BF16)
                mixw = max(NF * 512 + 1024, AH * 512 + 8 * D, 2 * (ND * 256 + 3 * D + 2048 + 1024))
                self.mixb = self.sb(p3, "mix3", [128, mixw], BF16)
                self.mix = self.mixb.bitcast(F32)
                self.WsT = self.sb(p3, "WsT3", [128, AH, 128], BF16)
                self.T2 = self.sb(p3, "T23", [128, AH, 128], F32)
                self.h2halo = self.sb(p3, "h2halo", [128, D], F32)
                self.setup_A(1)
                OTg = self.mixb[:, 0:H * 512].rearrange("p (a b) -> p a b", a=H)

                def load_scr(tiles, srows):
                    for t, (c0, rows) in enumerate(tiles):
                        S.dma('sp', self.xg[:rows, t, :], xscr[srows[t]:srows[t] + rows, :], writes=[('xg', t)])

                def store_scr(tiles, srows, only=None):
                    for t, (c0, rows) in enumerate(tiles):
                        if only is None or t in only:
                            S.dma('sp', xscr[srows[t]:srows[t] + rows, :], self.xg[:rows, t, :], reads=[('xg', t)])

                def l1post(tiles, srows):
                    ntok = sum(r for _, r in tiles)
                    self.cur_tiles = tiles
                    load_scr(tiles, srows)
                    for t, (c0, rows) in enumerate(tiles):
                        S.dma('sp', OTg[:, :, c0:c0 + rows], OTs[:, :, srows[t]:srows[t] + rows], writes=['OTg'])
                    self.tm_stage(w['b_w_out'][0], 0, 0, D, H, H, OTg, 'OTg', tiles, self.resid_add(self.xg))
                    S.barrier(light=True)
                    self.ffn(1, self.xg, tiles, ntok)

                def rest(kind, tiles, first_b, pool_out, ydst, yrow):
                    ntok = sum(r for _, r in tiles)
                    self.cur_tiles = tiles
                    if kind == 'S':
                        self.pool_sample(state, pools, g2_d, cs_d, bss_d, bsn_d)
                    else:
                        self.pool_prompt('B', tiles, ntok, g2_d, cs_d, bm_d, first_b, pool_out)
                    self.ffn(2, self.xg, tiles, ntok)
                    so = None
                    if kind == 'S':
                        S.dma('sp', self.xg[:NSMP, 2, :], lnrep_d[1, 0:1, :].to_broadcast([NSMP, D]), writes=[('xg', 2)])
                        S.dma('sp', self.xg[:NSMP, 3, :], lnrep_d[1, 1:2, :].to_broadcast([NSMP, D]), writes=[('xg', 3)])
                        so = avs[1, :, :]
                    self.mixer_A(1, 3, self.xg, tiles, ntok, sample_out=so)
                    self.ffn(3, self.xg, tiles, ntok)
                    self.store_x(ydst, yrow, tiles)
                    S.barrier(light=True)

                rows_b0 = [128, 256, 384, 512]
                rows_b1 = [640, 768, 896, 1024]
                t_hs = [(0, 128), (128, NSMP)]
                t_s = [(0, NSMP)]
                l1post(full, rows_b0)
                store_scr(full, rows_b0)
                S.barrier(light=True)
                l1post(t_hs, [0, 128 + HALF])
                self.pool_prompt('H', t_hs, 132, g2_d, cs_d, bm_d, False, None)
                store_scr(t_hs, [0, 128 + HALF], only=[1])
                S.barrier(light=True)
                self.ck(6)
                load_scr(full, rows_b0)
                rest('B', full, True, None, yB, 0)
                l1post(full, rows_b1)
                rest('B', full, False, poolp, yB, 512)
                load_scr(t_s, [128 + HALF])
                rest('S', t_s, False, None, yS, 0)
            S.barrier()

    def pool_common(self, tiles, ntok, zT, csrep):
        S, cfg = self.S, self.cfg
        NCG, CGW = cfg.NCG, cfg.CGW
        for pg in range(4):
            buf, bk = self.load_slab(self.wslab(self.w['c_w'][0][pg], 0, cfg.NCG, 0, cfg.CGW))
            for t, (c0, rows) in enumerate(tiles):
                for s0 in range(0, CGW, 512):
                    wd = min(512, CGW - s0)
                    ps, pk = self.next_ps()
                    for kc in range(NCG):
                        S.op('pe', lambda e, kc=kc, ps=ps, buf=buf: e.matmul(
                            ps[:rows, 0:wd], lhsT=zT[:, pg * NCG + kc, c0:c0 + rows], rhs=buf[:, kc, s0:s0 + wd],
                            start=(kc == 0), stop=(kc == NCG - 1)),
                            reads=[bk, 'zT'], writes=[pk], inc=(kc == NCG - 1))
                    col = pg * CGW + s0
                    tb = self.mtmp_rr
                    self.mtmp_rr = (tb + 1) % 2
                    tmp = self.ptmp[tb]
                    S.op('dve', lambda e, ps=ps, tmp=tmp: e.tensor_tensor(out=tmp[:rows, 0:wd], in0=ps[:rows, 0:wd],
                                                                        in1=csrep[:rows, col:col + wd], op=ALU.mult),
                         reads=[pk, 'csrep'], writes=[('ptmp', tb)])
                    S.op('dve', lambda e, tmp=tmp, t=t: e.tensor_tensor(
                        out=self.xg[:rows, t, col:col + wd], in0=self.xg[:rows, t, col:col + wd], in1=tmp[:rows, 0:wd],
                        op=ALU.add), reads=[('ptmp', tb), ('xg', t)], writes=[('xg', t)])

    def pool_regions(self):
        cfg = self.cfg
        D, ND = cfg.D, cfg.ND
        zT = self.mixb[:, 0:ND * 512].rearrange("p (a b) -> p a b", a=ND)
        o = ND * 256
        g2rep = self.mix[:, o:o + D]
        csrep = self.mix[:, o + D:o + 2 * D]
        bm = self.mix[:, o + 2 * D:o + 2 * D + 2048].rearrange("p (a b c) -> p a b c", a=4, b=4)
        h2 = self.mix[:, o + 2 * D + 2048:o + 3 * D + 2048]
        pt0 = self.mix[:, o + 3 * D + 2048:o + 3 * D + 2048 + 512]
        pt1 = self.mix[:, o + 3 * D + 2560:o + 3 * D + 2560 + 512]
        self.ptmp = [pt0, pt1]
        return zT, g2rep, csrep, bm, h2

    def h2_rows(self, t, rows, g2rep, dst, dkey):
        S, cfg = self.S, self.cfg
        D = cfg.D
        nst = D // 512
        b = self.nrm_rr
        self.nrm_rr = (b + 1) % 2
        st, mv = self.nst[b], self.nmv[b]
        kk = ('nrm', b)
        for c in range(nst):
            S.op('dve', lambda e, c=c: e.bn_stats(out=st[:rows, c * 6:(c + 1) * 6],
                                                  in_=self.xg[:rows, t, c * 512:(c + 1) * 512]),
                 reads=[('xg', t)], writes=[kk])
        S.op('dve', lambda e: e.bn_aggr(out=mv[:rows, 0:2], in_=st[:rows, 0:nst * 6]), reads=[kk], writes=[kk])
        S.op('dve', lambda e: e.scalar_tensor_tensor(out=mv[:rows, 2:3], in0=mv[:rows, 0:1], scalar=mv[:rows, 0:1],
                                                     in1=mv[:rows, 1:2], op0=ALU.mult, op1=ALU.add),
             reads=[kk], writes=[kk])
        S.op('act', lambda e: e.activation(out=mv[:rows, 3:4], in_=mv[:rows, 2:3], func=AF.Sqrt, bias=self.epsc[:rows, 0:1], scale=1.0), reads=[kk], writes=[kk])
        S.op('dve', lambda e: e.reciprocal(out=mv[:rows, 3:4], in_=mv[:rows, 3:4]), reads=[kk], writes=[kk])
        S.op('dve', lambda e: e.scalar_tensor_tensor(out=dst[:rows, :], in0=self.xg[:rows, t, :], scalar=mv[:rows, 3:4],
                                                     in1=g2rep[:rows, :], op0=ALU.mult, op1=ALU.mult),
             reads=[kk, ('xg', t), 'g2rep'], writes=[dkey])

    def pool_prompt(self, kind, tiles, ntok, g2_d, cs_d, bm_d, first_b, pool_out):
        S, cfg = self.S, self.cfg
        D, ND, NCG = cfg.D, cfg.ND, cfg.NCG
        zT, g2rep, csrep, bm, h2 = self.pool_regions()
        S.dma('sp', g2rep, g2_d[0:1, :].to_broadcast([128, D]), writes=['g2rep'])
        if kind == 'H':
            self.h2_rows(0, 128, g2rep, h2, 'h2')
            S.dma('sp', self.h2halo[0:32, :], h2[96:128, :], reads=['h2'], writes=['h2halo'])
            return
        S.dma('sp', csrep, cs_d[0:1, :].to_broadcast([128, D]), writes=['csrep'])
        S.dma('sp', bm, bm_d[:, :, :, :], writes=['bm'])
        for t, (c0, rows) in enumerate(tiles):
            self.h2_rows(t, rows, g2rep, h2, 'h2')
            kc_, kp_ = (2, 3) if (first_b and t == 0) else (0, 1)
            for d0 in range(0, ND, 4):
                ps, pk = self.next_ps()
                for j in range(4):
                    dc = d0 + j
                    pg = dc // NCG
                    S.op('pe', lambda e, ps=ps, j=j, dc=dc, pg=pg: e.matmul(
                        ps[:, j * 128:(j + 1) * 128], lhsT=self.h2halo[0:32, dc * 128:(dc + 1) * 128],
                        rhs=bm[0:32, pg, kp_, :], start=True, stop=False),
                        reads=['h2halo', 'bm'], writes=[pk], inc=False)
                    S.op('pe', lambda e, ps=ps, j=j, dc=dc, pg=pg: e.matmul(
                        ps[:, j * 128:(j + 1) * 128], lhsT=h2[:, dc * 128:(dc + 1) * 128],
                        rhs=bm[:, pg, kc_, :], start=False, stop=True),
                        reads=['h2', 'bm'], writes=[pk], inc=(j == 3))
                src = ps[:, 0:512].rearrange("p (a b) -> p a b", a=4)
                S.op('act', lambda e, src=src, d0=d0: e.activation(out=zT[:, d0:d0 + 4, c0:c0 + rows], in_=src, func=AF.Identity),
                     reads=[pk], writes=['zT'])
            S.dma('sp', self.h2halo[0:32, :], h2[96:128, :], reads=['h2'], writes=['h2halo'])
            if pool_out is not None and t == len(tiles) - 1:
                S.dma('sp', pool_out[0:15, :], h2[113:128, :], reads=['h2'])
        self.pool_common(tiles, ntok, zT, csrep)
        S.barrier(light=True)

    def pool_sample(self, state, pools, g2_d, cs_d, bss_d, bsn_d):
        S, cfg = self.S, self.cfg
        D, ND, NCG = cfg.D, cfg.ND, cfg.NCG
        zT, g2rep, csrep, bm, h2 = self.pool_regions()
        tiles = [(0, NSMP)]
        S.dma('sp', g2rep, g2_d[0:1, :].to_broadcast([128, D]), writes=['g2rep'])
        S.dma('sp', csrep, cs_d[0:1, :].to_broadcast([128, D]), writes=['csrep'])
        bss = bm[0:15, 0, :, 0:4]
        bsn = bm[0:4, 1, :, 0:4]
        S.dma('sp', bss, bss_d[:, :, :], writes=['bm'])
        S.dma('sp', bsn, bsn_d[:, :, :], writes=['bm'])
        st15 = self.h2halo
        S.dma('sp', st15[0:15, :], state[:, :], writes=['h2halo'])
        S.dma('sp', pools[0:11, :], state[4:15, :])
        self.h2_rows(0, NSMP, g2rep, h2, 'h2')
        S.dma('sp', pools[11:15, :], h2[0:NSMP, :], reads=['h2'])
        for d0 in range(0, ND, 4):
            ps, pk = self.next_ps()
            for j in range(4):
                dc = d0 + j
                pg = dc // NCG
                S.op('pe', lambda e, ps=ps, j=j, dc=dc, pg=pg: e.matmul(
                    ps[:, j * 128:j * 128 + NSMP], lhsT=st15[0:15, dc * 128:(dc + 1) * 128], rhs=bss[:, pg, :],
                    start=True, stop=False), reads=['h2halo', 'bm'], writes=[pk], inc=False)
                S.op('pe', lambda e, ps=ps, j=j, dc=dc, pg=pg: e.matmul(
                    ps[:, j * 128:j * 128 + NSMP], lhsT=h2[0:NSMP, dc * 128:(dc + 1) * 128], rhs=bsn[:, pg, :],
                    start=False, stop=True), reads=['h2', 'bm'], writes=[pk], inc=(j == 3))
            src = ps[:, 0:512].rearrange("p (a b) -> p a b", a=4)[:, :, 0:NSMP]
            S.op('act', lambda e, src=src, d0=d0: e.activation(out=zT[:, d0:d0 + 4, 0:NSMP], in_=src, func=AF.Identity),
                 reads=[pk], writes=['zT'])
        self.pool_common(tiles, NSMP, zT, csrep)
        S.barrier(light=True)

    def phase2(self, p2, hT1s, OTs, cache, kvp, kvs):
        S, cfg, nc = self.S, self.cfg, self.nc
        D, ND, H = cfg.D, cfg.ND, cfg.H
        NTK = SEQ + NSMP
        scale = 128.0 ** -0.5
        self.slabs = [self.sb(p2, "slab2_%d" % i, [128, ND * 256], BF16) for i in range(5)]
        self.slab_rr = 0
        hT1 = self.sb(p2, "hT1", [128, ND, NTK], BF16)
        KT = self.sb(p2, "KT", [128, 2, NTK], BF16)
        QT = self.sb(p2, "QT", [128, 2, NOT], BF16)
        Vb = self.sb(p2, "Vb", [128, 16, 256], BF16)
        Vsb = self.sb(p2, "Vsb", [NSMP, 256], BF16)
        acc = self.sb(p2, "acc", [128, 2, 2, NOT], F32)
        OTb = self.sb(p2, "OTb", [128, 2, NOT], BF16)
        vf = [self.sb(p2, "vf%d" % i, [128, 256], F32) for i in range(2)]
        Pb = [self.sb(p2, "Pb%d" % i, [128, 256], BF16) for i in range(6)]
        kcb = [self.sb(p2, "kcb%d" % i, [128, 256], BF16) for i in range(4)]
        vcb = [self.sb(p2, "vcb%d" % i, [128, 256], BF16) for i in range(4)]
        KcT = [self.sb(p2, "KcT%d" % i, [128, 128], BF16) for i in range(4)]
        Ps = [self.sb(p2, "Ps%d" % i, [128, 8], BF16) for i in range(4)]
        NROT = {'kn': 2, 'vf': 2, 'P': 6, 'kc': 4, 'KcT': 4, 'Ps': 4}
        kraw = self.sb(p2, "kraw", [128, 17, 256], F32)
        ksq = self.sb(p2, "ksq", [128, 17, 256], F32)
        kbf = self.sb(p2, "kbf", [128, 17, 256], BF16)
        kss = self.sb(p2, "kss", [128, 34], F32)
        rr = {'kn': 0, 'vf': 0, 'P': 0, 'kc': 0, 'KcT': 0, 'Ps': 0}

        def nxt(name):
            i = rr[name]
            rr[name] = (i + 1) % NROT[name]
            return i

        for c0 in range(0, NTK, 1024):
            n = min(1024, NTK - c0)
            S.dma('sp', hT1[:, :, c0:c0 + n], hT1s[:, :, c0:c0 + n], writes=['hT1'])
        Wqkv = self.w['b_w_qkv'][0]
        self.ck(10)

        def proj(buf, bk, cols, rows, step=1):
            ps, pk = self.next_ps()
            for kc in range(ND):
                lhsT = hT1[:, kc, cols:cols + (rows - 1) * step + 1:step] if step > 1 else hT1[:, kc, cols:cols + rows]
                S.op('pe', lambda e, kc=kc, lhsT=lhsT, ps=ps: e.matmul(ps[:rows, 0:256], lhsT=lhsT, rhs=buf[:, kc, :],
                                                                      start=(kc == 0), stop=(kc == ND - 1)),
                     reads=[bk, 'hT1'], writes=[pk], inc=(kc == ND - 1))
            return ps, pk

        def qk_norm(ps, pk, rows, gi):
            i = nxt('kn')
            f, b_, st = knf[i], knb[i], qst[i]
            S.op('dve', lambda e: e.tensor_copy(out=f[:rows, :], in_=ps[:rows, 0:256]),
                 reads=[pk], writes=[('knf', i)])
            for h in range(2):
                S.op('dve', lambda e, h=h: e.bn_stats(out=st[:rows, 0:6], in_=f[:rows, h * 128:(h + 1) * 128]),
                     reads=[('knf', i)], writes=[('qst', i)])
                S.op('dve', lambda e: e.bn_aggr(out=st[:rows, 6:8], in_=st[:rows, 0:6]), reads=[('qst', i)],
                     writes=[('qst', i)])
                S.op('dve', lambda e: e.scalar_tensor_tensor(out=st[:rows, 0:1], in0=st[:rows, 6:7], scalar=st[:rows, 6:7],
                                                             in1=st[:rows, 7:8], op0=ALU.mult, op1=ALU.add),
                     reads=[('qst', i)], writes=[('qst', i)])
                S.op('act', lambda e: e.activation(out=st[:rows, 1:2], in_=st[:rows, 0:1], func=AF.Sqrt, bias=self.epsc[:rows, 0:1], scale=1.0), reads=[('qst', i)], writes=[('qst', i)])
                S.op('dve', lambda e: e.reciprocal(out=st[:rows, 1:2], in_=st[:rows, 1:2]), reads=[('qst', i)], writes=[('qst', i)])
                S.op('dve', lambda e, h=h: e.scalar_tensor_tensor(
                    out=f[:rows, h * 128:(h + 1) * 128], in0=f[:rows, h * 128:(h + 1) * 128], scalar=st[:rows, 1:2],
                    in1=self.qkg[:rows, gi, :], op0=ALU.mult, op1=ALU.mult),
                    reads=[('qst', i), ('knf', i)], writes=[('knf', i)])
            S.op('act', lambda e: e.activation(out=b_[:rows, :], in_=f[:rows, :], func=AF.Identity),
                 reads=[('knf', i)], writes=[('knb', i)])
            return i

        def to_T(i, rows, dst, dkey, col):
            pt, pk = self.next_pt()
            for h in range(2):
                S.op('pe', lambda e, h=h: e.transpose(out=pt[:, h * 128:h * 128 + rows],
                                                      in_=knb[i][:rows, h * 128:(h + 1) * 128],
                                                      identity=self.ident[:rows, :rows]),
                     reads=[('knb', i)], writes=[pk], inc=(h == 1))
            src = pt[:, 0:256].rearrange("p (a b) -> p a b", a=2)[:, :, 0:rows]
            S.op('act', lambda e: e.activation(out=dst[:, :, col:col + rows], in_=src, func=AF.Identity),
                 reads=[pk], writes=[dkey])

        S.op('dve', lambda e: e.memset(kraw[:, :, :], 0.0), writes=[('kraw', 0), ('kraw', 1)])

        def qk_pc(buf, bkey, tl, s0, R, gi, out_fn):
            n = len(tl)
            kr, kq_, kb_, ks_ = ('kraw', R), ('ksq', R), ('kbf', R), ('kss', R)
            for j, (cols, rows) in enumerate(tl):
                ps, pk = proj(buf, bkey, cols, rows)
                if j % 2 == 0:
                    S.op('act', lambda e, j=j, ps=ps, rows=rows: e.activation(out=kraw[:rows, s0 + j, :], in_=ps[:rows, 0:256],
                                                                            func=AF.Identity), reads=[pk], writes=[kr])
                else:
                    S.op('dve', lambda e, j=j, ps=ps, rows=rows: e.tensor_copy(out=kraw[:rows, s0 + j, :], in_=ps[:rows, 0:256]),
                         reads=[pk], writes=[kr])
            k3 = kraw[:, s0:s0 + n, :].rearrange("p t (h e) -> p (t h) e", h=2)
            q3 = ksq[:, s0:s0 + n, :].rearrange("p t (h e) -> p (t h) e", h=2)
            ss = kss[:, 2 * s0:2 * s0 + 2 * n]
            S.op('dve', lambda e: e.tensor_tensor(out=ksq[:, s0:s0 + n, :], in0=kraw[:, s0:s0 + n, :],
                                                  in1=kraw[:, s0:s0 + n, :], op=ALU.mult), reads=[kr], writes=[kq_])
            S.op('dve', lambda e: e.tensor_reduce(out=ss, in_=q3, axis=mybir.AxisListType.X, op=ALU.add),
                 reads=[kq_], writes=[ks_])
            S.op('act', lambda e: e.activation(out=ss, in_=ss, func=AF.Sqrt, bias=self.epsc[:, 0:1], scale=1.0 / 128.0),
                 reads=[ks_], writes=[ks_])
            S.op('dve', lambda e: e.reciprocal(out=ss, in_=ss), reads=[ks_], writes=[ks_])
            S.op('dve', lambda e: e.tensor_tensor(out=k3, in0=k3, in1=ss.unsqueeze(2).to_broadcast([128, 2 * n, 128]),
                                                  op=ALU.mult), reads=[ks_, kr], writes=[kr])
            S.op('dve', lambda e: e.tensor_tensor(out=k3, in0=k3,
                                                  in1=self.qkg[:, gi, :].unsqueeze(1).to_broadcast([128, 2 * n, 128]),
                                                  op=ALU.mult), reads=[kr], writes=[kr])
            S.op('act', lambda e: e.activation(out=kbf[:, s0:s0 + n, :], in_=kraw[:, s0:s0 + n, :], func=AF.Identity),
                 reads=[kr], writes=[kb_])
            for j, (cols, rows) in enumerate(tl):
                out_fn(s0 + j, cols, rows, kr)

        def qk_T(tl, s0, R, dstT, dkey, dcol):
            n = len(tl)
            kb_ = ('kbf', R)
            for j0 in range(0, n, 4):
                nj = min(4, n - j0)
                pt, ptk = self.next_pt()
                full = all(tl[j0 + jj][1] == 128 for jj in range(nj))
                for jj in range(nj):
                    rows = tl[j0 + jj][1]
                    for h in range(2):
                        S.op('pe', lambda e, jj=jj, h=h, rows=rows: e.transpose(
                            out=pt[:, (jj * 2 + h) * 128:(jj * 2 + h) * 128 + rows],
                            in_=kbf[:rows, s0 + j0 + jj, h * 128:(h + 1) * 128], identity=self.ident[:rows, :rows]),
                            reads=[kb_], writes=[ptk], inc=(jj == nj - 1 and h == 1))
                if full and all(dcol(tl[j0 + jj][0]) == dcol(tl[j0][0]) + 128 * jj for jj in range(nj)):
                    c0 = dcol(tl[j0][0])
                    for h in range(2):
                        src = pt[:, 0:nj * 256].rearrange("p (t h e) -> p t h e", t=nj, h=2)[:, :, h, :]
                        dst = dstT[:, h, c0:c0 + nj * 128].rearrange("p (t e) -> p t e", t=nj)
                        if h == 0:
                            S.op('act', lambda e, src=src, dst=dst: e.activation(out=dst, in_=src, func=AF.Identity),
                                 reads=[ptk], writes=[dkey, ptk])
                        else:
                            S.op('dve', lambda e, src=src, dst=dst: e.tensor_copy(out=dst, in_=src),
                                 reads=[ptk], writes=[dkey, ptk])
                else:
                    for jj in range(nj):
                        cols, rows = tl[j0 + jj]
                        src = pt[:, jj * 256:(jj + 1) * 256].rearrange("p (a b) -> p a b", a=2)[:, :, 0:rows]
                        S.op('act', lambda e, src=src, cols=cols, rows=rows: e.activation(
                            out=dstT[:, :, dcol(cols):dcol(cols) + rows], in_=src, func=AF.Identity),
                            reads=[ptk], writes=[dkey, ptk])

        for hb in range(H // 2):
            h0 = hb * 2
            S.op('dve', lambda e: e.memset(acc[:, :, :, :], 0.0), writes=['acc'])
            for g in range(3):
                dil = DILS[g]
                base = (g * 3) * D + h0 * 128
                bq, kq = self.load_slab(self.wslab(Wqkv, 0, ND, base, 256))
                bk_, kk_ = self.load_slab(self.wslab(Wqkv, 0, ND, base + D, 256))
                bv, kv_ = self.load_slab(self.wslab(Wqkv, 0, ND, base + 2 * D, 256))
                keep0 = (SEQ - (128, 512, 1024)[g])
                def k_out(slot, cols, rows, kr, g=g, h0=h0, keep0=keep0):
                    src = kraw[:rows, slot, :].rearrange("p (a b) -> p a b", a=2)
                    if rows == NSMP:
                        S.dma('sp', kvs[g][:, 0, h0:h0 + 2, :], src, reads=[kr])
                    elif cols >= keep0:
                        S.dma('sp', kvp[g][cols - keep0:cols - keep0 + 128, 0, h0:h0 + 2, :], src, reads=[kr])
                no_out = lambda slot, c, r, kr: None
                tka = [(ti * 128, 128) for ti in range(0, 9)]
                tkb = [(ti * 128, 128) for ti in range(9, 16)] + [(SEQ, NSMP)]
                tqa = [(ti * 128, 128) for ti in range(7, 12)]
                tqb = [(ti * 128, 128) for ti in range(12, 16)] + [(SEQ, NSMP)]
                kcol = lambda c: c
                qcol = lambda c: (NQ if c == SEQ else c - NQ0)
                qk_pc(bk_, kk_, tka, 0, 0, 3 + g, k_out)
                qk_pc(bk_, kk_, tkb, 9, 1, 3 + g, k_out)
                qk_T(tka, 0, 0, KT, 'KT', kcol)
                qk_pc(bq, kq, tqa, 0, 0, g, no_out)
                qk_T(tkb, 9, 1, KT, 'KT', kcol)
                qk_pc(bq, kq, tqb, 9, 1, g, no_out)
                if hb == 0 and g == 0:
                    self.ck(11)
                L = SEQ // dil
                nb = L // 128
                for p in range(16):
                    r, c = divmod(p, nb)
                    tok0 = r + dil * 128 * c
                    ps, pk = proj(bv, kv_, tok0, 128, step=dil)
                    S.op('act', lambda e, p=p, ps=ps: e.activation(out=Vb[:, p, :], in_=ps[:, 0:256], func=AF.Identity),
                         reads=[pk], writes=[('Vb', p)])
                    i0 = 0
                    while tok0 + dil * i0 < keep0:
                        i0 += 1
                        if i0 >= 128:
                            break
                    if i0 < 128 and i0 in (0, 64):
                        j = nxt('vf')
                        S.op('dve', lambda e, j=j, ps=ps: e.tensor_copy(out=vf[j][:, :], in_=ps[:, 0:256]),
                             reads=[pk], writes=[('vf', j), pk])
                        o0 = tok0 + dil * i0 - keep0
                        n = 128 - i0
                        dst = kvp[g][o0:o0 + dil * (n - 1) + 1:dil, 1, h0:h0 + 2, :] if dil > 1 else \
                            kvp[g][o0:o0 + n, 1, h0:h0 + 2, :]
                        S.dma('sp', dst, vf[j][i0:128, :].rearrange("p (a b) -> p a b", a=2), reads=[('vf', j)])
                ps, pk = proj(bv, kv_, SEQ, NSMP)
                S.op('act', lambda e, ps=ps: e.activation(out=Vsb[:, :], in_=ps[:NSMP, 0:256], func=AF.Identity),
                     reads=[pk], writes=['Vsb'])
                j = nxt('vf')
                S.op('dve', lambda e, j=j, ps=ps: e.tensor_copy(out=vf[j][:NSMP, :], in_=ps[:NSMP, 0:256]),
                     reads=[pk], writes=[('vf', j), pk])
                S.dma('sp', kvs[g][:, 1, h0:h0 + 2, :], vf[j][:NSMP, :].rearrange("p (a b) -> p a b", a=2),
                      reads=[('vf', j)])

                qk_T(tqa, 0, 0, QT, 'QT', qcol)
                qk_T(tqb, 9, 1, QT, 'QT', qcol)
                if hb == 0 and g == 0:
                    self.ck(13)
                blocks = []
                if g == 0:
                    for c in range(7, 16):
                        blocks.append((128 * (c - 7), 128, 0,
                                       [(128 * (c - 1), c - 1, 0 if c - 1 <= 7 else 2),
                                        (128 * c, c, 1 if c <= 7 else 3)]))
                elif g == 1:
                    for r in range(4):
                        for c in range(1, 4):
                            i0 = 96 if c == 1 else 0
                            q0 = r + 4 * (128 * c + i0) - NQ0
                            blocks.append((q0, 128 - i0, i0,
                                           [(r + 512 * (c - 1), r * 4 + c - 1, 0 if c - 1 <= 1 else 2),
                                            (r + 512 * c, r * 4 + c, 1 if c <= 1 else 3)]))
                else:
                    for r in range(16):
                        blocks.append((r, 72, 56, [(r, r, 4)]))
                def p_front(h, blk):
                    (q0, nq, i0, kbl) = blk
                    qs = QT[:, h, q0:q0 + dil * (nq - 1) + 1:dil] if dil > 1 else QT[:, h, q0:q0 + nq]
                    ps, pk = self.next_ps()
                    for bi, (k0, vt, mi) in enumerate(kbl):
                        ks = KT[:, h, k0:k0 + dil * 127 + 1:dil] if dil > 1 else KT[:, h, k0:k0 + 128]
                        S.op('pe', lambda e, ps=ps, bi=bi, ks=ks, qs=qs, nq=nq: e.matmul(
                            ps[:, bi * 128:bi * 128 + nq], lhsT=ks, rhs=qs, start=True, stop=True),
                            reads=['KT', 'QT'], writes=[pk], inc=(bi == len(kbl) - 1))
                    pi = nxt('P')
                    P = Pb[pi]
                    nb_ = len(kbl)
                    src = ps[:, 0:nb_ * 128].rearrange("p (a b) -> p a b", a=nb_)[:, :, 0:nq]
                    dstP = P[:, 0:nb_ * 128].rearrange("p (a b) -> p a b", a=nb_)[:, :, 0:nq]
                    S.op('act', lambda e, src=src, dstP=dstP: e.activation(out=dstP, in_=src, func=AF.Exp, scale=scale),
                         reads=[pk], writes=[('P', pi)])
                    if nb_ == 2 and kbl[1][2] == kbl[0][2] + 1:
                        m0 = kbl[0][2]
                        S.op('dve', lambda e, P=P, nq=nq, i0=i0, m0=m0, dstP=dstP: e.tensor_tensor(
                            out=dstP, in0=dstP, in1=self.amask[:, m0:m0 + 2, i0:i0 + nq], op=ALU.mult),
                            reads=[('P', pi)], writes=[('P', pi)])
                    else:
                        for bi, (k0, vt, mi) in enumerate(kbl):
                            S.op('dve', lambda e, bi=bi, mi=mi, P=P, nq=nq, i0=i0: e.tensor_tensor(
                                out=P[:, bi * 128:bi * 128 + nq], in0=P[:, bi * 128:bi * 128 + nq],
                                in1=self.amask[:, mi, i0:i0 + nq], op=ALU.mult),
                                reads=[('P', pi)], writes=[('P', pi)])
                    return (h, blk, P, pi)

                def p_back(st):
                    (h, (q0, nq, i0, kbl), P, pi) = st
                    nb_ = len(kbl)
                    po, pok = self.next_ps()
                    for bi, (k0, vt, mi) in enumerate(kbl):
                        S.op('pe', lambda e, po=po, bi=bi, vt=vt, P=P, nq=nq, nb_=nb_, h=h: e.matmul(
                            po[:, 0:nq], lhsT=Vb[:, vt, h * 128:(h + 1) * 128], rhs=P[:, bi * 128:bi * 128 + nq],
                            start=(bi == 0), stop=(bi == nb_ - 1)),
                            reads=[('P', pi), ('Vb', vt)], writes=[pok], inc=False)
                    for bi, (k0, vt, mi) in enumerate(kbl):
                        S.op('pe', lambda e, po=po, bi=bi, P=P, nq=nq, nb_=nb_: e.matmul(
                            po[:, 256:256 + nq], lhsT=self.ones_bf[:, :], rhs=P[:, bi * 128:bi * 128 + nq],
                            start=(bi == 0), stop=(bi == nb_ - 1)),
                            reads=[('P', pi)], writes=[pok], inc=(bi == nb_ - 1))
                    src2 = po[:, 0:512].rearrange("p (a b) -> p a b", a=2)[:, :, 0:nq]
                    a2 = acc[:, :, h, q0:q0 + dil * (nq - 1) + 1:dil] if dil > 1 else acc[:, :, h, q0:q0 + nq]
                    S.op('dve', lambda e, src2=src2, a2=a2: e.tensor_tensor(out=a2, in0=a2, in1=src2, op=ALU.add),
                         reads=[pok, 'acc'], writes=['acc'])

                pend = []
                for h in range(2):
                    for blk in blocks:
                        pend.append(p_front(h, blk))
                        if len(pend) > 3:
                            p_back(pend.pop(0))
                while pend:
                    p_back(pend.pop(0))

                if hb == 0 and g == 0:
                    self.ck(14)
                nblk = 1 if g == 0 else 4

                def s_A(rho, h, ci):
                    pt, ptk = self.next_pt()
                    S.op('pe', lambda e, pt=pt, ci=ci, h=h: e.transpose(out=pt[:, 0:128], in_=kcb[ci][:, h * 128:(h + 1) * 128],
                                                                      identity=self.ident[:, :]),
                         reads=[('kcb', ci)], writes=[ptk])
                    ki = nxt('KcT')
                    S.op('act', lambda e, pt=pt, ki=ki: e.activation(out=KcT[ki][:, :], in_=pt[:, 0:128], func=AF.Identity),
                         reads=[ptk], writes=[('KcT', ki)])
                    return (rho, h, ci, ki)

                def s_B(st):
                    (rho, h, ci, ki) = st
                    ps, pk = self.next_ps()
                    S.op('pe', lambda e, ps=ps, ki=ki, h=h: e.matmul(ps[:, 0:NSMP], lhsT=KcT[ki][:, :],
                                                                   rhs=QT[:, h, NQ:NQ + NSMP], start=True, stop=True),
                         reads=[('KcT', ki), 'QT'], writes=[pk])
                    if rho == 0:
                        S.op('pe', lambda e, ps=ps, h=h: e.matmul(ps[0:NSMP, 8:8 + NSMP], lhsT=KT[:, h, SEQ:SEQ + NSMP],
                                                                rhs=QT[:, h, NQ:NQ + NSMP], start=True, stop=True),
                             reads=['KT', 'QT'], writes=[pk])
                    si = nxt('Ps')
                    Pq = Ps[si]
                    S.op('act', lambda e, ps=ps, Pq=Pq: e.activation(out=Pq[:, 0:NSMP], in_=ps[:, 0:NSMP], func=AF.Exp,
                                                                   scale=scale), reads=[pk], writes=[('Ps', si)])
                    mi = 0 if g == 0 else 1 + rho
                    S.op('dve', lambda e, Pq=Pq, mi=mi: e.tensor_tensor(out=Pq[:, 0:NSMP], in0=Pq[:, 0:NSMP],
                                                                       in1=self.smask[:, mi, :], op=ALU.mult),
                         reads=[('Ps', si)], writes=[('Ps', si)])
                    if rho == 0:
                        S.op('act', lambda e, ps=ps, Pq=Pq: e.activation(out=Pq[0:NSMP, 4:8], in_=ps[0:NSMP, 8:8 + NSMP],
                                                                       func=AF.Exp, scale=scale),
                             reads=[pk, ('Ps', si)], writes=[('Ps', si)])
                        mn = 5 if g == 0 else 6
                        S.op('dve', lambda e, Pq=Pq, mn=mn: e.tensor_tensor(out=Pq[0:NSMP, 4:8], in0=Pq[0:NSMP, 4:8],
                                                                           in1=self.smask[0:NSMP, mn, :], op=ALU.mult),
                             reads=[('Ps', si)], writes=[('Ps', si)])
                    return (rho, h, ci, si)

                def s_C(st):
                    (rho, h, ci, si) = st
                    Pq = Ps[si]
                    po, pok = self.next_ps()
                    S.op('pe', lambda e, po=po, ci=ci, h=h, Pq=Pq: e.matmul(
                        po[:, 0:NSMP], lhsT=vcb[ci][:, h * 128:(h + 1) * 128], rhs=Pq[:, 0:NSMP],
                        start=True, stop=(rho != 0)), reads=[('vcb', ci), ('Ps', si)], writes=[pok], inc=False)
                    if rho == 0:
                        S.op('pe', lambda e, po=po, h=h, Pq=Pq: e.matmul(
                            po[:, 0:NSMP], lhsT=Vsb[0:NSMP, h * 128:(h + 1) * 128], rhs=Pq[0:NSMP, 4:8],
                            start=False, stop=True), reads=['Vsb', ('Ps', si)], writes=[pok], inc=False)
                    S.op('pe', lambda e, po=po, Pq=Pq: e.matmul(
                        po[:, 256:256 + NSMP], lhsT=self.ones_bf[:, :], rhs=Pq[:, 0:NSMP],
                        start=True, stop=(rho != 0)), reads=[('Ps', si)], writes=[pok], inc=(rho != 0))
                    if rho == 0:
                        S.op('pe', lambda e, po=po, Pq=Pq: e.matmul(
                            po[:, 256:256 + NSMP], lhsT=self.ones_bf[0:NSMP, :], rhs=Pq[0:NSMP, 4:8],
                            start=False, stop=True), reads=[('Ps', si)], writes=[pok])
                    src2 = po[:, 0:512].rearrange("p (a b) -> p a b", a=2)[:, :, 0:NSMP]
                    a2 = acc[:, :, h, NQ:NQ + NSMP]
                    S.op('dve', lambda e, src2=src2, a2=a2: e.tensor_tensor(out=a2, in0=a2, in1=src2, op=ALU.add),
                         reads=[pok, 'acc'], writes=['acc'])

                sunits = []
                for rho in range(nblk):
                    ci = nxt('kc')
                    rsl = slice(0, 128) if g == 0 else slice(rho, rho + dil * 127 + 1, dil)
                    S.dma('pool', kcb[ci][:, :].rearrange("p (a b) -> p a b", a=2), cache[g][rsl, 0, h0:h0 + 2, :],
                          writes=[('kcb', ci)])
                    S.dma('pool', vcb[ci][:, :].rearrange("p (a b) -> p a b", a=2), cache[g][rsl, 1, h0:h0 + 2, :],
                          writes=[('vcb', ci)])
                    for h in range(2):
                        sunits.append((rho, h, ci))
                stA, stB = [], []
                n_u = len(sunits)
                for i in range(n_u + 2):
                    if i < n_u:
                        stA.append(s_A(*sunits[i]))
                    if 0 <= i - 1 < n_u:
                        stB.append(s_B(stA[i - 1]))
                    if 0 <= i - 2 < n_u:
                        s_C(stB[i - 2])
                if hb == 0:
                    self.ck(15 + g)
            S.op('dve', lambda e: e.tensor_scalar(out=acc[:, 1, :, :], in0=acc[:, 1, :, :], scalar1=1e-30, scalar2=None,
                                                  op0=ALU.max), reads=['acc'], writes=['acc'])
            S.op('dve', lambda e: e.reciprocal(out=acc[:, 1, :, :], in_=acc[:, 1, :, :]), reads=['acc'], writes=['acc'])
            S.op('dve', lambda e: e.tensor_tensor(out=OTb[:, :, :], in0=acc[:, 0, :, :], in1=acc[:, 1, :, :], op=ALU.mult),
                 reads=['acc'], writes=['OTb'])
            S.dma('sp', OTs[:, h0:h0 + 2, :], OTb[:, :, :], reads=['OTb'])
            if hb == 0:
                self.ck(18)


def host_consts(cfg, half):
    flag = 1.0 if half == 1 else 0.0
    k = np.arange(128)[:, None]
    q = np.arange(128)[None, :]
    U = (k >= q).astype(np.float32)
    Lm = (k <= q).astype(np.float32)
    L2 = Lm * np.where(k < 64, flag, 1.0)
    amask = np.stack([U * flag, Lm * flag, U, Lm, L2], axis=1).astype(np.float32)
    smask = np.zeros((128, 7, 4), np.float32)
    t = np.arange(4)[None, :]
    smask[:, 0, :] = (k >= t)
    for rho in range(4):
        smask[:, 1 + rho, rho] = 1.0
    smask[:4, 5, :] = (np.arange(4)[:, None] <= t)
    smask[:4, 6, :] = np.eye(4)
    bm = np.zeros((128, 4, 4, 128), np.float32)
    j = np.arange(128)[:, None]
    tt = np.arange(128)[None, :]
    for pg, wd in enumerate(POOL_W):
        cur = ((j <= tt) & (j >= tt - wd + 1)).astype(np.float32) / wd - (j == tt)
        prev = ((j - 32 >= tt - wd + 1) & (j < 32)).astype(np.float32) / wd
        bm[:, pg, 0, :] = cur
        bm[:, pg, 1, :] = prev
        if half == 1:
            bm[:, pg, 2, :] = cur
            bm[:, pg, 3, :] = prev
        else:
            cnt = np.minimum(wd, tt + 1).astype(np.float32)
            bm[:, pg, 2, :] = ((j <= tt) & (j >= tt - wd + 1)).astype(np.float32) / cnt - (j == tt)
            bm[:, pg, 3, :] = 0.0
    bss = np.zeros((15, 4, 4), np.float32)
    bsn = np.zeros((4, 4, 4), np.float32)
    for pg, wd in enumerate(POOL_W):
        for t_ in range(4):
            pos = 15 + t_
            for jj in range(pos - wd + 1, pos + 1):
                if jj < 15:
                    bss[jj, pg, t_] += 1.0 / wd
                else:
                    bsn[jj - 15, pg, t_] += 1.0 / wd
            bsn[t_, pg, t_] -= 1.0
    return amask, smask, bm, bss, bsn


_NC_CACHE = {}


def make_in_maps(cfg, inp, n_cores):
    D, ND, AH = cfg.D, cfg.ND, cfg.AH
    f = lambda a: np.ascontiguousarray(np.asarray(a, dtype=np.float32))
    gcols = np.concatenate([f(inp['norm_mix_g']), f(inp['norm_ffn_g'])], axis=0)
    gcols = np.ascontiguousarray(gcols.reshape(8, ND, 128).transpose(2, 0, 1))
    ln = np.stack([f(inp['a_ln_g']), f(inp['a_ln_b'])], axis=1)
    lncols = np.ascontiguousarray(ln.reshape(2, 2, AH, 128).transpose(3, 0, 1, 2))
    qkg = np.concatenate([f(inp['b_q_g'])[0].reshape(-1), f(inp['b_k_g'])[0].reshape(-1)])[None, :]
    qkg = np.ascontiguousarray(qkg)
    shared = {
        'gcols': gcols, 'lncols': lncols, 'lnrep': np.ascontiguousarray(ln), 'qkg': qkg,
        'g2row': f(inp['norm_mix_g'])[2:3], 'csrow': f(inp['c_scale'])[0:1],
        'ident': np.eye(128, dtype=np.float32),
    }
    for k_ in ('a_w_in', 'a_w_s', 'a_b_s', 'a_w_out', 'b_w_qkv', 'b_w_out', 'c_w', 'ffn_w1', 'ffn_w3', 'ffn_w2'):
        shared[k_] = f(inp[k_])
    xp, xs = f(inp['x_prompt']), f(inp['x_sample'])
    caches = [f(inp[k_]) for k_ in ('cache_b_kv0', 'cache_b_kv1', 'cache_b_kv2')]
    st = f(inp['state_c_pool'])
    consts = {h: host_consts(cfg, h) for h in (0, 1)}
    maps = []
    for c in range(n_cores):
        s, h = divmod(c, 2)
        m = dict(shared)
        m['xB'] = np.ascontiguousarray(xp[s, h * HALF:(h + 1) * HALF])
        m['xA'] = np.ascontiguousarray(xp[s, 0:HALF]) if h == 1 else np.zeros((HALF, D), np.float32)
        m['xS'] = np.ascontiguousarray(xs[c])
        for g in range(3):
            m['cache%d' % g] = np.ascontiguousarray(caches[g][0, c])
        m['state'] = np.ascontiguousarray(st[0, c])
        am, sm, bm, bss, bsn = consts[h]
        m['amask'], m['smask'], m['poolp'], m['poolss'], m['poolsn'] = am, sm, bm, bss, bsn
        maps.append(m)
    return maps


def assemble(cfg, res, n_cores):
    D, H = cfg.D, cfg.H
    nseq = n_cores // 2
    y_prompt = np.stack([np.concatenate([res[2 * s]['yB'], res[2 * s + 1]['yB']], axis=0) for s in range(nseq)])
    y_sample = np.stack([res[c]['yS'] for c in range(n_cores)])
    av = np.stack([res[c]['avs'] for c in range(n_cores)], axis=1)
    kv0p = np.stack([res[2 * s + 1]['kvp0'] for s in range(nseq)])[None]
    kv1p = np.stack([res[2 * s + 1]['kvp1'] for s in range(nseq)])[None]
    kv2p = np.stack([np.concatenate([res[2 * s]['kvp2'], res[2 * s + 1]['kvp2']], axis=0) for s in range(nseq)])[None]
    kvs = [np.stack([res[c]['kvs%d' % g] for c in range(n_cores)])[None] for g in range(3)]
    pp = np.stack([res[2 * s + 1]['poolpo'] for s in range(nseq)])[None]
    psm = np.stack([res[c]['poolso'] for c in range(n_cores)])[None]
    outs = (y_prompt, y_sample, av, kv0p, kv1p, kv2p, kvs[0], kvs[1], kvs[2], pp, psm)
    return tuple(np.ascontiguousarray(o.astype(np.float32)) for o in outs)


def kernel(**inputs):
    cfg = Cfg(2048)
    n_cores = 8
    nc = Builder(cfg).build()
    maps = make_in_maps(cfg, inputs, n_cores)
    res = run_bass_kernel_spmd(nc, maps, core_ids=list(range(n_cores)))
    return assemble(cfg, res.results, n_cores)
```
